# Optimizing a Trainium2 kernel written in Bass

```python
import jax, jax.numpy as jnp
from jax import lax
import numpy as np

D_MODEL = 1024
BATCH = 1
SEQ = 16384
DEPTH = 1

GRID_W = 64
CTX_LEN = 256
N_HEADS = 8
HEAD_DIM = 64
ATT_W = N_HEADS * HEAD_DIM
WIN_H = 8
WIN_W = 16
POOL_WINDOWS = (2, 4, 8, 16)
POOL_GROUPS = 4
POOL_DIM = 128
POOL_W = POOL_GROUPS * POOL_DIM
PROJ_W = 3 * ATT_W + POOL_W + 2 * D_MODEL
SPLIT_POINTS = (ATT_W, 2 * ATT_W, 3 * ATT_W, 3 * ATT_W + POOL_W, 3 * ATT_W + POOL_W + D_MODEL)
N_GROUPS = 4
EXPERTS_PER_GROUP = 8
N_EXPERTS = N_GROUPS * EXPERTS_PER_GROUP
TOP_K_EXPERT = 2
D_EXPERT = 512
N_MOD = 6
DEEPNORM_ALPHA = (2.0 * DEPTH) ** 0.25
DEEPNORM_BETA = (8.0 * DEPTH) ** -0.25
LN_EPS = 1e-5
NEG_INF = -1e30

kernel_name = "hybrid_natten_pool_hmoe_dit"


def layer_norm(x, g, b):
    xf = x.astype(jnp.float32)
    mu = jnp.mean(xf, axis=-1, keepdims=True)
    var = jnp.mean(jnp.square(xf - mu), axis=-1, keepdims=True)
    y = (xf - mu) * lax.rsqrt(var + LN_EPS)
    return (y * g.astype(jnp.float32) + b.astype(jnp.float32)).astype(x.dtype)


def adaln_params(cond, w_mod, b_mod):
    mod = (jax.nn.silu(cond) @ w_mod + b_mod)[:, None, :]
    return jnp.split(mod, N_MOD, axis=-1)


def modulate(h, shift, scale):
    return h * (1.0 + scale) + shift


def split_heads(t):
    B, L, _ = t.shape
    return t.reshape(B, L, N_HEADS, HEAD_DIM)


def neighbourhood_attention(q, k, v, k_ctx, v_ctx, rpb):
    B, L, H, Dh = q.shape
    rows = L // GRID_W
    kh = min(WIN_H, rows)
    n_loc = kh * GRID_W
    scale = Dh ** -0.5
    qg = q.reshape(B, rows, GRID_W, H, Dh)
    kg = k.reshape(B, rows, GRID_W, H, Dh)
    vg = v.reshape(B, rows, GRID_W, H, Dh)
    r = jnp.arange(rows, dtype=jnp.int32)
    row_start = jnp.clip(r - kh // 2, 0, rows - kh)
    col = jnp.arange(GRID_W, dtype=jnp.int32)
    col_start = jnp.clip(col - WIN_W // 2, 0, GRID_W - WIN_W)
    col_mask = (col[None, :] >= col_start[:, None]) & (col[None, :] < col_start[:, None] + WIN_W)
    col_off = jnp.clip(col[None, :] - col[:, None], 1 - WIN_W, WIN_W - 1) + (WIN_W - 1)
    rpb32 = rpb.astype(jnp.float32)

    def row_block(args):
        q_row, rs, rq = args
        kb = lax.dynamic_slice_in_dim(kg, rs, kh, axis=1)
        vb = lax.dynamic_slice_in_dim(vg, rs, kh, axis=1)
        row_off = rs + jnp.arange(kh, dtype=jnp.int32) - rq + (WIN_H - 1)
        bias = rpb32[:, row_off[None, :, None], col_off[:, None, :]]
        s_loc = jnp.einsum('bqhd,bikhd->bhqik', q_row, kb,
                           preferred_element_type=jnp.float32) * scale + bias[None]
        s_loc = jnp.where(col_mask[:, None, :], s_loc, NEG_INF).reshape(B, H, GRID_W, n_loc)
        s_ctx = jnp.einsum('bqhd,bchd->bhqc', q_row, k_ctx,
                           preferred_element_type=jnp.float32) * scale
        p = jax.nn.softmax(jnp.concatenate([s_loc, s_ctx], axis=-1), axis=-1).astype(v.dtype)
        o = (jnp.einsum('bhqk,bkhd->bqhd', p[..., :n_loc], vb.reshape(B, n_loc, H, Dh))
             + jnp.einsum('bhqc,bchd->bqhd', p[..., n_loc:], v_ctx))
        return o

    out = lax.map(row_block, (jnp.moveaxis(qg, 1, 0), row_start, r))
    return jnp.moveaxis(out, 0, 1).reshape(B, L, H * Dh)


def context_attention(q, k, v):
    B, C, H, Dh = q.shape
    s = jnp.einsum('bqhd,bkhd->bhqk', q, k, preferred_element_type=jnp.float32) * (Dh ** -0.5)
    p = jax.nn.softmax(s, axis=-1).astype(v.dtype)
    return jnp.einsum('bhqk,bkhd->bqhd', p, v).reshape(B, C, H * Dh)


def multiscale_pool(u, w_grp, layer_scale):
    B, L, _ = u.shape
    ug = u.reshape(B, L, POOL_GROUPS, POOL_DIM)
    ugf = ug.astype(jnp.float32)
    csum = jnp.concatenate([jnp.zeros((B, 1, POOL_GROUPS, POOL_DIM), jnp.float32),
                            jnp.cumsum(ugf, axis=1)], axis=1)
    win = jnp.asarray(POOL_WINDOWS, dtype=jnp.int32)
    t = jnp.arange(L, dtype=jnp.int32)[:, None]
    lo = jnp.clip(t - win // 2, 0, L)
    hi = jnp.clip(t + win - win // 2, 0, L)
    grp = jnp.arange(POOL_GROUPS, dtype=jnp.int32)[None, :]
    mean = (csum[:, hi, grp] - csum[:, lo, grp]) / (hi - lo).astype(jnp.float32)[None, :, :, None]
    pooled = (mean - ugf).astype(u.dtype)
    y = jnp.einsum('blgi,gio->blgo', pooled, w_grp).reshape(B, L, POOL_W)
    return y * layer_scale


def merge_branches(y_attn, y_pool, gate_a, gate_b, w_attn_proj, w_pool_proj, w_out):
    ya = y_attn @ w_attn_proj
    yp = y_pool @ w_pool_proj
    return (jax.nn.sigmoid(gate_a) * ya + jax.nn.sigmoid(gate_b) * yp) @ w_out


def hierarchical_moe(h, w_rg, b_rg, w_re, b_re, w_gate, w_up, w_down):
    B, L, D = h.shape
    T = B * L
    ht = h.reshape(T, D)
    p_group = jax.nn.softmax((ht @ w_rg).astype(jnp.float32) + b_rg.astype(jnp.float32), axis=-1)
    p_top_group, group_idx = lax.top_k(p_group, 1)
    logit_e = ((ht @ w_re).astype(jnp.float32) + b_re.astype(jnp.float32)).reshape(T, N_GROUPS, EXPERTS_PER_GROUP)
    logit_sel = logit_e[jnp.arange(T), group_idx[:, 0]]
    p_e = jax.nn.softmax(logit_sel, axis=-1)
    p_top_e, e_idx = lax.top_k(p_e, TOP_K_EXPERT)
    p_top_e = p_top_e / jnp.sum(p_top_e, axis=-1, keepdims=True)
    combine = p_top_group * p_top_e
    expert_id = group_idx * EXPERTS_PER_GROUP + e_idx
    gates = jnp.sum(jax.nn.one_hot(expert_id, N_EXPERTS, dtype=jnp.float32) * combine[..., None],
                    axis=1).astype(h.dtype)
    y = jnp.zeros((T, D), jnp.float32)
    for g in range(N_GROUPS):
        sl = slice(g * EXPERTS_PER_GROUP, (g + 1) * EXPERTS_PER_GROUP)
        a = jnp.einsum('td,edf->tef', ht, w_gate[sl])
        u = jnp.einsum('td,edf->tef', ht, w_up[sl])
        act = jax.nn.silu(a) * u * gates[:, sl, None]
        y = y + jnp.einsum('tef,efd->td', act, w_down[sl], preferred_element_type=jnp.float32)
    return y.astype(h.dtype).reshape(B, L, D)


def setup_inputs(seed: int = 0) -> dict:
    key = jax.random.key(seed)
    ks = jax.random.split(key, 32)
    f32 = jnp.float32

    def nrm(k, shape, fan_in, gain=1.0):
        return jax.random.normal(k, shape, f32) * (gain * fan_in ** -0.5)

    def small(k, shape, s):
        return jax.random.normal(k, shape, f32) * s

    D = D_MODEL
    return {
        "x": jax.random.normal(ks[0], (BATCH, SEQ, D), f32),
        "c": jax.random.normal(ks[1], (BATCH, D), f32),
        "ctx": jax.random.normal(ks[2], (BATCH, CTX_LEN, D), f32),
        "c_ctx": jax.random.normal(ks[3], (D,), f32),
        "ln_in_g": 1.0 + small(ks[4], (D,), 0.02),
        "ln_in_b": small(ks[5], (D,), 0.02),
        "w_mod": nrm(ks[6], (DEPTH, D, N_MOD * D), D, 0.2),
        "b_mod": small(ks[7], (DEPTH, N_MOD * D), 0.02),
        "w_in": nrm(ks[8], (DEPTH, D, PROJ_W), D),
        "rpb": small(ks[9], (DEPTH, N_HEADS, 2 * WIN_H - 1, 2 * WIN_W - 1), 0.5),
        "w_pool_grp": nrm(ks[10], (DEPTH, POOL_GROUPS, POOL_DIM, POOL_DIM), POOL_DIM),
        "pool_scale": 1.0 + small(ks[11], (DEPTH, POOL_W), 0.02),
        "w_attn_proj": nrm(ks[12], (DEPTH, ATT_W, D), ATT_W),
        "w_pool_proj": nrm(ks[13], (DEPTH, POOL_W, D), POOL_W),
        "w_out": nrm(ks[14], (DEPTH, D, D), D, DEEPNORM_BETA),
        "ln1_g": 1.0 + small(ks[15], (DEPTH, D), 0.02),
        "ln1_b": small(ks[16], (DEPTH, D), 0.02),
        "w_router_group": nrm(ks[17], (DEPTH, D, N_GROUPS), D),
        "b_router_group": small(ks[18], (DEPTH, N_GROUPS), 0.01),
        "w_router_expert": nrm(ks[19], (DEPTH, D, N_EXPERTS), D),
        "b_router_expert": small(ks[20], (DEPTH, N_EXPERTS), 0.01),
        "w_expert_gate": nrm(ks[21], (DEPTH, N_EXPERTS, D, D_EXPERT), D),
        "w_expert_up": nrm(ks[22], (DEPTH, N_EXPERTS, D, D_EXPERT), D),
        "w_expert_down": nrm(ks[23], (DEPTH, N_EXPERTS, D_EXPERT, D), D_EXPERT, DEEPNORM_BETA),
        "ln2_g": 1.0 + small(ks[24], (DEPTH, D), 0.02),
        "ln2_b": small(ks[25], (DEPTH, D), 0.02),
    }


def reference(x, c, ctx, c_ctx, ln_in_g, ln_in_b, w_mod, b_mod, w_in, rpb, w_pool_grp, pool_scale,
              w_attn_proj, w_pool_proj, w_out, ln1_g, ln1_b, w_router_group, b_router_group,
              w_router_expert, b_router_expert, w_expert_gate, w_expert_up, w_expert_down, ln2_g, ln2_b):
    alpha = DEEPNORM_ALPHA
    h = layer_norm(x, ln_in_g, ln_in_b)
    hc = layer_norm(ctx, ln_in_g, ln_in_b)

    for layer in range(DEPTH):
        last = layer == DEPTH - 1
        sh1, sc1, g1, sh2, sc2, g2 = adaln_params(c, w_mod[layer], b_mod[layer])
        csh1, csc1, cg1, csh2, csc2, cg2 = adaln_params(c_ctx[None, :], w_mod[layer], b_mod[layer])
        wl = w_in[layer]

        hc_mod = modulate(hc, csh1, csc1)
        if last:
            k_c, v_c = jnp.split(hc_mod @ wl[:, ATT_W:3 * ATT_W], 2, axis=-1)
        else:
            q_c, k_c, v_c, pool_c, ga_c, gb_c = jnp.split(hc_mod @ wl, SPLIT_POINTS, axis=-1)
        k_c = split_heads(k_c)
        v_c = split_heads(v_c)

        q, k, v, pool_in, ga, gb = jnp.split(modulate(h, sh1, sc1) @ wl, SPLIT_POINTS, axis=-1)
        y_attn = neighbourhood_attention(split_heads(q), split_heads(k), split_heads(v), k_c, v_c, rpb[layer])
        y_pool = multiscale_pool(pool_in, w_pool_grp[layer], pool_scale[layer])
        y = merge_branches(y_attn, y_pool, ga, gb, w_attn_proj[layer], w_pool_proj[layer], w_out[layer])
        h_new = layer_norm(alpha * h + g1 * y, ln1_g[layer], ln1_b[layer])

        ffn = hierarchical_moe(modulate(h_new, sh2, sc2), w_router_group[layer], b_router_group[layer],
                               w_router_expert[layer], b_router_expert[layer],
                               w_expert_gate[layer], w_expert_up[layer], w_expert_down[layer])
        h_new = layer_norm(alpha * h_new + g2 * ffn, ln2_g[layer], ln2_b[layer])

        if not last:
            yc_attn = context_attention(split_heads(q_c), k_c, v_c)
            yc_pool = multiscale_pool(pool_c, w_pool_grp[layer], pool_scale[layer])
            yc = merge_branches(yc_attn, yc_pool, ga_c, gb_c, w_attn_proj[layer], w_pool_proj[layer], w_out[layer])
            hc = layer_norm(alpha * hc + cg1 * yc, ln1_g[layer], ln1_b[layer])
            ffn_c = hierarchical_moe(modulate(hc, csh2, csc2), w_router_group[layer], b_router_group[layer],
                                     w_router_expert[layer], b_router_expert[layer],
                                     w_expert_gate[layer], w_expert_up[layer], w_expert_down[layer])
            hc = layer_norm(alpha * hc + cg2 * ffn_c, ln2_g[layer], ln2_b[layer])
        h = h_new

    return h
```

```python
import numpy as np
import concourse.bass as bass
import concourse.mybir as mybir
from concourse.bass_utils import run_bass_kernel_spmd
from contextlib import ExitStack

F32 = mybir.dt.float32
BF = mybir.dt.bfloat16
AF = mybir.ActivationFunctionType
ALU = mybir.AluOpType
AX = mybir.AxisListType

NCORES = 8
D = 1024
GW = 64
ROWS = 256
RPC = 32
HALO = 4
EXT_ROWS = 40
NEXT = EXT_ROWS * GW
NOWN = RPC * GW
NCTX = 256
NALL = NEXT + NCTX
OWN0 = HALO * GW
NH = 8
HD = 64
NE = 32
DE = 512
ALPHA = 2.0 ** 0.25
EPS = 1e-5
NEG = -30000.0
SCALE = HD ** -0.5

SPECIAL = {0: (0, 6), 1: (0, 6), 2: (1, 5), 3: (1, 5), 29: (14, 5), 30: (14, 5), 31: (14, 6)}
SPECIAL_SLOT = {0: 0, 1: 1, 2: 2, 3: 3, 29: 4, 30: 5, 31: 6}


def row_window(rl):
    if rl in SPECIAL:
        f, n = SPECIAL[rl]
        return f, n, "S"
    if rl % 2 == 0:
        return rl // 2, 4, "E"
    return (rl - 1) // 2, 5, "O"


class Sched:
    def __init__(self, nc, sems, dma_sems):
        self.nc = nc
        self.eng = {"pe": nc.tensor, "act": nc.scalar, "dve": nc.vector, "pool": nc.gpsimd, "sp": nc.sync}
        self.sem = dict(sems)
        self.cnt = {k: 0 for k in sems}
        self.ndma = len(dma_sems)
        for i, s in enumerate(dma_sems):
            self.sem[("dma", i)] = s
            self.cnt[("dma", i)] = 0
        self.next_dma = 0
        self.named = {}
        self.n_named = 16
        self.seen = {e: {} for e in self.eng}
        self.last_w = {}
        self.readers = {}
        self.ninst = 0
        self.region = None
        self.rstack = []
        self.rec = None

    def _emit(self, e, fn):
        if self.region is None:
            fn()
        else:
            self.region["items"][e].append(fn)
        self.ninst += 1

    def _wait(self, e, tok):
        k, v = tok
        if e == "pe" and k == "pe":
            return
        if self.seen[e].get(k, 0) >= v:
            return
        self._emit(e, lambda: self.eng[e].wait_ge(self.sem[k], v))
        self.seen[e][k] = v

    def begin_region(self):
        self.rstack.append({"items": {e: [] for e in self.eng}, "seen0": {e: dict(self.seen[e]) for e in self.eng},
                            "cnt0": dict(self.cnt), "dmas": {e: [] for e in self.eng}})
        self.region = self.rstack[-1]

    def end_region(self, cond_fn):
        reg = self.rstack.pop()
        parent = self.rstack[-1] if self.rstack else None
        self.region = parent
        for e, items in reg["items"].items():
            if not items:
                continue
            n = self.cnt[e] - reg["cnt0"][e]
            cnt0 = reg["cnt0"][e]
            dmas = list(reg["dmas"][e])

            def emit_e(e=e, items=items, n=n, cnt0=cnt0, dmas=dmas):
                eng = self.eng[e]
                with eng.If(cond_fn(e)):
                    for it in items:
                        it()
                with eng.Else():
                    if n > 0:
                        if cnt0 > 0:
                            eng.wait_ge(self.sem[e], cnt0)
                        eng.sem_inc(self.sem[e], n)
                    for (k, prev) in dmas:
                        if prev > 0:
                            eng.wait_ge(self.sem[k], prev)
                        eng.sem_inc(self.sem[k], 16)

            if parent is None:
                emit_e()
            else:
                parent["items"][e].append(emit_e)
                parent["dmas"][e].extend(dmas)
            self.seen[e] = reg["seen0"][e]

    def raw(self, e, fn, r=()):
        self._deps(e, r, ())
        self._emit(e, lambda: fn(self.eng[e]))

    def _deps(self, e, r, w):
        for key in r:
            t = self.last_w.get(key)
            if t is not None:
                self._wait(e, t)
        for key in w:
            t = self.last_w.get(key)
            if t is not None:
                self._wait(e, t)
            for t in self.readers.get(key, ()):
                self._wait(e, t)

    def _commit(self, tok, r, w):
        for key in r:
            self.readers.setdefault(key, []).append(tok)
            if len(self.readers[key]) > 12:
                best = {}
                for k, v in self.readers[key]:
                    if best.get(k, 0) < v:
                        best[k] = v
                self.readers[key] = list(best.items())
        for key in w:
            self.last_w[key] = tok
            self.readers[key] = []

    def record(self):
        self.rec = []

    def stop(self):
        r_, self.rec = self.rec, None
        return r_

    def play(self, seqs, width):
        seqs = [q for q in seqs if q]
        active, nxt = [], 0
        while nxt < len(seqs) or active:
            while len(active) < width and nxt < len(seqs):
                active.append([seqs[nxt], 0])
                nxt += 1
            for a in list(active):
                seq = a[0]
                while True:
                    kind, args, kw = seq[a[1]]
                    a[1] += 1
                    getattr(self, kind)(*args, **kw)
                    glue = kind == "op" and not kw.get("inc", True)
                    if a[1] >= len(seq) or not glue:
                        break
                if a[1] >= len(seq):
                    active.remove(a)

    def op(self, e, fn, r=(), w=(), inc=True):
        if self.rec is not None:
            self.rec.append(("op", (e, fn), dict(r=list(r), w=list(w), inc=inc)))
            return
        self._deps(e, r, w)
        if inc:
            self.cnt[e] += 1
            self._emit(e, lambda: fn(self.eng[e]).then_inc(self.sem[e], 1))
            tok = (e, self.cnt[e])
        else:
            self._emit(e, lambda: fn(self.eng[e]))
            tok = (e, self.cnt[e] + 1)
        self._commit(tok, r, w)

    def dma(self, q, out, in_, r=(), w=(), indirect=None, semkey=None):
        if self.rec is not None:
            self.rec.append(("dma", (q, out, in_), dict(r=list(r), w=list(w), indirect=indirect, semkey=semkey)))
            return
        self._deps(q, r, w)
        if semkey is not None:
            if semkey not in self.named:
                assert len(self.named) < self.n_named
                self.named[semkey] = len(self.named)
            i = self.named[semkey]
        else:
            i = self.n_named + self.next_dma
            self.next_dma = (self.next_dma + 1) % (self.ndma - self.n_named)
        k = ("dma", i)
        prev = self.cnt[k]
        if prev > 0:
            self._wait(q, (k, prev))
        self.cnt[k] += 16
        if self.region is not None:
            self.region["dmas"][q].append((k, prev))
        if indirect is None:
            self._emit(q, lambda: self.eng[q].dma_start(out=out, in_=in_).then_inc(self.sem[k], 16))
        elif indirect[0] == "gather":
            self._emit(q, lambda: self.eng[q].indirect_dma_start(
                out=out, out_offset=None, in_=in_,
                in_offset=bass.IndirectOffsetOnAxis(ap=indirect[1], axis=0)).then_inc(self.sem[k], 16))
        else:
            self._emit(q, lambda: self.eng[q].indirect_dma_start(
                out=out, out_offset=bass.IndirectOffsetOnAxis(ap=indirect[1], axis=0), in_=in_,
                in_offset=None).then_inc(self.sem[k], 16))
        self._commit((k, self.cnt[k]), r, w)

    def barrier(self):
        for e in self.eng:
            for k, v in self.cnt.items():
                if v > 0 and k != e:
                    self._wait(e, (k, v))
        self.last_w = {}
        self.readers = {}


def carve(arena, off, shape, dt):
    esz = 4 if dt == F32 else 2
    n = int(np.prod(shape[1:]))
    nbytes = n * esz
    assert off % 4 == 0 and nbytes % 4 == 0
    assert off + nbytes <= arena.shape[1] * 4, (off, nbytes, arena.shape)
    ap = arena[:, off // 4:(off + nbytes) // 4]
    if dt != F32:
        ap = ap.bitcast(dt)
    if len(shape) == 3:
        ap = ap.rearrange("p (a b) -> p a b", b=shape[2])
    elif len(shape) == 4:
        ap = ap.rearrange("p (a b c) -> p a b c", b=shape[2], c=shape[3])
    return ap


A1_BYTES = 8 * NALL * 2
A2_BYTES = 65536
A3_BYTES = 49152
A4_BYTES = 43008


def build_nc(debug=False, stop_after=None):
    nc = bass.Bass("TRN2", target_bir_lowering=False)

    def din(name, shape, dt=F32):
        return nc.dram_tensor(name, list(shape), dt, kind="ExternalInput").ap()

    x_ext = din("x_ext", [NEXT, D])
    ctx_d = din("ctx", [NCTX, D])
    cc_d = din("cc", [128, 8, 2])
    w_mod = din("w_mod", [D, 6 * D])
    bmod_d = din("b_modc", [128, 48])
    lning_d = din("ln_in_g", [D])
    lninb_d = din("ln_in_b", [D])
    w_in = din("w_in", [D, 4096])
    tabE_d = din("tab_even", [128, 8, 4, 64])
    tabO_d = din("tab_odd", [128, 8, 5, 64])
    tabS_d = din("tab_sp", [7, 128, 8, 6, 64])
    wgrp_d = din("w_pool_grp", [4, 128, 128])
    pscale_d = din("pool_scale_c", [128, 4])
    pfix_d = din("pool_fix", [128, 16 + 64])
    wa_d = din("w_attn_proj", [512, D])
    wp_d = din("w_pool_proj", [512, D])
    wout_d = din("w_out", [D, D])
    ln1g_d = din("ln1_g", [D])
    ln1b_d = din("ln1_b", [D])
    ln2g_d = din("ln2_g", [D])
    ln2b_d = din("ln2_b", [D])
    wr_d = din("w_r", [D, 36])
    br_d = din("b_r", [36])
    n_exp_in = NE if stop_after in (None, "E") else 1
    weg_d = din("w_expert_gate", [n_exp_in, D, DE])
    weu_d = din("w_expert_up", [n_exp_in, D, DE])
    wed_d = din("w_expert_down", [n_exp_in, DE, D])
    out_d = nc.dram_tensor("out", [NOWN, D], F32, kind="ExternalOutput").ap()
    h_scr = nc.dram_tensor("h_scr", [NOWN, D], F32, kind="Internal").ap()
    hn_scr = nc.dram_tensor("hn_scr", [NOWN, D], F32, kind="Internal").ap()
    mod_scr = nc.dram_tensor("mod_scr", [48 * 128], F32, kind="Internal").ap()
    dbg = {}
    if debug:
        for nm, shp, dt in [("dbg_hmT", [128, 8, NALL], BF), ("dbg_qT", [128, 4, NOWN], BF),
                            ("dbg_kT", [128, 4, NALL], BF), ("dbg_V", [128, 22, 8, 65], BF),
                            ("dbg_ypoolT", [128, 4, NOWN], BF), ("dbg_yattnT", [128, 4, NOWN], BF),
                            ("dbg_mod", [128, 48, 2], F32), ("dbg_mergedT", [128, 8, NOWN], BF),
                            ("dbg_hm2T", [128, 8, NOWN], BF), ("dbg_gates", [128, 16, 32], F32),
                            ("dbg_yacc", [128, 16, D], F32)]:
            dbg[nm] = nc.dram_tensor(nm, shp, dt, kind="ExternalOutput").ap()

    es = ExitStack()
    with es:
        def sb(name, shape, dt=F32):
            return es.enter_context(nc.sbuf_tensor(name, list(shape), dt))

        A1 = sb("A1", [128, A1_BYTES // 4])
        A2 = sb("A2", [128, A2_BYTES // 4])
        A3 = sb("A3", [128, A3_BYTES // 4])
        A4 = sb("A4", [128, A4_BYTES // 4])
        ident_b = sb("ident_b", [128, 128], BF)
        ident_f = sb("ident_f", [128, 128], F32)
        mhalf = sb("mhalf", [128, 1])
        cc_t = sb("cc_t", [128, 8, 2])
        sc_t = sb("sc_t", [128, 8, 2])
        bmod_t = sb("bmod_t", [128, 48])
        modT = sb("modT", [128, 48, 2])
        modL = sb("modL", [128, 48])
        modrow = sb("modrow", [48, 128])
        A1c = sb("A1c", [128, 8, 2])
        stats = sb("stats", [128, 8, 2, 6])
        mv = sb("mv", [128, 8, 2])
        rs = sb("rs", [128, 8, 4])
        pscale_t = sb("pscale_t", [128, 4])
        pfix_t = sb("pfix_t", [128, 80])
        wgrp_t = sb("wgrp_t", [128, 4, 128], BF)
        wr_t = sb("wr_t", [128, 8, 36], BF)
        br_t = sb("br_t", [128, 36])
        gates_all = sb("gates_all", [128, 16, 32])
        rt = sb("rt", [128, 2, 96])
        rden = sb("rden", [64, 2, 8])
        ps = [es.enter_context(nc.psum_tensor(f"ps{i}", [128, 512], F32)) for i in range(8)]
        sem_names = ["pe", "act", "dve", "pool", "sp"]
        sems = {k: es.enter_context(nc.semaphore("s_" + k)) for k in sem_names}
        dma_sems = [es.enter_context(nc.semaphore(f"dq{i}")) for i in range(40)]
        S = Sched(nc, sems, dma_sems)

        def dbg_dump(name, ap, key):
            if debug:
                S.dma("sp", dbg[name], ap, r=[key])

        S.op("pool", lambda e: e.memset(ident_b[:], 0.0), w=["ident_b"])
        S.op("pool", lambda e: e.affine_select(out=ident_b[:], in_=ident_b[:], pattern=[[-1, 128]],
                                              compare_op=ALU.not_equal, fill=1.0, base=0, channel_multiplier=1),
             r=["ident_b"], w=["ident_b"])
        S.op("pool", lambda e: e.memset(ident_f[:], 0.0), w=["ident_f"])
        S.op("pool", lambda e: e.affine_select(out=ident_f[:], in_=ident_f[:], pattern=[[-1, 128]],
                                              compare_op=ALU.not_equal, fill=1.0, base=0, channel_multiplier=1),
             r=["ident_f"], w=["ident_f"])
        S.op("pool", lambda e: e.memset(mhalf[:], -0.5), w=["mhalf"])

        S.dma("sp", cc_t[:], cc_d, w=["cc"])
        S.dma("sp", bmod_t[:], bmod_d, w=["bmod"])
        S.dma("sp", pscale_t[:], pscale_d, w=["pscale"])
        S.dma("sp", pfix_t[:], pfix_d, w=["pfix"])
        S.dma("sp", br_t[:], br_d.partition_broadcast(128), w=["br"])
        S.dma("pool", wgrp_t[:], wgrp_d.rearrange("g i o -> i g o"), w=["wgrp"])
        S.dma("pool", wr_t[:], wr_d.rearrange("(k p) n -> p k n", p=128), w=["wr"])

        S.op("act", lambda e: e.activation(out=sc_t[:], in_=cc_t[:], func=AF.Silu), r=["cc"], w=["sc"])
        wm = [carve(A2, 0, [128, 8, 1024], F32), carve(A2, 32768, [128, 8, 1024], F32)]
        psm = ps[0]
        for j in range(6):
            S.dma("sp", wm[j % 2], w_mod[:, j * 1024:(j + 1) * 1024].rearrange("(k p) n -> p k n", p=128),
                  w=[("wm", j % 2)])
            for mcl in range(8):
                mc = j * 8 + mcl
                for kc in range(8):
                    S.op("pe", lambda e, mc=mc, mcl=mcl, kc=kc, j=j: e.matmul(
                        psm[:, mc * 2:(mc + 1) * 2], lhsT=wm[j % 2][:, kc, mcl * 128:(mcl + 1) * 128],
                        rhs=sc_t[:, kc, :], start=(kc == 0), stop=(kc == 7)),
                        r=[("wm", j % 2), "sc"], w=["psm"], inc=(kc == 7 and mcl == 7))
        S.op("dve", lambda e: e.tensor_tensor(
            out=modT[:], in0=psm[:, 0:96].rearrange("p (a b) -> p a b", b=2),
            in1=bmod_t[:].unsqueeze(2).broadcast_to([128, 48, 2]), op=ALU.add),
            r=["psm", "bmod"], w=["modT"])
        S.op("dve", lambda e: e.tensor_scalar(out=A1c[:], in0=modT[:, 8:16, :], scalar1=1.0, scalar2=None,
                                             op0=ALU.add), r=["modT"], w=["A1c"])
        S.op("dve", lambda e: e.tensor_copy(out=modL[:], in_=modT[:, :, 0]), r=["modT"], w=["modL"])
        S.op("pe", lambda e: e.transpose(out=ps[1][0:48, 0:128], in_=modL[:], identity=ident_f[:]),
             r=["modL", "ident_f"], w=["ps1"])
        S.op("dve", lambda e: e.tensor_copy(out=modrow[:], in_=ps[1][0:48, 0:128]), r=["ps1"], w=["modrow"])
        S.dma("sp", mod_scr.rearrange("(m p) -> m p", p=128), modrow[:], r=["modrow"], w=["mod_scr"])
        dbg_dump("dbg_mod", modT[:], "modT")

        def mod_bc(idx):
            return mod_scr[idx * 1024:(idx + 1) * 1024].partition_broadcast(128)

        S.barrier()
        if stop_after == "mod":
            return nc

        hmT = carve(A1, 0, [128, 8, NALL], BF)
        xt = carve(A4, 0, [128, 8, 1024], F32)
        ginb = carve(A4, 32768, [128, 1024], F32)
        binb = carve(A4, 36864, [128, 1024], F32)
        S.dma("sp", ginb, lning_d.partition_broadcast(128), w=["ginb"])
        S.dma("sp", binb, lninb_d.partition_broadcast(128), w=["binb"])

        slot_ctr = [0]

        def layer_norm_stats(src, srckey):
            sl = slot_ctr[0] % 8
            slot_ctr[0] += 1
            k = ("ln", sl)
            S.op("dve", lambda e: e.bn_stats(out=stats[:, sl, 0, :], in_=src[:, 0:512]), r=[srckey], w=[k])
            S.op("dve", lambda e: e.bn_stats(out=stats[:, sl, 1, :], in_=src[:, 512:1024]), r=[srckey], w=[k])
            S.op("dve", lambda e: e.bn_aggr(out=mv[:, sl, :], in_=stats[:, sl, :, :]), r=[k], w=[k])
            S.op("pool", lambda e: e.tensor_scalar(out=rs[:, sl, 0:1], in0=mv[:, sl, 1:2], scalar1=EPS,
                                                  scalar2=None, op0=ALU.add), r=[k], w=[k])
            S.op("pool", lambda e: e.tensor_tensor(out=rs[:, sl, 1:2], in0=rs[:, sl, 0:1], in1=mhalf[:],
                                                  op=ALU.pow), r=[k, "mhalf"], w=[k])
            S.op("dve", lambda e: e.tensor_scalar(out=rs[:, sl, 2:3], in0=mv[:, sl, 0:1], scalar1=rs[:, sl, 1:2],
                                                 scalar2=-1.0, op0=ALU.mult, op1=ALU.mult), r=[k], w=[k])
            return k, rs[:, sl, 1:2], rs[:, sl, 2:3]

        groups = [list(range(g * 4, g * 4 + 4)) for g in range(5)] + [[20, 21]]
        def a_group(gi, tiles):
            which = 1 if gi == 5 else 0
            bufs = []
            for i, t in enumerate(tiles):
                bi = (gi % 2) * 4 + i
                buf = xt[:, bi, :]
                bk = ("xt", bi)
                bufs.append((buf, bk))
                src = x_ext[t * 128:(t + 1) * 128, :] if t < 20 else ctx_d[(t - 20) * 128:(t - 19) * 128, :]
                S.dma("sp", buf, src, w=[bk])
                k, rstd, nbias = layer_norm_stats(buf, bk)
                S.op("act", lambda e, buf=buf, rstd=rstd, nbias=nbias: e.activation(
                    out=buf, in_=buf, func=AF.Identity, scale=rstd, bias=nbias), r=[bk, k], w=[bk])
                S.op("dve", lambda e, buf=buf: e.tensor_tensor(out=buf, in0=buf, in1=ginb, op=ALU.mult),
                     r=[bk, "ginb"], w=[bk])
                S.op("pool", lambda e, buf=buf: e.tensor_tensor(out=buf, in0=buf, in1=binb, op=ALU.add),
                     r=[bk, "binb"], w=[bk])
                if 2 <= t < 18:
                    S.dma("sp", h_scr[(t - 2) * 128:(t - 1) * 128, :], buf, r=[bk], w=[("h_scr", t - 2)])
            ntok = 128 * len(tiles)
            tok0 = tiles[0] * 128
            for kc in range(8):
                pb = ps[kc % 4]
                pk = ("ps", kc % 4)
                for i, (buf, bk) in enumerate(bufs):
                    S.op("pe", lambda e, pb=pb, i=i, buf=buf, kc=kc: e.transpose(
                        out=pb[:, i * 128:(i + 1) * 128], in_=buf[:, kc * 128:(kc + 1) * 128], identity=ident_f[:]),
                        r=[bk, "ident_f"], w=[pk], inc=(i == len(bufs) - 1))
                S.op("act", lambda e, pb=pb, kc=kc, ntok=ntok, tok0=tok0, which=which: e.activation(
                    out=hmT[:, kc, tok0:tok0 + ntok], in_=pb[:, 0:ntok], func=AF.Identity,
                    scale=A1c[:, kc, which:which + 1], bias=modT[:, kc, which:which + 1]),
                    r=[pk, "A1c", "modT"], w=[("hmT", gi)])

        seqs = []
        for gi, tiles in enumerate(groups):
            S.record()
            a_group(gi, tiles)
            seqs.append(S.stop())
        S.play(seqs, 2)
        dbg_dump("dbg_hmT", hmT, ("hmT", 0))
        S.barrier()
        if stop_after == "A":
            return nc

        qT = carve(A2, 0, [128, 4, NOWN], BF)
        kT = carve(A2, 16384, [128, 4, NALL], BF)
        V = carve(A2, 38912, [128, 176, 65], BF)
        ypoolT = carve(A3, 0, [128, 4, NOWN], BF)
        yattnT = carve(A3, 16384, [128, 4, NOWN], BF)
        pu = carve(A3, 16384, [128, NEXT], F32)
        pa = carve(A3, 16384 + 10240, [128, NEXT], F32)
        pbuf = carve(A3, 16384 + 20480, [128, NEXT], F32)
        wblk = [carve(A4, 0, [128, 8, 512], BF), carve(A4, 8192, [128, 8, 512], BF)]
        pooledT = carve(A4, 16384, [128, NOWN], BF)
        tmp8 = carve(A4, 20480, [128, 16], F32)

        def load_wblk(j):
            S.dma("pool", wblk[j % 2], w_in[:, j * 512:(j + 1) * 512].rearrange("(k p) n -> p k n", p=128),
                  w=[("wblk", j % 2)])

        evac_ctr = [0]

        def evac_copy(out, in_, r, w, eng=None):
            evac_ctr[0] += 1
            if (eng == "act") or (eng is None and evac_ctr[0] % 2 == 0):
                S.op("act", lambda e: e.activation(out=out, in_=in_, func=AF.Copy), r=r, w=w)
            else:
                S.op("dve", lambda e: e.tensor_copy(out=out, in_=in_), r=r, w=w)

        pctr = [0]

        def next_ps(nbanks=8):
            i = pctr[0] % nbanks
            pctr[0] += 1
            return ps[i], ("ps", i)

        load_wblk(0)
        load_wblk(1)
        S.op("pool", lambda e: e.memset(V[:, :, 64:65], 1.0), w=["Vones"])
        for mc in range(4):
            for tb in range(4):
                pb, pk = next_ps()
                for kc in range(8):
                    S.op("pe", lambda e, pb=pb, kc=kc, mc=mc, tb=tb: e.matmul(
                        pb[:, :], lhsT=wblk[0][:, kc, mc * 128:(mc + 1) * 128],
                        rhs=hmT[:, kc, OWN0 + tb * 512:OWN0 + (tb + 1) * 512], start=(kc == 0), stop=(kc == 7)),
                        r=[("wblk", 0), "hmT"], w=[pk], inc=(kc == 7))
                evac_copy(qT[:, mc, tb * 512:(tb + 1) * 512], pb[:, :], [pk], ["qT"])
        load_wblk(2)
        for mc in range(4):
            for blk in range(6):
                n = 512 if blk < 5 else 256
                t0 = blk * 512
                pb, pk = next_ps()
                for kc in range(8):
                    S.op("pe", lambda e, pb=pb, kc=kc, mc=mc, t0=t0, n=n: e.matmul(
                        pb[:, 0:n], lhsT=wblk[1][:, kc, mc * 128:(mc + 1) * 128],
                        rhs=hmT[:, kc, t0:t0 + n], start=(kc == 0), stop=(kc == 7)),
                        r=[("wblk", 1), "hmT"], w=[pk], inc=(kc == 7))
                evac_copy(kT[:, mc, t0:t0 + n], pb[:, 0:n], [pk], ["kT"])
        load_wblk(3)
        for t in range(22):
            pb, pk = next_ps()
            for kc in range(8):
                S.op("pe", lambda e, pb=pb, kc=kc, t=t: e.matmul(
                    pb[:, :], lhsT=hmT[:, kc, t * 128:(t + 1) * 128], rhs=wblk[0][:, kc, :],
                    start=(kc == 0), stop=(kc == 7)),
                    r=[("wblk", 0), "hmT"], w=[pk], inc=(kc == 7))
            evac_copy(V[:, t * 8:(t + 1) * 8, 0:64], pb[:, :].rearrange("p (h d) -> p h d", d=64), [pk], [("V", t)])
        for g in range(4):
            wdw = 2 ** (g + 1)
            hw = wdw // 2
            for blk in range(5):
                pb, pk = next_ps()
                for kc in range(8):
                    S.op("pe", lambda e, pb=pb, kc=kc, g=g, blk=blk: e.matmul(
                        pb[:, :], lhsT=wblk[1][:, kc, g * 128:(g + 1) * 128],
                        rhs=hmT[:, kc, blk * 512:(blk + 1) * 512], start=(kc == 0), stop=(kc == 7)),
                        r=[("wblk", 1), "hmT"], w=[pk], inc=(kc == 7))
                evac_copy(pu[:, blk * 512:(blk + 1) * 512], pb[:, :], [pk], ["pu"])
            S.op("dve", lambda e: e.tensor_tensor(out=pu[:, OWN0 - 8:OWN0], in0=pu[:, OWN0 - 8:OWN0],
                                                 in1=pfix_t[:, 0:8], op=ALU.mult), r=["pu", "pfix"], w=["pu"])
            S.op("dve", lambda e: e.tensor_tensor(out=pu[:, OWN0 + NOWN:OWN0 + NOWN + 8],
                                                 in0=pu[:, OWN0 + NOWN:OWN0 + NOWN + 8],
                                                 in1=pfix_t[:, 8:16], op=ALU.mult), r=["pu", "pfix"], w=["pu"])
            lo = OWN0 - 16
            src, srck = pu, "pu"
            dsts = [(pa, "pa"), (pbuf, "pb")]
            for k in range(g + 1):
                sh = 2 ** k
                hi = OWN0 + NOWN + 32 - 2 * sh * 2
                dst, dstk = dsts[k % 2]
                S.op("dve", lambda e, src=src, dst=dst, sh=sh, hi=hi: e.tensor_tensor(
                    out=dst[:, lo:hi], in0=src[:, lo:hi], in1=src[:, lo + sh:hi + sh], op=ALU.add),
                    r=[srck], w=[dstk])
                src, srck = dst, dstk
            S.op("dve", lambda e, src=src: e.scalar_tensor_tensor(
                out=pooledT[:, :], in0=src[:, OWN0 - hw:OWN0 + NOWN - hw], scalar=1.0 / wdw,
                in1=pu[:, OWN0:OWN0 + NOWN], op0=ALU.mult, op1=ALU.subtract), r=[srck, "pu"], w=["pooledT"])
            for side in range(2):
                o0 = OWN0 if side == 0 else OWN0 + NOWN - 8
                S.op("dve", lambda e, src=src, o0=o0, side=side: e.tensor_tensor(
                    out=tmp8[:, side * 8:(side + 1) * 8], in0=src[:, o0 - hw:o0 - hw + 8],
                    in1=pfix_t[:, 16 + g * 16 + side * 8:16 + g * 16 + side * 8 + 8], op=ALU.mult),
                    r=[srck, "pfix"], w=["tmp8"])
                S.op("dve", lambda e, o0=o0, side=side: e.scalar_tensor_tensor(
                    out=pooledT[:, o0 - OWN0:o0 - OWN0 + 8], in0=tmp8[:, side * 8:(side + 1) * 8],
                    scalar=1.0 / wdw, in1=pu[:, o0:o0 + 8], op0=ALU.mult, op1=ALU.subtract),
                    r=["tmp8", "pu"], w=["pooledT"])
            for tb in range(4):
                pb, pk = next_ps()
                S.op("pe", lambda e, pb=pb, tb=tb: e.matmul(
                    pb[:, :], lhsT=wgrp_t[:, g, :], rhs=pooledT[:, tb * 512:(tb + 1) * 512], start=True, stop=True),
                    r=["wgrp", "pooledT"], w=[pk])
                S.op("act", lambda e, pb=pb, tb=tb: e.activation(
                    out=ypoolT[:, g, tb * 512:(tb + 1) * 512], in_=pb[:, :], func=AF.Identity,
                    scale=pscale_t[:, g:g + 1]), r=[pk, "pscale"], w=["ypoolT"])
        dbg_dump("dbg_qT", qT, "qT")
        dbg_dump("dbg_kT", kT, "kT")
        if debug:
            S.barrier()
            S.dma("sp", dbg["dbg_V"].rearrange("p t h d -> p (t h) d"), V)
        dbg_dump("dbg_ypoolT", ypoolT, "ypoolT")
        S.barrier()
        if stop_after == "B":
            return nc

        tabE = carve(A4, 0, [128, 8, 256], F32)
        tabO = carve(A4, 8192, [128, 8, 320], F32)
        tabS2 = [carve(A4, 18432, [128, 8, 384], F32), carve(A3, 32768, [128, 8, 384], F32)]
        sbuf_s = [carve(A4, 30720 + i * 1536, [128, 384], F32) for i in range(2)]
        pT = [carve(A4, 33792 + i * 1024, [128, 512], BF) for i in range(3)]
        Obf = [carve(A4, 36864 + i * 1024, [128, 512], BF) for i in range(2)]
        S.dma("sp", tabE, tabE_d.rearrange("p h j q -> p h (j q)"), w=["tabE"])
        S.dma("sp", tabO, tabO_d.rearrange("p h j q -> p h (j q)"), w=["tabO"])
        units = [(rl, h) for rl in range(RPC) for h in range(NH)]

        def emit_qk(u):
            rl, h = units[u]
            f, nl, kind = row_window(rl)
            hp, half = h // 2, h % 2
            pb = ps[u % 3]
            pk = ("ps", u % 3)
            if kind == "S" and h == 0:
                sp_ = SPECIAL_SLOT[rl] % 2
                S.dma("sp", tabS2[sp_], tabS_d[SPECIAL_SLOT[rl]].rearrange("p h j q -> p h (j q)"), w=[("tabS", sp_)])
            q_ap = qT[half * 64:(half + 1) * 64, hp, rl * 64:(rl + 1) * 64]
            for j in range(nl + 2):
                t0 = (f + j) * 128 if j < nl else NEXT + (j - nl) * 128
                S.op("pe", lambda e, pb=pb, j=j, t0=t0: e.matmul(
                    pb[:, j * 64:(j + 1) * 64], lhsT=kT[half * 64:(half + 1) * 64, hp, t0:t0 + 128], rhs=q_ap,
                    start=True, stop=True), r=["qT", "kT"], w=[pk], inc=(j == nl + 1))

        def emit_softmax(u):
            rl, h = units[u]
            f, nl, kind = row_window(rl)
            pb = ps[u % 3]
            pk = ("ps", u % 3)
            if kind == "S":
                sp_ = SPECIAL_SLOT[rl] % 2
                tab, tk = tabS2[sp_], ("tabS", sp_)
            else:
                tab, tk = {"E": (tabE, "tabE"), "O": (tabO, "tabO")}[kind]
            sbt = sbuf_s[u % 2]
            sk = ("sbs", u % 2)
            pt = pT[u % 3]
            ptk = ("pT", u % 3)
            S.op("dve", lambda e: e.scalar_tensor_tensor(
                out=sbt[:, 0:nl * 64], in0=pb[:, 0:nl * 64], scalar=SCALE,
                in1=tab[:, h, 0:nl * 64], op0=ALU.mult, op1=ALU.add), r=[pk, tk], w=[sk])
            S.op("act", lambda e: e.activation(out=pt[:, 0:nl * 64], in_=sbt[:, 0:nl * 64], func=AF.Exp),
                 r=[sk], w=[ptk])
            S.op("act", lambda e: e.activation(out=pt[:, nl * 64:(nl + 2) * 64], in_=pb[:, nl * 64:(nl + 2) * 64],
                                              func=AF.Exp, scale=SCALE), r=[pk], w=[ptk])

        def emit_pv(u):
            rl, h = units[u]
            f, nl, kind = row_window(rl)
            pt = pT[u % 3]
            ptk = ("pT", u % 3)
            bank = 3 + (rl % 2) * 2 + h // 4
            ob = ps[bank]
            hh = h % 4
            for j in range(nl + 2):
                vt = f + j if j < nl else 20 + (j - nl)
                S.op("pe", lambda e, j=j, vt=vt: e.matmul(
                    ob[0:64, hh * 65:(hh + 1) * 65], lhsT=pt[:, j * 64:(j + 1) * 64], rhs=V[:, vt * 8 + h, :],
                    start=(j == 0), stop=(j == nl + 1)),
                    r=[ptk, ("V", vt), "Vones"], w=[("ps", bank)], inc=(j == nl + 1))

        def emit_norm(rl):
            par = rl % 2
            for b in range(2):
                bank = 3 + par * 2 + b
                ob = ps[bank][0:64, 0:260].rearrange("p (h d) -> p h d", d=65)
                S.op("dve", lambda e, ob=ob, b=b: e.reciprocal(out=rden[:, par, b * 4:(b + 1) * 4], in_=ob[:, :, 64]),
                     r=[("ps", bank)], w=[("rden", par, b)])
                S.op("dve", lambda e, ob=ob, b=b: e.tensor_tensor(
                    out=Obf[par][0:64, b * 256:(b + 1) * 256].rearrange("p (h d) -> p h d", d=64),
                    in0=ob[:, :, 0:64],
                    in1=rden[:, par, b * 4:(b + 1) * 4].unsqueeze(2).broadcast_to([64, 4, 64]), op=ALU.mult),
                    r=[("ps", bank), ("rden", par, b)], w=[("Obf", par)])

        def emit_finish(rl):
            par = rl % 2
            pst = ps[7][:, 0:128].bitcast(BF)
            for mc in range(4):
                S.op("pe", lambda e, mc=mc: e.transpose(
                    out=pst[:, mc * 64:(mc + 1) * 64], in_=Obf[par][0:64, mc * 128:(mc + 1) * 128],
                    identity=ident_b[0:64, 0:64]), r=[("Obf", par), "ident_b"], w=[("ps", 7)], inc=(mc == 3))
            evac_copy(yattnT[:, :, rl * 64:(rl + 1) * 64], pst.rearrange("p (a b) -> p a b", b=64),
                      [("ps", 7)], ["yattnT"])

        nu = len(units)
        emit_qk(0)
        emit_qk(1)
        for u in range(nu):
            rl, h = units[u]
            if u + 2 < nu:
                emit_qk(u + 2)
            emit_softmax(u)
            emit_pv(u)
            if h == 7:
                emit_norm(rl)
            if h == 1 and rl > 0:
                emit_finish(rl - 1)
        emit_finish(RPC - 1)
        dbg_dump("dbg_yattnT", yattnT, "yattnT")
        S.barrier()
        if stop_after == "C":
            return nc

        wa_t = carve(A2, 0, [128, 4, 1024], BF)
        wp_t = carve(A2, 8192, [128, 4, 1024], BF)
        mergedT = carve(A2, 16384, [128, 8, NOWN], BF)
        wout_b = carve(A2, 49152, [128, 8, 1024], BF)
        wga = carve(A4, 0, [128, 8, 1024], BF)
        wgb = carve(A4, 16384, [128, 8, 1024], BF)
        sg = [carve(A4, 32768 + i * 1024, [128, 512], BF) for i in range(4)]
        t12 = [carve(A4, 36864 + i * 2048, [128, 512], F32) for i in range(3)]
        S.dma("pool", wga, w_in[:, 2048:3072].rearrange("(k p) n -> p k n", p=128), w=["wga"])
        S.dma("pool", wgb, w_in[:, 3072:4096].rearrange("(k p) n -> p k n", p=128), w=["wgb"])
        S.dma("pool", wa_t, wa_d.rearrange("(k p) n -> p k n", p=128), w=["wa"])
        S.dma("pool", wp_t, wp_d.rearrange("(k p) n -> p k n", p=128), w=["wp"])
        un = 0
        for tb in range(4):
            tsl = slice(tb * 512, (tb + 1) * 512)
            hsl = slice(OWN0 + tb * 512, OWN0 + (tb + 1) * 512)
            for mc in range(8):
                base = (un % 2) * 4
                un += 1
                pg, pgb, pya, pyp = ps[base], ps[base + 1], ps[base + 2], ps[base + 3]
                kg, kgb, kya, kyp = [("ps", base + i) for i in range(4)]
                msl = slice(mc * 128, (mc + 1) * 128)
                for kc in range(8):
                    S.op("pe", lambda e, kc=kc, pg=pg, msl=msl, hsl=hsl: e.matmul(
                        pg[:, :], lhsT=wga[:, kc, msl], rhs=hmT[:, kc, hsl], start=(kc == 0), stop=(kc == 7)),
                        r=["wga", "hmT"], w=[kg], inc=(kc == 7))
                for kc in range(8):
                    S.op("pe", lambda e, kc=kc, pgb=pgb, msl=msl, hsl=hsl: e.matmul(
                        pgb[:, :], lhsT=wgb[:, kc, msl], rhs=hmT[:, kc, hsl], start=(kc == 0), stop=(kc == 7)),
                        r=["wgb", "hmT"], w=[kgb], inc=(kc == 7))
                for kc in range(4):
                    S.op("pe", lambda e, kc=kc, pya=pya, msl=msl, tsl=tsl: e.matmul(
                        pya[:, :], lhsT=wa_t[:, kc, msl], rhs=yattnT[:, kc, tsl], start=(kc == 0), stop=(kc == 3)),
                        r=["wa", "yattnT"], w=[kya], inc=(kc == 3))
                for kc in range(4):
                    S.op("pe", lambda e, kc=kc, pyp=pyp, msl=msl, tsl=tsl: e.matmul(
                        pyp[:, :], lhsT=wp_t[:, kc, msl], rhs=ypoolT[:, kc, tsl], start=(kc == 0), stop=(kc == 3)),
                        r=["wp", "ypoolT"], w=[kyp], inc=(kc == 3))
                s0, s1 = sg[(un % 2) * 2], sg[(un % 2) * 2 + 1]
                ks0, ks1 = ("sg", (un % 2) * 2), ("sg", (un % 2) * 2 + 1)
                S.op("act", lambda e, s0=s0, pg=pg: e.activation(out=s0, in_=pg[:, :], func=AF.Sigmoid),
                     r=[kg], w=[ks0])
                S.op("act", lambda e, s1=s1, pgb=pgb: e.activation(out=s1, in_=pgb[:, :], func=AF.Sigmoid),
                     r=[kgb], w=[ks1])
                ta, tbb = t12[0], t12[1]
                S.op("dve", lambda e, s0=s0, pya=pya: e.tensor_tensor(out=ta, in0=s0, in1=pya[:, :], op=ALU.mult),
                     r=[ks0, kya], w=["t12a"])
                S.op("dve", lambda e, s1=s1, pyp=pyp: e.tensor_tensor(out=tbb, in0=s1, in1=pyp[:, :], op=ALU.mult),
                     r=[ks1, kyp], w=["t12b"])
                S.op("dve", lambda e, mc=mc, tsl=tsl: e.tensor_tensor(out=mergedT[:, mc, tsl], in0=ta, in1=tbb,
                                                                    op=ALU.add),
                     r=["t12a", "t12b"], w=["mergedT"])
        dbg_dump("dbg_mergedT", mergedT, "mergedT")
        S.barrier()
        if stop_after == "D1":
            return nc

        I32 = mybir.dt.int32
        hm2Tt = [carve(A1, i * 2048, [128, 8, 128], BF) for i in range(2)]
        oh1_all = carve(A1, 4096, [128, 16, 32], F32)
        oh2_all = carve(A1, 6144, [128, 16, 32], F32)
        Pall = carve(A1, 8192, [128, 16, 32], F32)
        tmpP = carve(A1, 10240, [128, 16, 32], F32)
        Mbf = carve(A1, 12288, [128, 16, 32], BF)
        c12 = carve(A1, 13312, [128, 2, 16], F32)
        Crun = carve(A1, 13440, [128, 32], F32)
        ebase = carve(A1, 13568, [128, 32], F32)
        ebase_i = carve(A1, 13696, [128, 32], F32).bitcast(I32)
        destf = carve(A1, 13824, [128, 32], F32)
        dest_i = carve(A1, 13952, [128, 32], F32).bitcast(I32)
        tokid_i = carve(A1, 14080, [128, 16], F32).bitcast(I32)
        lsrc = carve(A1, 14144, [128, 64], F32)
        lsrc_i = lsrc.bitcast(I32)
        counts_i = carve(A1, 14400, [128, 32], F32).bitcast(I32)
        Lstrict = carve(A1, 14528, [128, 128], BF)
        ones_bf = carve(A1, 14784, [128, 128], BF)
        zeros_i = carve(A1, 15040, [128, 1024], F32)
        Ball = carve(A1, 21504, [128, 16, 32], F32)
        hm2_scr = nc.dram_tensor("hm2_scr", [NOWN, D], BF, kind="Internal").ap()
        list_scr = nc.dram_tensor("list_scr", [NE * NOWN, 2], I32, kind="Internal").ap()
        ybuf = nc.dram_tensor("ybuf", [NE * NOWN, D], F32, kind="Internal").ap()

        wo32 = carve(A3, 0, [128, 8, 1024], F32)
        bc_g1 = carve(A3, 32768, [128, 1024], F32)
        bc_onep = carve(A3, 36864, [128, 1024], F32)
        bc_ln1g = carve(A4, 0, [128, 1024], F32)
        bc_ln1b = carve(A4, 4096, [128, 1024], F32)
        bc_A2 = carve(A4, 8192, [128, 1024], F32)
        bc_B2 = carve(A4, 12288, [128, 1024], F32)
        htile = [carve(A4, 16384 + i * 4096, [128, 1024], F32) for i in range(2)]
        rtile = [carve(A4, 24576 + i * 4096, [128, 1024], F32) for i in range(2)]
        hntile = [carve(A4, 32768 + i * 4096, [128, 1024], F32) for i in range(2)]
        hm2tiles = [carve(A4, 40960, [128, 1024], BF), carve(A1, 19456, [128, 1024], BF)]
        S.dma("sp", wo32, wout_d.rearrange("(k p) n -> p k n", p=128), w=["wo32"])
        S.dma("sp", bc_g1, mod_bc(2), w=["bc_g1"])
        S.dma("sp", bc_onep, mod_bc(4), w=["bc_onep"])
        S.dma("sp", bc_B2, mod_bc(3), w=["bc_B2"])
        S.dma("sp", bc_ln1g, ln1g_d.partition_broadcast(128), w=["bc_ln1g"])
        S.dma("sp", bc_ln1b, ln1b_d.partition_broadcast(128), w=["bc_ln1b"])
        S.op("pool", lambda e: e.memset(zeros_i, 0.0), w=["zeros_i"])
        S.dma("sp", list_scr.rearrange("(p r) c -> p (r c)", p=128), zeros_i.bitcast(I32), r=["zeros_i"],
              w=["list_scr"])
        S.op("pool", lambda e: e.memset(ones_bf, 1.0), w=["ones_bf"])
        S.op("pool", lambda e: e.memset(Lstrict, 1.0), w=["Lstrict"])
        S.op("pool", lambda e: e.affine_select(out=Lstrict, in_=Lstrict, pattern=[[1, 128]], compare_op=ALU.is_gt,
                                              fill=0.0, base=0, channel_multiplier=-1),
             r=["Lstrict"], w=["Lstrict"])
        S.op("pool", lambda e: e.iota(out=ebase_i, pattern=[[NOWN, 32]], base=0, channel_multiplier=0),
             w=["ebase_i"])
        S.op("pool", lambda e: e.tensor_copy(out=ebase, in_=ebase_i), r=["ebase_i"], w=["ebase"])
        S.op("pool", lambda e: e.iota(out=tokid_i, pattern=[[128, 16]], base=0, channel_multiplier=1),
             w=["tokid_i"])
        S.op("pool", lambda e: e.memset(Crun, 0.0), w=["Crun"])
        for kc in range(8):
            S.op("dve", lambda e, kc=kc: e.tensor_tensor(out=wout_b[:, kc, :], in0=wo32[:, kc, :], in1=bc_g1,
                                                        op=ALU.mult), r=["wo32", "bc_g1"], w=["wout_b"])
        S.op("pool", lambda e: e.tensor_scalar(out=bc_onep, in0=bc_onep, scalar1=1.0, scalar2=None, op0=ALU.add),
             r=["bc_onep"], w=["bc_onep"])
        S.op("pool", lambda e: e.tensor_tensor(out=bc_A2, in0=bc_ln1g, in1=bc_onep, op=ALU.mult),
             r=["bc_ln1g", "bc_onep"], w=["bc_A2"])
        S.op("pool", lambda e: e.tensor_tensor(out=bc_onep, in0=bc_ln1b, in1=bc_onep, op=ALU.mult),
             r=["bc_ln1b", "bc_onep", "bc_A2"], w=["bc_onep"])
        S.op("pool", lambda e: e.tensor_tensor(out=bc_B2, in0=bc_B2, in1=bc_onep, op=ALU.add),
             r=["bc_B2", "bc_onep"], w=["bc_B2"])

        def routing(t, pr, prk):
            sl = t % 2
            R = rt[:, sl, :]
            k = ("rt", sl)
            lg = R[:, 0:36]
            S.op("dve", lambda e: e.tensor_tensor(out=lg, in0=pr[:, 0:36], in1=br_t[:], op=ALU.add),
                 r=[prk, "br"], w=[k])
            gmax, ngmax, gsum, ptg = R[:, 36:37], R[:, 37:38], R[:, 38:39], R[:, 39:40]
            eg, oh, pen = R[:, 40:44], R[:, 44:48], R[:, 48:52]
            ml = R[:, 52:84]
            top8 = R[:, 84:92]
            e2, den, c1 = R[:, 92:93], R[:, 93:94], R[:, 94:95]
            S.op("dve", lambda e: e.tensor_reduce(out=gmax, in_=lg[:, 0:4], axis=AX.X, op=ALU.max), r=[k], w=[k])
            S.op("dve", lambda e: e.tensor_scalar(out=ngmax, in0=gmax, scalar1=-1.0, scalar2=None, op0=ALU.mult),
                 r=[k], w=[k])
            S.op("act", lambda e: e.activation(out=eg, in_=lg[:, 0:4], func=AF.Exp, bias=ngmax, scale=1.0),
                 r=[k], w=[k])
            S.op("dve", lambda e: e.tensor_reduce(out=gsum, in_=eg, axis=AX.X, op=ALU.add), r=[k], w=[k])
            S.op("dve", lambda e: e.reciprocal(out=ptg, in_=gsum), r=[k], w=[k])
            S.op("dve", lambda e: e.tensor_scalar(out=oh, in0=lg[:, 0:4], scalar1=gmax, scalar2=None,
                                                 op0=ALU.is_equal), r=[k], w=[k])
            S.op("dve", lambda e: e.tensor_scalar(out=pen, in0=oh, scalar1=1e30, scalar2=-1e30, op0=ALU.mult,
                                                 op1=ALU.add), r=[k], w=[k])
            S.op("dve", lambda e: e.tensor_tensor(
                out=ml.rearrange("p (g x) -> p g x", x=8), in0=lg[:, 4:36].rearrange("p (g x) -> p g x", x=8),
                in1=pen.unsqueeze(2).broadcast_to([128, 4, 8]), op=ALU.add), r=[k], w=[k])
            S.op("dve", lambda e: e.max(out=top8, in_=ml), r=[k], w=[k])
            S.op("dve", lambda e: e.tensor_scalar(out=den, in0=top8[:, 0:1], scalar1=-1.0, scalar2=None,
                                                 op0=ALU.mult), r=[k], w=[k])
            S.op("act", lambda e: e.activation(out=e2, in_=top8[:, 1:2], func=AF.Exp, bias=den, scale=1.0),
                 r=[k], w=[k])
            S.op("dve", lambda e: e.tensor_scalar(out=den, in0=e2, scalar1=1.0, scalar2=None, op0=ALU.add),
                 r=[k], w=[k])
            S.op("dve", lambda e: e.reciprocal(out=c1, in_=den), r=[k], w=[k])
            rk_ = ("route", t)
            S.op("dve", lambda e: e.tensor_tensor(out=c12[:, 0, t:t + 1], in0=c1, in1=ptg, op=ALU.mult),
                 r=[k], w=[rk_])
            S.op("dve", lambda e: e.tensor_tensor(out=c12[:, 1, t:t + 1], in0=c12[:, 0, t:t + 1], in1=e2,
                                                 op=ALU.mult), r=[k, rk_], w=[rk_])
            S.op("dve", lambda e: e.tensor_scalar(out=oh1_all[:, t, :], in0=ml, scalar1=top8[:, 0:1], scalar2=None,
                                                 op0=ALU.is_equal), r=[k], w=[rk_])
            S.op("dve", lambda e: e.tensor_scalar(out=oh2_all[:, t, :], in0=ml, scalar1=top8[:, 1:2], scalar2=None,
                                                 op0=ALU.is_equal), r=[k], w=[rk_])
            S.op("dve", lambda e: e.tensor_tensor(out=Mbf[:, t, :], in0=oh1_all[:, t, :], in1=oh2_all[:, t, :],
                                                 op=ALU.add), r=[rk_], w=[rk_])
            pq = ps[6 + t % 2]
            S.op("pe", lambda e: e.matmul(pq[:, 64:96], lhsT=Lstrict, rhs=Mbf[:, t, :], start=True, stop=True),
                 r=[rk_, "Lstrict"], w=[prk], inc=False)
            S.op("pe", lambda e: e.matmul(pq[:, 96:128], lhsT=ones_bf, rhs=Mbf[:, t, :], start=True, stop=True),
                 r=[rk_, "ones_bf"], w=[prk])
            S.op("dve", lambda e: e.tensor_copy(out=Pall[:, t, :], in_=pq[:, 64:96]), r=[prk], w=[rk_])
            S.op("dve", lambda e: e.tensor_copy(out=Ball[:, t, :], in_=pq[:, 96:128]), r=[prk], w=[rk_])

        def d2_tile(t):
            par = t % 2
            hm2tile, hm2k = hm2tiles[par], ("hm2tile", par)
            ht, hk = htile[par], ("htile", par)
            S.dma("sp", ht, h_scr[t * 128:(t + 1) * 128, :], r=[("h_scr", t)], w=[hk])
            r_t, rk = rtile[par], ("rtile", par)
            for half in range(2):
                pb = ps[par * 2 + half]
                pk = ("ps", par * 2 + half)
                for kc in range(8):
                    S.op("pe", lambda e, pb=pb, kc=kc, half=half: e.matmul(
                        pb[:, :], lhsT=mergedT[:, kc, t * 128:(t + 1) * 128],
                        rhs=wout_b[:, kc, half * 512:(half + 1) * 512], start=(kc == 0), stop=(kc == 7)),
                        r=["mergedT", "wout_b"], w=[pk], inc=(kc == 7))
                S.op("dve", lambda e, pb=pb, half=half: e.scalar_tensor_tensor(
                    out=r_t[:, half * 512:(half + 1) * 512], in0=ht[:, half * 512:(half + 1) * 512], scalar=ALPHA,
                    in1=pb[:, :], op0=ALU.mult, op1=ALU.add), r=[hk, pk], w=[rk])
            k, rstd, nbias = layer_norm_stats(r_t, rk)
            S.op("act", lambda e, rstd=rstd, nbias=nbias: e.activation(
                out=r_t, in_=r_t, func=AF.Identity, scale=rstd, bias=nbias), r=[rk, k], w=[rk])
            hn, hnk = hntile[par], ("hntile", par)
            S.op("pool", lambda e: e.tensor_tensor(out=hn, in0=r_t, in1=bc_ln1g, op=ALU.mult),
                 r=[rk, "bc_ln1g"], w=[hnk])
            S.op("pool", lambda e: e.tensor_tensor(out=hn, in0=hn, in1=bc_ln1b, op=ALU.add),
                 r=[hnk, "bc_ln1b"], w=[hnk])
            S.dma("sp", hn_scr[t * 128:(t + 1) * 128, :], hn, r=[hnk], w=[("hn_scr", t)])
            S.op("dve", lambda e: e.tensor_tensor(out=r_t, in0=r_t, in1=bc_A2, op=ALU.mult),
                 r=[rk, "bc_A2"], w=[rk])
            S.op("dve", lambda e, hm2tile=hm2tile, r_t=r_t: e.tensor_tensor(out=hm2tile, in0=r_t, in1=bc_B2, op=ALU.add),
                 r=[rk, "bc_B2"], w=[hm2k])
            S.dma("sp", hm2_scr[t * 128:(t + 1) * 128, :], hm2tile, r=[hm2k], w=[("hm2_scr", t)])
            pstb = ps[4 + par]
            pstk = ("ps", 4 + par)
            pst = pstb[:, :].bitcast(BF)
            for kc in range(8):
                S.op("pe", lambda e, kc=kc, pst=pst, hm2tile=hm2tile: e.transpose(
                    out=pst[:, kc * 128:(kc + 1) * 128], in_=hm2tile[:, kc * 128:(kc + 1) * 128],
                    identity=ident_b[:]), r=[hm2k, "ident_b"], w=[pstk], inc=(kc == 7))
            h2t, h2k = hm2Tt[par], ("hm2Tt", par)
            evac_copy(h2t, pst.rearrange("p (a b) -> p a b", b=128), [pstk], [h2k])
            pr = ps[6 + par]
            prk = ("ps", 6 + par)
            for kc in range(8):
                S.op("pe", lambda e, kc=kc, pr=pr, h2t=h2t: e.matmul(
                    pr[:, 0:36], lhsT=h2t[:, kc, :], rhs=wr_t[:, kc, :],
                    start=(kc == 0), stop=(kc == 7)), r=[h2k, "wr"], w=[prk], inc=(kc == 7))
            routing(t, pr, prk)

        seqs = []
        for t in range(16):
            S.record()
            d2_tile(t)
            seqs.append(S.stop())
        S.play(seqs, 2)
        allr = [("route", t) for t in range(16)]
        for t in range(16):
            S.op("dve", lambda e, t=t: e.tensor_tensor(out=Pall[:, t, :], in0=Pall[:, t, :], in1=Crun, op=ALU.add),
                 r=[("route", t), "Crun"], w=[("route", t)])
            S.op("dve", lambda e, t=t: e.tensor_tensor(out=Crun, in0=Ball[:, t, :], in1=Crun, op=ALU.add),
                 r=[("route", t), "Crun"], w=["Crun"])
        S.op("dve", lambda e: e.tensor_tensor(out=Pall, in0=Pall, in1=ebase.unsqueeze(1).broadcast_to([128, 16, 32]),
                                             op=ALU.add), r=allr + ["ebase"], w=["Pall"])
        for kk, oh_all in enumerate((oh1_all, oh2_all)):
            S.op("dve", lambda e, oh_all=oh_all: e.tensor_tensor(out=tmpP, in0=Pall, in1=oh_all, op=ALU.mult),
                 r=["Pall"] + allr, w=["tmpP"])
            S.op("dve", lambda e, kk=kk: e.tensor_reduce(out=destf[:, kk * 16:(kk + 1) * 16], in_=tmpP, axis=AX.X,
                                                        op=ALU.add), r=["tmpP"], w=["destf"])
        S.op("dve", lambda e: e.tensor_copy(out=dest_i, in_=destf), r=["destf"], w=["dest_i"])
        S.op("dve", lambda e: e.tensor_copy(out=counts_i, in_=Crun), r=["Crun"], w=["counts_i"])
        lsrc4 = lsrc.rearrange("p (k t c) -> p k t c", k=2, t=16)
        lsrc4_i = lsrc_i.rearrange("p (k t c) -> p k t c", k=2, t=16)
        for kk in range(2):
            S.op("dve", lambda e, kk=kk: e.tensor_copy(out=lsrc4_i[:, kk, :, 0], in_=tokid_i), r=["tokid_i"],
                 w=["lsrc"])
            S.op("dve", lambda e, kk=kk: e.tensor_copy(out=lsrc4[:, kk, :, 1], in_=c12[:, kk, :]), r=allr,
                 w=["lsrc"])
        for kk in range(2):
            for t in range(16):
                S.dma("pool", list_scr, lsrc4_i[:, kk, t, :], r=["lsrc", "dest_i", "list_scr"], w=[("lscat", kk, t)],
                      indirect=("scatter", dest_i[:, kk * 16 + t:kk * 16 + t + 1]))
        if debug:
            S.barrier()
            S.dma("sp", dbg["dbg_gates"][:, 0, :], destf)
            S.dma("sp", dbg["dbg_gates"][:, 1, :], Crun)
            S.dma("sp", dbg["dbg_gates"][:, 2, :], c12.rearrange("p k t -> p (k t)"))
        S.barrier()
        if stop_after == "D2":
            return nc

        wexp = []
        for i in range(2):
            o = i * 24576
            wexp.append((carve(A3, o, [128, 8, 512], BF), carve(A3, o + 8192, [128, 8, 512], BF),
                         carve(A3, o + 16384, [128, 4, 1024], BF)))
        lst = [carve(A1, 23552 + i * 8, [128, 2], F32) for i in range(5)]
        RB = 24576
        xg = [carve(A4, 64 + i * 2048, [128, 1024], BF) for i in range(2)] + [carve(A1, RB, [128, 1024], BF)]
        xgT = [carve(A4, 4160 + i * 2048, [128, 8, 128], BF) for i in range(2)] + [carve(A1, RB + 2048, [128, 8, 128], BF)]
        sa_t = [carve(A4, 8256 + i * 2048, [128, 512], F32) for i in range(2)] + [carve(A1, RB + 4096, [128, 512], F32)]
        act_t = [carve(A4, 12352 + i * 1024, [128, 512], BF) for i in range(2)] + [carve(A1, RB + 6144, [128, 512], BF)]
        actT = [carve(A4, 14400 + i * 1024, [128, 4, 128], BF) for i in range(2)] + [carve(A1, RB + 7168, [128, 4, 128], BF)]
        ysb = [carve(A4, 16448 + i * 4096, [128, 1024], F32) for i in range(2)] + [carve(A1, RB + 8192, [128, 1024], F32)]
        cond_engs = ("pe", "act", "dve", "pool", "sp")
        warm_rhs = carve(A1, RB + 12288, [128, 512], BF)
        S.op("pool", lambda e: e.memset(warm_rhs, 0.0), w=["warm_rhs"])

        def keep_warm(bank, bkey, n):
            for i in range(n):
                S.op("pe", lambda e: e.matmul(bank[:, :], lhsT=ones_bf, rhs=warm_rhs, start=True, stop=True),
                     r=["warm_rhs", "ones_bf"], w=[bkey], inc=False)

        stage = [carve(A2, i * 16384, [128, 4096], F32) for i in range(4)]
        stg_ctr = [0]

        def chunks(ex):
            wg_t, wu_t, wd_t = wexp[ex % 2]
            return (("g", wg_t, weg_d[ex], 8), ("u", wu_t, weu_d[ex], 8), ("d", wd_t, wed_d[ex], 4))

        def issue_loads(ex):
            for c, (nm, dst, src, kdim) in enumerate(chunks(ex)):
                si = (3 * ex + c) % 4
                st3 = stage[si].rearrange("p (k n) -> p k n", k=kdim)
                S.dma("sp", st3, src.rearrange("(p k) n -> p k n", p=128), w=[("stage", si)])

        def cast_expert(ex):
            for c, (nm, dst, src, kdim) in enumerate(chunks(ex)):
                si = (3 * ex + c) % 4
                st3 = stage[si].rearrange("p (k n) -> p k n", k=kdim)
                sk = ("stage", si)
                if True:
                    h = kdim // 2
                    S.op("act", lambda e, dst=dst, st3=st3, h=h: e.activation(out=dst[:, 0:h, :], in_=st3[:, 0:h, :],
                                                                          func=AF.Copy),
                         r=[sk], w=[("wexp", ex % 2, nm, 0)])
                    S.op("dve", lambda e, dst=dst, st3=st3, h=h: e.tensor_copy(out=dst[:, h:, :], in_=st3[:, h:, :]),
                         r=[sk], w=[("wexp", ex % 2, nm, 1)])

        def prefetch_lists(ex_):
            for par_ in range(2):
                li_ = (2 * ex_ + par_) % 4
                r0_ = ex_ * NOWN + par_ * 128
                S.dma("pool", lst[li_].bitcast(I32), list_scr[r0_:r0_ + 128, :], w=[("lst", li_)],
                      semkey=("lst", li_))

        n_exp = NE
        prefetch_lists(0)
        issue_loads(0)
        cast_expert(0)
        issue_loads(1)
        sctr = 0
        for ex in range(n_exp):
            if ex + 1 < n_exp:
                prefetch_lists(ex + 1)
            wg_t, wu_t, wd_t = wexp[ex % 2]
            wk = ("wexp", ex % 2)
            cvals = {}
            cregs = {}
            for en in cond_engs:
                cregs[en] = S.eng[en].alloc_register(f"ncnt_{en}_{ex}")
                S.raw(en, lambda e, en=en, ex=ex: e.reg_load(cregs[en], counts_i[0:1, ex:ex + 1]), r=["counts_i"])
                cvals[en] = S.eng[en].snap(cregs[en], donate=True)
            def slot_r1(j):
                par = j % 2
                bi = par if j < 2 else 2
                row0 = ex * NOWN + j * 128
                li = (2 * ex + par) % 4 if j < 2 else 4
                lt, lk = lst[li], ("lst", li)
                if j >= 2:
                    S.dma("pool", lt.bitcast(I32), list_scr[row0:row0 + 128, :], w=[lk], semkey=("lst", li))
                xgt, xgk = xg[bi], ("xg", bi)
                S.dma("pool", xgt, hm2_scr, r=[lk], w=[xgk], indirect=("gather", lt.bitcast(I32)[:, 0:1]),
                      semkey=("xg", bi))
                pxk = ("ps", 0)
                pxt = ps[0][:, :].bitcast(BF)
                for kc in range(8):
                    S.op("pe", lambda e, kc=kc: e.transpose(
                        out=pxt[:, kc * 128:(kc + 1) * 128],
                        in_=xgt.rearrange("t (p k) -> t k p", k=8)[:, kc, :],
                        identity=ident_b[:]), r=[xgk, "ident_b"], w=[pxk], inc=(kc == 7))
                xT, xTk = xgT[bi], ("xgT", bi)
                evac_copy(xT, pxt.rearrange("p (a b) -> p a b", b=128), [pxk], [xTk], eng="dve")
                pa_, pu_ = ps[1 + 2 * par], ps[2 + 2 * par]
                ka, ku = ("ps", 1 + 2 * par), ("ps", 2 + 2 * par)
                keep_warm(pa_, ka, 3)
                for kc in range(8):
                    S.op("pe", lambda e, kc=kc: e.matmul(
                        pa_[:, :], lhsT=xT[:, kc, :], rhs=wg_t[:, kc, :], start=(kc == 0), stop=(kc == 7)),
                        r=[xTk, wk + ("g", 0), wk + ("g", 1)], w=[ka], inc=(kc == 7))
                for kc in range(8):
                    S.op("pe", lambda e, kc=kc: e.matmul(
                        pu_[:, :], lhsT=xT[:, kc, :], rhs=wu_t[:, kc, :], start=(kc == 0), stop=(kc == 7)),
                        r=[xTk, wk + ("u", 0), wk + ("u", 1)], w=[ku], inc=(kc == 7))
                st, stk = sa_t[bi], ("sa", bi)
                S.op("act", lambda e: e.activation(out=st, in_=pa_[:, :], func=AF.Silu), r=[ka], w=[stk])
                at_, atk = act_t[bi], ("act", bi)
                S.op("dve", lambda e: e.scalar_tensor_tensor(
                    out=at_, in0=st, scalar=lt[:, 1:2], in1=pu_[:, :], op0=ALU.mult, op1=ALU.mult),
                    r=[stk, ku, lk], w=[atk])

            def slot_r2(j):
                par = j % 2
                bi = par if j < 2 else 2
                row0 = ex * NOWN + j * 128
                at_, atk = act_t[bi], ("act", bi)
                pak = ("ps5", par)
                pat = ps[5][:, par * 256:(par + 1) * 256].bitcast(BF)
                keep_warm(ps[6], ("ps", 6), 2)
                for fc in range(4):
                    S.op("pe", lambda e, fc=fc: e.transpose(
                        out=pat[:, fc * 128:(fc + 1) * 128],
                        in_=at_.rearrange("t (p k) -> t k p", k=4)[:, fc, :],
                        identity=ident_b[:]), r=[atk, "ident_b"], w=[pak], inc=(fc == 3))
                aT, aTk = actT[bi], ("actT", bi)
                evac_copy(aT, pat.rearrange("p (a b) -> p a b", b=128), [pak], [aTk], eng="dve")
                yb, ybk = ysb[bi], ("ysb", bi)
                keep_warm(ps[7], ("ps", 7), 2)
                for half in range(2):
                    pd = ps[6 + half]
                    pdk = ("ps", 6 + half)
                    for fc in range(4):
                        S.op("pe", lambda e, fc=fc, pd=pd, half=half: e.matmul(
                            pd[:, :], lhsT=aT[:, fc, :], rhs=wd_t[:, fc, half * 512:(half + 1) * 512],
                            start=(fc == 0), stop=(fc == 3)), r=[aTk, wk + ("d", 0), wk + ("d", 1)], w=[pdk], inc=(fc == 3))
                    evac_copy(yb[:, half * 512:(half + 1) * 512], pd[:, :], [pdk], [ybk],
                              eng=("dve" if half == 0 else "act"))
                S.dma("act", ybuf[row0:row0 + 128, :], yb, r=[ybk], w=[("ybuf", ex, j)], semkey=("yst", bi))

            def cond(thr):
                return lambda en: cvals[en] > thr

            def emit_pair(j0):
                for fn_, jj in ((slot_r1, j0), (slot_r1, j0 + 1), (slot_r2, j0), (slot_r2, j0 + 1)):
                    S.begin_region()
                    fn_(jj)
                    S.end_region(cond(jj * 128))

            emit_pair(0)
            S.begin_region()
            emit_pair(2)
            S.begin_region()
            emit_pair(4)
            emit_pair(6)
            S.begin_region()
            for j0 in (8, 10, 12, 14):
                emit_pair(j0)
            S.end_region(cond(1024))
            S.end_region(cond(512))
            S.end_region(cond(256))
            if ex + 1 < n_exp:
                cast_expert(ex + 1)
            if ex + 2 < n_exp:
                issue_loads(ex + 2)
            for en in cond_engs:
                S.eng[en].free_register(cregs[en])
        S.barrier()
        if stop_after == "E":
            return nc

        bc_g2 = carve(A4, 0, [128, 1024], F32)
        bc_ln2g = carve(A4, 4096, [128, 1024], F32)
        bc_ln2b = carve(A4, 8192, [128, 1024], F32)
        FW = 4
        hnt = [carve(A3, i * 4096, [128, 1024], F32) for i in range(FW)]
        ot = [carve(A3, 16384 + i * 4096, [128, 1024], F32) for i in range(FW)]
        y1t = [carve(A2, i * 4096, [128, 1024], F32) for i in range(FW)]
        y2t = [carve(A2, 16384 + i * 4096, [128, 1024], F32) for i in range(FW)]
        S.dma("sp", bc_g2, mod_bc(5), w=["bc_g2"])
        S.dma("sp", bc_ln2g, ln2g_d.partition_broadcast(128), w=["bc_ln2g"])
        S.dma("sp", bc_ln2b, ln2b_d.partition_broadcast(128), w=["bc_ln2b"])
        def f_tile(t):
            par = t % FW
            hn, hnk = hnt[par], ("hnt", par)
            S.dma("sp", hn, hn_scr[t * 128:(t + 1) * 128, :], w=[hnk])
            y1, y1k = y1t[par], ("y1t", par)
            y2, y2k = y2t[par], ("y2t", par)
            S.dma("pool", y1, ybuf, w=[y1k], indirect=("gather", dest_i[:, t:t + 1]))
            S.dma("pool", y2, ybuf, w=[y2k], indirect=("gather", dest_i[:, 16 + t:17 + t]))
            o_t, ok = ot[par], ("ot", par)
            S.op("dve", lambda e: e.tensor_tensor(out=y1, in0=y1, in1=y2, op=ALU.add), r=[y1k, y2k], w=[y1k])
            S.op("dve", lambda e: e.tensor_tensor(out=o_t, in0=y1, in1=bc_g2, op=ALU.mult),
                 r=["bc_g2", y1k], w=[ok])
            S.op("dve", lambda e: e.scalar_tensor_tensor(
                out=o_t, in0=hn, scalar=ALPHA, in1=o_t, op0=ALU.mult, op1=ALU.add), r=[hnk, ok], w=[ok])
            k, rstd, nbias = layer_norm_stats(o_t, ok)
            S.op("act", lambda e: e.activation(
                out=o_t, in_=o_t, func=AF.Identity, scale=rstd, bias=nbias), r=[ok, k], w=[ok])
            S.op("dve", lambda e: e.tensor_tensor(out=o_t, in0=o_t, in1=bc_ln2g, op=ALU.mult),
                 r=[ok, "bc_ln2g"], w=[ok])
            S.op("pool", lambda e: e.tensor_tensor(out=o_t, in0=o_t, in1=bc_ln2b, op=ALU.add),
                 r=[ok, "bc_ln2b"], w=[ok])
            S.dma("sp", out_d[t * 128:(t + 1) * 128, :], o_t, r=[ok], w=[("out", t)])

        seqs = []
        for t in range(16):
            S.record()
            f_tile(t)
            seqs.append(S.stop())
        S.play(seqs, FW)
        S.barrier()
    return nc


def _tables(rpb, core):
    col = np.arange(GW)
    col_start = np.clip(col - 8, 0, GW - 16)
    cmask = (col[None, :] >= col_start[:, None]) & (col[None, :] < col_start[:, None] + 16)
    col_off = np.clip(col[None, :] - col[:, None], -15, 15) + 15

    def table(c, rl, first, n):
        r = RPC * c + rl
        rs_ = min(max(r - 4, 0), ROWS - 8)
        out = np.full((128, NH, n, GW), NEG, np.float32)
        for j in range(n):
            for il in range(2):
                er = 2 * (first + j) + il
                gr = RPC * c - HALO + er
                if not (rs_ <= gr < rs_ + 8):
                    continue
                ro = gr - r + 7
                vals = rpb[:, ro, :][:, col_off]
                vals = np.where(cmask[None], vals, np.float32(NEG))
                out[il * 64:(il + 1) * 64, :, j, :] = np.transpose(vals, (2, 0, 1))
        return out

    tabE = table(1, 4, 2, 4)
    tabO = table(1, 5, 2, 5)
    tabS = np.full((7, 128, NH, 6, GW), NEG, np.float32)
    for rl, slot in SPECIAL_SLOT.items():
        f, n = SPECIAL[rl]
        tabS[slot, :, :, :n, :] = table(core, rl, f, n)
    return tabE, tabO, tabS


def _pool_fix(core):
    L = ROWS * GW
    fix = np.ones((128, 80), np.float32)
    if core == 0:
        fix[:, 0:8] = 0.0
    if core == NCORES - 1:
        fix[:, 8:16] = 0.0
    base = core * NOWN
    for g, w in enumerate((2, 4, 8, 16)):
        for side in range(2):
            for j in range(8):
                t = base + (j if side == 0 else NOWN - 8 + j)
                lo = min(max(t - w // 2, 0), L)
                hi = min(max(t + w - w // 2, 0), L)
                fix[:, 16 + g * 16 + side * 8 + j] = np.float32(w) / np.float32(hi - lo)
    return fix


def make_in_maps(inputs):
    f = lambda a: np.ascontiguousarray(np.asarray(a, dtype=np.float32))
    x = f(inputs["x"])[0]
    ctx = f(inputs["ctx"])[0]
    c = f(inputs["c"])[0]
    c_ctx = f(inputs["c_ctx"])
    cc = np.stack([c, c_ctx], axis=-1).reshape(8, 128, 2).transpose(1, 0, 2)
    b_modc = f(inputs["b_mod"])[0].reshape(48, 128).T
    rpb = f(inputs["rpb"])[0]
    shared = {
        "ctx": ctx, "cc": f(cc), "w_mod": f(inputs["w_mod"])[0], "b_modc": f(b_modc),
        "ln_in_g": f(inputs["ln_in_g"]), "ln_in_b": f(inputs["ln_in_b"]), "w_in": f(inputs["w_in"])[0],
        "w_pool_grp": f(inputs["w_pool_grp"])[0],
        "pool_scale_c": f(f(inputs["pool_scale"])[0].reshape(4, 128).T),
        "w_attn_proj": f(inputs["w_attn_proj"])[0], "w_pool_proj": f(inputs["w_pool_proj"])[0],
        "w_out": f(inputs["w_out"])[0],
        "ln1_g": f(inputs["ln1_g"])[0], "ln1_b": f(inputs["ln1_b"])[0],
        "ln2_g": f(inputs["ln2_g"])[0], "ln2_b": f(inputs["ln2_b"])[0],
        "w_r": f(np.concatenate([f(inputs["w_router_group"])[0], f(inputs["w_router_expert"])[0]], axis=1)),
        "b_r": f(np.concatenate([f(inputs["b_router_group"])[0], f(inputs["b_router_expert"])[0]], axis=0)),
        "w_expert_gate": f(inputs["w_expert_gate"])[0], "w_expert_up": f(inputs["w_expert_up"])[0],
        "w_expert_down": f(inputs["w_expert_down"])[0],
    }
    in_maps = []
    for core in range(NCORES):
        xe = np.zeros((NEXT, D), np.float32)
        g0 = (RPC * core - HALO) * GW
        lo, hi = max(g0, 0), min(g0 + NEXT, ROWS * GW)
        xe[lo - g0:hi - g0] = x[lo:hi]
        tabE, tabO, tabS = _tables(rpb, core)
        m = dict(shared)
        m.update({"x_ext": xe, "tab_even": tabE, "tab_odd": tabO, "tab_sp": tabS, "pool_fix": _pool_fix(core)})
        in_maps.append(m)
    return in_maps


def kernel(**inputs):
    in_maps = make_in_maps(inputs)
    nc = build_nc()
    res = run_bass_kernel_spmd(nc, in_maps, core_ids=list(range(NCORES)))
    out = np.concatenate([np.asarray(r["out"]) for r in res.results], axis=0)
    return out.reshape(1, ROWS * GW, D).astype(np.float32)
```

```python
import numpy as np
import concourse.bass as bass
import concourse.mybir as mybir
from concourse.bass_utils import run_bass_kernel_spmd
from contextlib import ExitStack

F32 = mybir.dt.float32
BF = mybir.dt.bfloat16
AF = mybir.ActivationFunctionType
ALU = mybir.AluOpType
AX = mybir.AxisListType

NCORES = 8
D = 1024
GW = 64
ROWS = 256
RPC = 32
HALO = 4
EXT_ROWS = 40
NEXT = EXT_ROWS * GW
NOWN = RPC * GW
NCTX = 256
NALL = NEXT + NCTX
OWN0 = HALO * GW
NH = 8
HD = 64
NE = 32
DE = 512
ALPHA = 2.0 ** 0.25
EPS = 1e-5
NEG = -30000.0
SCALE = HD ** -0.5

SPECIAL = {0: (0, 6), 1: (0, 6), 2: (1, 5), 3: (1, 5), 29: (14, 5), 30: (14, 5), 31: (14, 6)}
SPECIAL_SLOT = {0: 0, 1: 1, 2: 2, 3: 3, 29: 4, 30: 5, 31: 6}


def row_window(rl):
    if rl in SPECIAL:
        f, n = SPECIAL[rl]
        return f, n, "S"
    if rl % 2 == 0:
        return rl // 2, 4, "E"
    return (rl - 1) // 2, 5, "O"


class Sched:
    def __init__(self, nc, sems, dma_sems):
        self.nc = nc
        self.eng = {"pe": nc.tensor, "act": nc.scalar, "dve": nc.vector, "pool": nc.gpsimd, "sp": nc.sync}
        self.sem = dict(sems)
        self.cnt = {k: 0 for k in sems}
        self.ndma = len(dma_sems)
        for i, s in enumerate(dma_sems):
            self.sem[("dma", i)] = s
            self.cnt[("dma", i)] = 0
        self.next_dma = 0
        self.named = {}
        self.n_named = 16
        self.seen = {e: {} for e in self.eng}
        self.last_w = {}
        self.readers = {}
        self.ninst = 0
        self.region = None
        self.rstack = []
        self.rec = None

    def _emit(self, e, fn):
        if self.region is None:
            fn()
        else:
            self.region["items"][e].append(fn)
        self.ninst += 1

    def _wait(self, e, tok):
        k, v = tok
        if e == "pe" and k == "pe":
            return
        if self.seen[e].get(k, 0) >= v:
            return
        self._emit(e, lambda: self.eng[e].wait_ge(self.sem[k], v))
        self.seen[e][k] = v

    def begin_region(self):
        self.rstack.append({"items": {e: [] for e in self.eng}, "seen0": {e: dict(self.seen[e]) for e in self.eng},
                            "cnt0": dict(self.cnt), "dmas": {e: [] for e in self.eng}})
        self.region = self.rstack[-1]

    def end_region(self, cond_fn):
        reg = self.rstack.pop()
        parent = self.rstack[-1] if self.rstack else None
        self.region = parent
        for e, items in reg["items"].items():
            if not items:
                continue
            n = self.cnt[e] - reg["cnt0"][e]
            cnt0 = reg["cnt0"][e]
            dmas = list(reg["dmas"][e])

            def emit_e(e=e, items=items, n=n, cnt0=cnt0, dmas=dmas):
                eng = self.eng[e]
                with eng.If(cond_fn(e)):
                    for it in items:
                        it()
                with eng.Else():
                    if n > 0:
                        if cnt0 > 0:
                            eng.wait_ge(self.sem[e], cnt0)
                        eng.sem_inc(self.sem[e], n)
                    for (k, prev) in dmas:
                        if prev > 0:
                            eng.wait_ge(self.sem[k], prev)
                        eng.sem_inc(self.sem[k], 16)

            if parent is None:
                emit_e()
            else:
                parent["items"][e].append(emit_e)
                parent["dmas"][e].extend(dmas)
            self.seen[e] = reg["seen0"][e]

    def raw(self, e, fn, r=()):
        self._deps(e, r, ())
        self._emit(e, lambda: fn(self.eng[e]))

    def _deps(self, e, r, w):
        for key in r:
            t = self.last_w.get(key)
            if t is not None:
                self._wait(e, t)
        for key in w:
            t = self.last_w.get(key)
            if t is not None:
                self._wait(e, t)
            for t in self.readers.get(key, ()):
                self._wait(e, t)

    def _commit(self, tok, r, w):
        for key in r:
            self.readers.setdefault(key, []).append(tok)
            if len(self.readers[key]) > 12:
                best = {}
                for k, v in self.readers[key]:
                    if best.get(k, 0) < v:
                        best[k] = v
                self.readers[key] = list(best.items())
        for key in w:
            self.last_w[key] = tok
            self.readers[key] = []

    def record(self):
        self.rec = []

    def stop(self):
        r_, self.rec = self.rec, None
        return r_

    def play(self, seqs, width):
        seqs = [q for q in seqs if q]
        active, nxt = [], 0
        while nxt < len(seqs) or active:
            while len(active) < width and nxt < len(seqs):
                active.append([seqs[nxt], 0])
                nxt += 1
            for a in list(active):
                seq = a[0]
                while True:
                    kind, args, kw = seq[a[1]]
                    a[1] += 1
                    getattr(self, kind)(*args, **kw)
                    glue = kind == "op" and not kw.get("inc", True)
                    if a[1] >= len(seq) or not glue:
                        break
                if a[1] >= len(seq):
                    active.remove(a)

    def op(self, e, fn, r=(), w=(), inc=True):
        if self.rec is not None:
            self.rec.append(("op", (e, fn), dict(r=list(r), w=list(w), inc=inc)))
            return
        self._deps(e, r, w)
        if inc:
            self.cnt[e] += 1
            self._emit(e, lambda: fn(self.eng[e]).then_inc(self.sem[e], 1))
            tok = (e, self.cnt[e])
        else:
            self._emit(e, lambda: fn(self.eng[e]))
            tok = (e, self.cnt[e] + 1)
        self._commit(tok, r, w)

    def dma(self, q, out, in_, r=(), w=(), indirect=None, semkey=None):
        if self.rec is not None:
            self.rec.append(("dma", (q, out, in_), dict(r=list(r), w=list(w), indirect=indirect, semkey=semkey)))
            return
        self._deps(q, r, w)
        if semkey is not None:
            if semkey not in self.named:
                assert len(self.named) < self.n_named
                self.named[semkey] = len(self.named)
            i = self.named[semkey]
        else:
            i = self.n_named + self.next_dma
            self.next_dma = (self.next_dma + 1) % (self.ndma - self.n_named)
        k = ("dma", i)
        prev = self.cnt[k]
        if prev > 0:
            self._wait(q, (k, prev))
        self.cnt[k] += 16
        if self.region is not None:
            self.region["dmas"][q].append((k, prev))
        if indirect is None:
            self._emit(q, lambda: self.eng[q].dma_start(out=out, in_=in_).then_inc(self.sem[k], 16))
        elif indirect[0] == "gather":
            self._emit(q, lambda: self.eng[q].indirect_dma_start(
                out=out, out_offset=None, in_=in_,
                in_offset=bass.IndirectOffsetOnAxis(ap=indirect[1], axis=0)).then_inc(self.sem[k], 16))
        else:
            self._emit(q, lambda: self.eng[q].indirect_dma_start(
                out=out, out_offset=bass.IndirectOffsetOnAxis(ap=indirect[1], axis=0), in_=in_,
                in_offset=None).then_inc(self.sem[k], 16))
        self._commit((k, self.cnt[k]), r, w)

    def barrier(self):
        for e in self.eng:
            for k, v in self.cnt.items():
                if v > 0 and k != e:
                    self._wait(e, (k, v))
        self.last_w = {}
        self.readers = {}


def carve(arena, off, shape, dt):
    esz = 4 if dt == F32 else 2
    n = int(np.prod(shape[1:]))
    nbytes = n * esz
    assert off % 4 == 0 and nbytes % 4 == 0
    assert off + nbytes <= arena.shape[1] * 4, (off, nbytes, arena.shape)
    ap = arena[:, off // 4:(off + nbytes) // 4]
    if dt != F32:
        ap = ap.bitcast(dt)
    if len(shape) == 3:
        ap = ap.rearrange("p (a b) -> p a b", b=shape[2])
    elif len(shape) == 4:
        ap = ap.rearrange("p (a b c) -> p a b c", b=shape[2], c=shape[3])
    return ap


A1_BYTES = 8 * NALL * 2
A2_BYTES = 65536
A3_BYTES = 49152
A4_BYTES = 43008


def build_nc(debug=False, stop_after=None):
    nc = bass.Bass("TRN2", target_bir_lowering=False)

    def din(name, shape, dt=F32):
        return nc.dram_tensor(name, list(shape), dt, kind="ExternalInput").ap()

    x_ext = din("x_ext", [NEXT, D])
    ctx_d = din("ctx", [NCTX, D])
    cc_d = din("cc", [128, 8, 2])
    w_mod = din("w_mod", [D, 6 * D])
    bmod_d = din("b_modc", [128, 48])
    lning_d = din("ln_in_g", [D])
    lninb_d = din("ln_in_b", [D])
    w_in = din("w_in", [D, 4096])
    tabE_d = din("tab_even", [128, 8, 4, 64])
    tabO_d = din("tab_odd", [128, 8, 5, 64])
    tabS_d = din("tab_sp", [7, 128, 8, 6, 64])
    wgrp_d = din("w_pool_grp", [4, 128, 128])
    pscale_d = din("pool_scale_c", [128, 4])
    pfix_d = din("pool_fix", [128, 16 + 64])
    wa_d = din("w_attn_proj", [512, D])
    wp_d = din("w_pool_proj", [512, D])
    wout_d = din("w_out", [D, D])
    ln1g_d = din("ln1_g", [D])
    ln1b_d = din("ln1_b", [D])
    ln2g_d = din("ln2_g", [D])
    ln2b_d = din("ln2_b", [D])
    wr_d = din("w_r", [D, 36])
    br_d = din("b_r", [36])
    n_exp_in = NE if stop_after in (None, "E") else 1
    weg_d = din("w_expert_gate", [n_exp_in, D, DE])
    weu_d = din("w_expert_up", [n_exp_in, D, DE])
    wed_d = din("w_expert_down", [n_exp_in, DE, D])
    out_d = nc.dram_tensor("out", [NOWN, D], F32, kind="ExternalOutput").ap()
    h_scr = nc.dram_tensor("h_scr", [NOWN, D], F32, kind="Internal").ap()
    hn_scr = nc.dram_tensor("hn_scr", [NOWN, D], F32, kind="Internal").ap()
    mod_scr = nc.dram_tensor("mod_scr", [48 * 128], F32, kind="Internal").ap()
    dbg = {}
    if debug:
        for nm, shp, dt in [("dbg_hmT", [128, 8, NALL], BF), ("dbg_qT", [128, 4, NOWN], BF),
                            ("dbg_kT", [128, 4, NALL], BF), ("dbg_V", [128, 22, 8, 65], BF),
                            ("dbg_ypoolT", [128, 4, NOWN], BF), ("dbg_yattnT", [128, 4, NOWN], BF),
                            ("dbg_mod", [128, 48, 2], F32), ("dbg_mergedT", [128, 8, NOWN], BF),
                            ("dbg_hm2T", [128, 8, NOWN], BF), ("dbg_gates", [128, 16, 32], F32),
                            ("dbg_yacc", [128, 16, D], F32)]:
            dbg[nm] = nc.dram_tensor(nm, shp, dt, kind="ExternalOutput").ap()

    es = ExitStack()
    with es:
        def sb(name, shape, dt=F32):
            return es.enter_context(nc.sbuf_tensor(name, list(shape), dt))

        A1 = sb("A1", [128, A1_BYTES // 4])
        A2 = sb("A2", [128, A2_BYTES // 4])
        A3 = sb("A3", [128, A3_BYTES // 4])
        A4 = sb("A4", [128, A4_BYTES // 4])
        ident_b = sb("ident_b", [128, 128], BF)
        ident_f = sb("ident_f", [128, 128], F32)
        mhalf = sb("mhalf", [128, 1])
        cc_t = sb("cc_t", [128, 8, 2])
        sc_t = sb("sc_t", [128, 8, 2])
        bmod_t = sb("bmod_t", [128, 48])
        modT = sb("modT", [128, 48, 2])
        modL = sb("modL", [128, 48])
        modrow = sb("modrow", [48, 128])
        A1c = sb("A1c", [128, 8, 2])
        stats = sb("stats", [128, 8, 2, 6])
        mv = sb("mv", [128, 8, 2])
        rs = sb("rs", [128, 8, 4])
        pscale_t = sb("pscale_t", [128, 4])
        pfix_t = sb("pfix_t", [128, 80])
        wgrp_t = sb("wgrp_t", [128, 4, 128], BF)
        wr_t = sb("wr_t", [128, 8, 36], BF)
        br_t = sb("br_t", [128, 36])
        gates_all = sb("gates_all", [128, 16, 32])
        rt = sb("rt", [128, 2, 96])
        rden = sb("rden", [64, 2, 8])
        ps = [es.enter_context(nc.psum_tensor(f"ps{i}", [128, 512], F32)) for i in range(8)]
        sem_names = ["pe", "act", "dve", "pool", "sp"]
        sems = {k: es.enter_context(nc.semaphore("s_" + k)) for k in sem_names}
        dma_sems = [es.enter_context(nc.semaphore(f"dq{i}")) for i in range(40)]
        S = Sched(nc, sems, dma_sems)

        def dbg_dump(name, ap, key):
            if debug:
                S.dma("sp", dbg[name], ap, r=[key])

        S.op("pool", lambda e: e.memset(ident_b[:], 0.0), w=["ident_b"])
        S.op("pool", lambda e: e.affine_select(out=ident_b[:], in_=ident_b[:], pattern=[[-1, 128]],
                                              compare_op=ALU.not_equal, fill=1.0, base=0, channel_multiplier=1),
             r=["ident_b"], w=["ident_b"])
        S.op("pool", lambda e: e.memset(ident_f[:], 0.0), w=["ident_f"])
        S.op("pool", lambda e: e.affine_select(out=ident_f[:], in_=ident_f[:], pattern=[[-1, 128]],
                                              compare_op=ALU.not_equal, fill=1.0, base=0, channel_multiplier=1),
             r=["ident_f"], w=["ident_f"])
        S.op("pool", lambda e: e.memset(mhalf[:], -0.5), w=["mhalf"])

        S.dma("sp", cc_t[:], cc_d, w=["cc"])
        S.dma("sp", bmod_t[:], bmod_d, w=["bmod"])
        S.dma("sp", pscale_t[:], pscale_d, w=["pscale"])
        S.dma("sp", pfix_t[:], pfix_d, w=["pfix"])
        S.dma("sp", br_t[:], br_d.partition_broadcast(128), w=["br"])
        S.dma("pool", wgrp_t[:], wgrp_d.rearrange("g i o -> i g o"), w=["wgrp"])
        S.dma("pool", wr_t[:], wr_d.rearrange("(k p) n -> p k n", p=128), w=["wr"])

        S.op("act", lambda e: e.activation(out=sc_t[:], in_=cc_t[:], func=AF.Silu), r=["cc"], w=["sc"])
        wm = [carve(A2, 0, [128, 8, 1024], F32), carve(A2, 32768, [128, 8, 1024], F32)]
        psm = ps[0]
        for j in range(6):
            S.dma("sp", wm[j % 2], w_mod[:, j * 1024:(j + 1) * 1024].rearrange("(k p) n -> p k n", p=128),
                  w=[("wm", j % 2)])
            for mcl in range(8):
                mc = j * 8 + mcl
                for kc in range(8):
                    S.op("pe", lambda e, mc=mc, mcl=mcl, kc=kc, j=j: e.matmul(
                        psm[:, mc * 2:(mc + 1) * 2], lhsT=wm[j % 2][:, kc, mcl * 128:(mcl + 1) * 128],
                        rhs=sc_t[:, kc, :], start=(kc == 0), stop=(kc == 7)),
                        r=[("wm", j % 2), "sc"], w=["psm"], inc=(kc == 7 and mcl == 7))
        S.op("dve", lambda e: e.tensor_tensor(
            out=modT[:], in0=psm[:, 0:96].rearrange("p (a b) -> p a b", b=2),
            in1=bmod_t[:].unsqueeze(2).broadcast_to([128, 48, 2]), op=ALU.add),
            r=["psm", "bmod"], w=["modT"])
        S.op("dve", lambda e: e.tensor_scalar(out=A1c[:], in0=modT[:, 8:16, :], scalar1=1.0, scalar2=None,
                                             op0=ALU.add), r=["modT"], w=["A1c"])
        S.op("dve", lambda e: e.tensor_copy(out=modL[:], in_=modT[:, :, 0]), r=["modT"], w=["modL"])
        S.op("pe", lambda e: e.transpose(out=ps[1][0:48, 0:128], in_=modL[:], identity=ident_f[:]),
             r=["modL", "ident_f"], w=["ps1"])
        S.op("dve", lambda e: e.tensor_copy(out=modrow[:], in_=ps[1][0:48, 0:128]), r=["ps1"], w=["modrow"])
        S.dma("sp", mod_scr.rearrange("(m p) -> m p", p=128), modrow[:], r=["modrow"], w=["mod_scr"])
        dbg_dump("dbg_mod", modT[:], "modT")

        def mod_bc(idx):
            return mod_scr[idx * 1024:(idx + 1) * 1024].partition_broadcast(128)

        S.barrier()
        if stop_after == "mod":
            return nc

        hmT = carve(A1, 0, [128, 8, NALL], BF)
        xt = carve(A4, 0, [128, 8, 1024], F32)
        ginb = carve(A4, 32768, [128, 1024], F32)
        binb = carve(A4, 36864, [128, 1024], F32)
        S.dma("sp", ginb, lning_d.partition_broadcast(128), w=["ginb"])
        S.dma("sp", binb, lninb_d.partition_broadcast(128), w=["binb"])

        slot_ctr = [0]

        def layer_norm_stats(src, srckey):
            sl = slot_ctr[0] % 8
            slot_ctr[0] += 1
            k = ("ln", sl)
            S.op("dve", lambda e: e.bn_stats(out=stats[:, sl, 0, :], in_=src[:, 0:512]), r=[srckey], w=[k])
            S.op("dve", lambda e: e.bn_stats(out=stats[:, sl, 1, :], in_=src[:, 512:1024]), r=[srckey], w=[k])
            S.op("dve", lambda e: e.bn_aggr(out=mv[:, sl, :], in_=stats[:, sl, :, :]), r=[k], w=[k])
            S.op("pool", lambda e: e.tensor_scalar(out=rs[:, sl, 0:1], in0=mv[:, sl, 1:2], scalar1=EPS,
                                                  scalar2=None, op0=ALU.add), r=[k], w=[k])
            S.op("pool", lambda e: e.tensor_tensor(out=rs[:, sl, 1:2], in0=rs[:, sl, 0:1], in1=mhalf[:],
                                                  op=ALU.pow), r=[k, "mhalf"], w=[k])
            S.op("dve", lambda e: e.tensor_scalar(out=rs[:, sl, 2:3], in0=mv[:, sl, 0:1], scalar1=rs[:, sl, 1:2],
                                                 scalar2=-1.0, op0=ALU.mult, op1=ALU.mult), r=[k], w=[k])
            return k, rs[:, sl, 1:2], rs[:, sl, 2:3]

        groups = [list(range(g * 4, g * 4 + 4)) for g in range(5)] + [[20, 21]]
        def a_group(gi, tiles):
            which = 1 if gi == 5 else 0
            bufs = []
            for i, t in enumerate(tiles):
                bi = (gi % 2) * 4 + i
                buf = xt[:, bi, :]
                bk = ("xt", bi)
                bufs.append((buf, bk))
                src = x_ext[t * 128:(t + 1) * 128, :] if t < 20 else ctx_d[(t - 20) * 128:(t - 19) * 128, :]
                S.dma("sp", buf, src, w=[bk])
                k, rstd, nbias = layer_norm_stats(buf, bk)
                S.op("act", lambda e, buf=buf, rstd=rstd, nbias=nbias: e.activation(
                    out=buf, in_=buf, func=AF.Identity, scale=rstd, bias=nbias), r=[bk, k], w=[bk])
                S.op("dve", lambda e, buf=buf: e.tensor_tensor(out=buf, in0=buf, in1=ginb, op=ALU.mult),
                     r=[bk, "ginb"], w=[bk])
                S.op("pool", lambda e, buf=buf: e.tensor_tensor(out=buf, in0=buf, in1=binb, op=ALU.add),
                     r=[bk, "binb"], w=[bk])
                if 2 <= t < 18:
                    S.dma("sp", h_scr[(t - 2) * 128:(t - 1) * 128, :], buf, r=[bk], w=[("h_scr", t - 2)])
            ntok = 128 * len(tiles)
            tok0 = tiles[0] * 128
            for kc in range(8):
                pb = ps[kc % 4]
                pk = ("ps", kc % 4)
                for i, (buf, bk) in enumerate(bufs):
                    S.op("pe", lambda e, pb=pb, i=i, buf=buf, kc=kc: e.transpose(
                        out=pb[:, i * 128:(i + 1) * 128], in_=buf[:, kc * 128:(kc + 1) * 128], identity=ident_f[:]),
                        r=[bk, "ident_f"], w=[pk], inc=(i == len(bufs) - 1))
                S.op("act", lambda e, pb=pb, kc=kc, ntok=ntok, tok0=tok0, which=which: e.activation(
                    out=hmT[:, kc, tok0:tok0 + ntok], in_=pb[:, 0:ntok], func=AF.Identity,
                    scale=A1c[:, kc, which:which + 1], bias=modT[:, kc, which:which + 1]),
                    r=[pk, "A1c", "modT"], w=[("hmT", gi)])

        seqs = []
        for gi, tiles in enumerate(groups):
            S.record()
            a_group(gi, tiles)
            seqs.append(S.stop())
        S.play(seqs, 2)
        dbg_dump("dbg_hmT", hmT, ("hmT", 0))
        S.barrier()
        if stop_after == "A":
            return nc

        qT = carve(A2, 0, [128, 4, NOWN], BF)
        kT = carve(A2, 16384, [128, 4, NALL], BF)
        V = carve(A2, 38912, [128, 176, 65], BF)
        ypoolT = carve(A3, 0, [128, 4, NOWN], BF)
        yattnT = carve(A3, 16384, [128, 4, NOWN], BF)
        pu = carve(A3, 16384, [128, NEXT], F32)
        pa = carve(A3, 16384 + 10240, [128, NEXT], F32)
        pbuf = carve(A3, 16384 + 20480, [128, NEXT], F32)
        wblk = [carve(A4, 0, [128, 8, 512], BF), carve(A4, 8192, [128, 8, 512], BF)]
        pooledT = carve(A4, 16384, [128, NOWN], BF)
        tmp8 = carve(A4, 20480, [128, 16], F32)

        def load_wblk(j):
            S.dma("pool", wblk[j % 2], w_in[:, j * 512:(j + 1) * 512].rearrange("(k p) n -> p k n", p=128),
                  w=[("wblk", j % 2)])

        evac_ctr = [0]

        def evac_copy(out, in_, r, w, eng=None):
            evac_ctr[0] += 1
            if (eng == "act") or (eng is None and evac_ctr[0] % 2 == 0):
                S.op("act", lambda e: e.activation(out=out, in_=in_, func=AF.Copy), r=r, w=w)
            else:
                S.op("dve", lambda e: e.tensor_copy(out=out, in_=in_), r=r, w=w)

        pctr = [0]

        def next_ps(nbanks=8):
            i = pctr[0] % nbanks
            pctr[0] += 1
            return ps[i], ("ps", i)

        load_wblk(0)
        load_wblk(1)
        S.op("pool", lambda e: e.memset(V[:, :, 64:65], 1.0), w=["Vones"])
        for mc in range(4):
            for tb in range(4):
                pb, pk = next_ps()
                for kc in range(8):
                    S.op("pe", lambda e, pb=pb, kc=kc, mc=mc, tb=tb: e.matmul(
                        pb[:, :], lhsT=wblk[0][:, kc, mc * 128:(mc + 1) * 128],
                        rhs=hmT[:, kc, OWN0 + tb * 512:OWN0 + (tb + 1) * 512], start=(kc == 0), stop=(kc == 7)),
                        r=[("wblk", 0), "hmT"], w=[pk], inc=(kc == 7))
                evac_copy(qT[:, mc, tb * 512:(tb + 1) * 512], pb[:, :], [pk], ["qT"])
        load_wblk(2)
        for mc in range(4):
            for blk in range(6):
                n = 512 if blk < 5 else 256
                t0 = blk * 512
                pb, pk = next_ps()
                for kc in range(8):
                    S.op("pe", lambda e, pb=pb, kc=kc, mc=mc, t0=t0, n=n: e.matmul(
                        pb[:, 0:n], lhsT=wblk[1][:, kc, mc * 128:(mc + 1) * 128],
                        rhs=hmT[:, kc, t0:t0 + n], start=(kc == 0), stop=(kc == 7)),
                        r=[("wblk", 1), "hmT"], w=[pk], inc=(kc == 7))
                evac_copy(kT[:, mc, t0:t0 + n], pb[:, 0:n], [pk], ["kT"])
        load_wblk(3)
        for t in range(22):
            pb, pk = next_ps()
            for kc in range(8):
                S.op("pe", lambda e, pb=pb, kc=kc, t=t: e.matmul(
                    pb[:, :], lhsT=hmT[:, kc, t * 128:(t + 1) * 128], rhs=wblk[0][:, kc, :],
                    start=(kc == 0), stop=(kc == 7)),
                    r=[("wblk", 0), "hmT"], w=[pk], inc=(kc == 7))
            evac_copy(V[:, t * 8:(t + 1) * 8, 0:64], pb[:, :].rearrange("p (h d) -> p h d", d=64), [pk], [("V", t)])
        for g in range(4):
            wdw = 2 ** (g + 1)
            hw = wdw // 2
            for blk in range(5):
                pb, pk = next_ps()
                for kc in range(8):
                    S.op("pe", lambda e, pb=pb, kc=kc, g=g, blk=blk: e.matmul(
                        pb[:, :], lhsT=wblk[1][:, kc, g * 128:(g + 1) * 128],
                        rhs=hmT[:, kc, blk * 512:(blk + 1) * 512], start=(kc == 0), stop=(kc == 7)),
                        r=[("wblk", 1), "hmT"], w=[pk], inc=(kc == 7))
                evac_copy(pu[:, blk * 512:(blk + 1) * 512], pb[:, :], [pk], ["pu"])
            S.op("dve", lambda e: e.tensor_tensor(out=pu[:, OWN0 - 8:OWN0], in0=pu[:, OWN0 - 8:OWN0],
                                                 in1=pfix_t[:, 0:8], op=ALU.mult), r=["pu", "pfix"], w=["pu"])
            S.op("dve", lambda e: e.tensor_tensor(out=pu[:, OWN0 + NOWN:OWN0 + NOWN + 8],
                                                 in0=pu[:, OWN0 + NOWN:OWN0 + NOWN + 8],
                                                 in1=pfix_t[:, 8:16], op=ALU.mult), r=["pu", "pfix"], w=["pu"])
            lo = OWN0 - 16
            src, srck = pu, "pu"
            dsts = [(pa, "pa"), (pbuf, "pb")]
            for k in range(g + 1):
                sh = 2 ** k
                hi = OWN0 + NOWN + 32 - 2 * sh * 2
                dst, dstk = dsts[k % 2]
                S.op("dve", lambda e, src=src, dst=dst, sh=sh, hi=hi: e.tensor_tensor(
                    out=dst[:, lo:hi], in0=src[:, lo:hi], in1=src[:, lo + sh:hi + sh], op=ALU.add),
                    r=[srck], w=[dstk])
                src, srck = dst, dstk
            S.op("dve", lambda e, src=src: e.scalar_tensor_tensor(
                out=pooledT[:, :], in0=src[:, OWN0 - hw:OWN0 + NOWN - hw], scalar=1.0 / wdw,
                in1=pu[:, OWN0:OWN0 + NOWN], op0=ALU.mult, op1=ALU.subtract), r=[srck, "pu"], w=["pooledT"])
            for side in range(2):
                o0 = OWN0 if side == 0 else OWN0 + NOWN - 8
                S.op("dve", lambda e, src=src, o0=o0, side=side: e.tensor_tensor(
                    out=tmp8[:, side * 8:(side + 1) * 8], in0=src[:, o0 - hw:o0 - hw + 8],
                    in1=pfix_t[:, 16 + g * 16 + side * 8:16 + g * 16 + side * 8 + 8], op=ALU.mult),
                    r=[srck, "pfix"], w=["tmp8"])
                S.op("dve", lambda e, o0=o0, side=side: e.scalar_tensor_tensor(
                    out=pooledT[:, o0 - OWN0:o0 - OWN0 + 8], in0=tmp8[:, side * 8:(side + 1) * 8],
                    scalar=1.0 / wdw, in1=pu[:, o0:o0 + 8], op0=ALU.mult, op1=ALU.subtract),
                    r=["tmp8", "pu"], w=["pooledT"])
            for tb in range(4):
                pb, pk = next_ps()
                S.op("pe", lambda e, pb=pb, tb=tb: e.matmul(
                    pb[:, :], lhsT=wgrp_t[:, g, :], rhs=pooledT[:, tb * 512:(tb + 1) * 512], start=True, stop=True),
                    r=["wgrp", "pooledT"], w=[pk])
                S.op("act", lambda e, pb=pb, tb=tb: e.activation(
                    out=ypoolT[:, g, tb * 512:(tb + 1) * 512], in_=pb[:, :], func=AF.Identity,
                    scale=pscale_t[:, g:g + 1]), r=[pk, "pscale"], w=["ypoolT"])
        dbg_dump("dbg_qT", qT, "qT")
        dbg_dump("dbg_kT", kT, "kT")
        if debug:
            S.barrier()
            S.dma("sp", dbg["dbg_V"].rearrange("p t h d -> p (t h) d"), V)
        dbg_dump("dbg_ypoolT", ypoolT, "ypoolT")
        S.barrier()
        if stop_after == "B":
            return nc

        tabE = carve(A4, 0, [128, 8, 256], F32)
        tabO = carve(A4, 8192, [128, 8, 320], F32)
        tabS2 = [carve(A4, 18432, [128, 8, 384], F32), carve(A3, 32768, [128, 8, 384], F32)]
        sbuf_s = [carve(A4, 30720 + i * 1536, [128, 384], F32) for i in range(2)]
        pT = [carve(A4, 33792 + i * 1024, [128, 512], BF) for i in range(3)]
        Obf = [carve(A4, 36864 + i * 1024, [128, 512], BF) for i in range(2)]
        S.dma("sp", tabE, tabE_d.rearrange("p h j q -> p h (j q)"), w=["tabE"])
        S.dma("sp", tabO, tabO_d.rearrange("p h j q -> p h (j q)"), w=["tabO"])
        units = [(rl, h) for rl in range(RPC) for h in range(NH)]

        def emit_qk(u):
            rl, h = units[u]
            f, nl, kind = row_window(rl)
            hp, half = h // 2, h % 2
            pb = ps[u % 3]
            pk = ("ps", u % 3)
            if kind == "S" and h == 0:
                sp_ = SPECIAL_SLOT[rl] % 2
                S.dma("sp", tabS2[sp_], tabS_d[SPECIAL_SLOT[rl]].rearrange("p h j q -> p h (j q)"), w=[("tabS", sp_)])
            q_ap = qT[half * 64:(half + 1) * 64, hp, rl * 64:(rl + 1) * 64]
            for j in range(nl + 2):
                t0 = (f + j) * 128 if j < nl else NEXT + (j - nl) * 128
                S.op("pe", lambda e, pb=pb, j=j, t0=t0: e.matmul(
                    pb[:, j * 64:(j + 1) * 64], lhsT=kT[half * 64:(half + 1) * 64, hp, t0:t0 + 128], rhs=q_ap,
                    start=True, stop=True), r=["qT", "kT"], w=[pk], inc=(j == nl + 1))

        def emit_softmax(u):
            rl, h = units[u]
            f, nl, kind = row_window(rl)
            pb = ps[u % 3]
            pk = ("ps", u % 3)
            if kind == "S":
                sp_ = SPECIAL_SLOT[rl] % 2
                tab, tk = tabS2[sp_], ("tabS", sp_)
            else:
                tab, tk = {"E": (tabE, "tabE"), "O": (tabO, "tabO")}[kind]
            sbt = sbuf_s[u % 2]
            sk = ("sbs", u % 2)
            pt = pT[u % 3]
            ptk = ("pT", u % 3)
            S.op("dve", lambda e: e.scalar_tensor_tensor(
                out=sbt[:, 0:nl * 64], in0=pb[:, 0:nl * 64], scalar=SCALE,
                in1=tab[:, h, 0:nl * 64], op0=ALU.mult, op1=ALU.add), r=[pk, tk], w=[sk])
            S.op("act", lambda e: e.activation(out=pt[:, 0:nl * 64], in_=sbt[:, 0:nl * 64], func=AF.Exp),
                 r=[sk], w=[ptk])
            S.op("act", lambda e: e.activation(out=pt[:, nl * 64:(nl + 2) * 64], in_=pb[:, nl * 64:(nl + 2) * 64],
                                              func=AF.Exp, scale=SCALE), r=[pk], w=[ptk])

        def emit_pv(u):
            rl, h = units[u]
            f, nl, kind = row_window(rl)
            pt = pT[u % 3]
            ptk = ("pT", u % 3)
            bank = 3 + (rl % 2) * 2 + h // 4
            ob = ps[bank]
            hh = h % 4
            for j in range(nl + 2):
                vt = f + j if j < nl else 20 + (j - nl)
                S.op("pe", lambda e, j=j, vt=vt: e.matmul(
                    ob[0:64, hh * 65:(hh + 1) * 65], lhsT=pt[:, j * 64:(j + 1) * 64], rhs=V[:, vt * 8 + h, :],
                    start=(j == 0), stop=(j == nl + 1)),
                    r=[ptk, ("V", vt), "Vones"], w=[("ps", bank)], inc=(j == nl + 1))

        def emit_norm(rl):
            par = rl % 2
            for b in range(2):
                bank = 3 + par * 2 + b
                ob = ps[bank][0:64, 0:260].rearrange("p (h d) -> p h d", d=65)
                S.op("dve", lambda e, ob=ob, b=b: e.reciprocal(out=rden[:, par, b * 4:(b + 1) * 4], in_=ob[:, :, 64]),
                     r=[("ps", bank)], w=[("rden", par, b)])
                S.op("dve", lambda e, ob=ob, b=b: e.tensor_tensor(
                    out=Obf[par][0:64, b * 256:(b + 1) * 256].rearrange("p (h d) -> p h d", d=64),
                    in0=ob[:, :, 0:64],
                    in1=rden[:, par, b * 4:(b + 1) * 4].unsqueeze(2).broadcast_to([64, 4, 64]), op=ALU.mult),
                    r=[("ps", bank), ("rden", par, b)], w=[("Obf", par)])

        def emit_finish(rl):
            par = rl % 2
            pst = ps[7][:, 0:128].bitcast(BF)
            for mc in range(4):
                S.op("pe", lambda e, mc=mc: e.transpose(
                    out=pst[:, mc * 64:(mc + 1) * 64], in_=Obf[par][0:64, mc * 128:(mc + 1) * 128],
                    identity=ident_b[0:64, 0:64]), r=[("Obf", par), "ident_b"], w=[("ps", 7)], inc=(mc == 3))
            evac_copy(yattnT[:, :, rl * 64:(rl + 1) * 64], pst.rearrange("p (a b) -> p a b", b=64),
                      [("ps", 7)], ["yattnT"])

        nu = len(units)
        emit_qk(0)
        emit_qk(1)
        for u in range(nu):
            rl, h = units[u]
            if u + 2 < nu:
                emit_qk(u + 2)
            emit_softmax(u)
            emit_pv(u)
            if h == 7:
                emit_norm(rl)
            if h == 1 and rl > 0:
                emit_finish(rl - 1)
        emit_finish(RPC - 1)
        dbg_dump("dbg_yattnT", yattnT, "yattnT")
        S.barrier()
        if stop_after == "C":
            return nc

        wa_t = carve(A2, 0, [128, 4, 1024], BF)
        wp_t = carve(A2, 8192, [128, 4, 1024], BF)
        mergedT = carve(A2, 16384, [128, 8, NOWN], BF)
        wout_b = carve(A2, 49152, [128, 8, 1024], BF)
        wga = carve(A4, 0, [128, 8, 1024], BF)
        wgb = carve(A4, 16384, [128, 8, 1024], BF)
        sg = [carve(A4, 32768 + i * 1024, [128, 512], BF) for i in range(4)]
        t12 = [carve(A4, 36864 + i * 2048, [128, 512], F32) for i in range(3)]
        S.dma("pool", wga, w_in[:, 2048:3072].rearrange("(k p) n -> p k n", p=128), w=["wga"])
        S.dma("pool", wgb, w_in[:, 3072:4096].rearrange("(k p) n -> p k n", p=128), w=["wgb"])
        S.dma("pool", wa_t, wa_d.rearrange("(k p) n -> p k n", p=128), w=["wa"])
        S.dma("pool", wp_t, wp_d.rearrange("(k p) n -> p k n", p=128), w=["wp"])
        un = 0
        for tb in range(4):
            tsl = slice(tb * 512, (tb + 1) * 512)
            hsl = slice(OWN0 + tb * 512, OWN0 + (tb + 1) * 512)
            for mc in range(8):
                base = (un % 2) * 4
                un += 1
                pg, pgb, pya, pyp = ps[base], ps[base + 1], ps[base + 2], ps[base + 3]
                kg, kgb, kya, kyp = [("ps", base + i) for i in range(4)]
                msl = slice(mc * 128, (mc + 1) * 128)
                for kc in range(8):
                    S.op("pe", lambda e, kc=kc, pg=pg, msl=msl, hsl=hsl: e.matmul(
                        pg[:, :], lhsT=wga[:, kc, msl], rhs=hmT[:, kc, hsl], start=(kc == 0), stop=(kc == 7)),
                        r=["wga", "hmT"], w=[kg], inc=(kc == 7))
                for kc in range(8):
                    S.op("pe", lambda e, kc=kc, pgb=pgb, msl=msl, hsl=hsl: e.matmul(
                        pgb[:, :], lhsT=wgb[:, kc, msl], rhs=hmT[:, kc, hsl], start=(kc == 0), stop=(kc == 7)),
                        r=["wgb", "hmT"], w=[kgb], inc=(kc == 7))
                for kc in range(4):
                    S.op("pe", lambda e, kc=kc, pya=pya, msl=msl, tsl=tsl: e.matmul(
                        pya[:, :], lhsT=wa_t[:, kc, msl], rhs=yattnT[:, kc, tsl], start=(kc == 0), stop=(kc == 3)),
                        r=["wa", "yattnT"], w=[kya], inc=(kc == 3))
                for kc in range(4):
                    S.op("pe", lambda e, kc=kc, pyp=pyp, msl=msl, tsl=tsl: e.matmul(
                        pyp[:, :], lhsT=wp_t[:, kc, msl], rhs=ypoolT[:, kc, tsl], start=(kc == 0), stop=(kc == 3)),
                        r=["wp", "ypoolT"], w=[kyp], inc=(kc == 3))
                s0, s1 = sg[(un % 2) * 2], sg[(un % 2) * 2 + 1]
                ks0, ks1 = ("sg", (un % 2) * 2), ("sg", (un % 2) * 2 + 1)
                S.op("act", lambda e, s0=s0, pg=pg: e.activation(out=s0, in_=pg[:, :], func=AF.Sigmoid),
                     r=[kg], w=[ks0])
                S.op("act", lambda e, s1=s1, pgb=pgb: e.activation(out=s1, in_=pgb[:, :], func=AF.Sigmoid),
                     r=[kgb], w=[ks1])
                ta, tbb = t12[0], t12[1]
                S.op("dve", lambda e, s0=s0, pya=pya: e.tensor_tensor(out=ta, in0=s0, in1=pya[:, :], op=ALU.mult),
                     r=[ks0, kya], w=["t12a"])
                S.op("dve", lambda e, s1=s1, pyp=pyp: e.tensor_tensor(out=tbb, in0=s1, in1=pyp[:, :], op=ALU.mult),
                     r=[ks1, kyp], w=["t12b"])
                S.op("dve", lambda e, mc=mc, tsl=tsl: e.tensor_tensor(out=mergedT[:, mc, tsl], in0=ta, in1=tbb,
                                                                    op=ALU.add),
                     r=["t12a", "t12b"], w=["mergedT"])
        dbg_dump("dbg_mergedT", mergedT, "mergedT")
        S.barrier()
        if stop_after == "D1":
            return nc

        I32 = mybir.dt.int32
        hm2Tt = [carve(A1, i * 2048, [128, 8, 128], BF) for i in range(2)]
        oh1_all = carve(A1, 4096, [128, 16, 32], F32)
        oh2_all = carve(A1, 6144, [128, 16, 32], F32)
        Pall = carve(A1, 8192, [128, 16, 32], F32)
        tmpP = carve(A1, 10240, [128, 16, 32], F32)
        Mbf = carve(A1, 12288, [128, 16, 32], BF)
        c12 = carve(A1, 13312, [128, 2, 16], F32)
        Crun = carve(A1, 13440, [128, 32], F32)
        ebase = carve(A1, 13568, [128, 32], F32)
        ebase_i = carve(A1, 13696, [128, 32], F32).bitcast(I32)
        destf = carve(A1, 13824, [128, 32], F32)
        dest_i = carve(A1, 13952, [128, 32], F32).bitcast(I32)
        tokid_i = carve(A1, 14080, [128, 16], F32).bitcast(I32)
        lsrc = carve(A1, 14144, [128, 64], F32)
        lsrc_i = lsrc.bitcast(I32)
        counts_i = carve(A1, 14400, [128, 32], F32).bitcast(I32)
        Lstrict = carve(A1, 14528, [128, 128], BF)
        ones_bf = carve(A1, 14784, [128, 128], BF)
        zeros_i = carve(A1, 15040, [128, 1024], F32)
        Ball = carve(A1, 21504, [128, 16, 32], F32)
        hm2_scr = nc.dram_tensor("hm2_scr", [NOWN, D], BF, kind="Internal").ap()
        list_scr = nc.dram_tensor("list_scr", [NE * NOWN, 2], I32, kind="Internal").ap()
        ybuf = nc.dram_tensor("ybuf", [NE * NOWN, D], F32, kind="Internal").ap()

        wo32 = carve(A3, 0, [128, 8, 1024], F32)
        bc_g1 = carve(A3, 32768, [128, 1024], F32)
        bc_onep = carve(A3, 36864, [128, 1024], F32)
        bc_ln1g = carve(A4, 0, [128, 1024], F32)
        bc_ln1b = carve(A4, 4096, [128, 1024], F32)
        bc_A2 = carve(A4, 8192, [128, 1024], F32)
        bc_B2 = carve(A4, 12288, [128, 1024], F32)
        htile = [carve(A4, 16384 + i * 4096, [128, 1024], F32) for i in range(2)]
        rtile = [carve(A4, 24576 + i * 4096, [128, 1024], F32) for i in range(2)]
        hntile = [carve(A4, 32768 + i * 4096, [128, 1024], F32) for i in range(2)]
        hm2tiles = [carve(A4, 40960, [128, 1024], BF), carve(A1, 19456, [128, 1024], BF)]
        S.dma("sp", wo32, wout_d.rearrange("(k p) n -> p k n", p=128), w=["wo32"])
        S.dma("sp", bc_g1, mod_bc(2), w=["bc_g1"])
        S.dma("sp", bc_onep, mod_bc(4), w=["bc_onep"])
        S.dma("sp", bc_B2, mod_bc(3), w=["bc_B2"])
        S.dma("sp", bc_ln1g, ln1g_d.partition_broadcast(128), w=["bc_ln1g"])
        S.dma("sp", bc_ln1b, ln1b_d.partition_broadcast(128), w=["bc_ln1b"])
        S.op("pool", lambda e: e.memset(zeros_i, 0.0), w=["zeros_i"])
        S.dma("sp", list_scr.rearrange("(p r) c -> p (r c)", p=128), zeros_i.bitcast(I32), r=["zeros_i"],
              w=["list_scr"])
        S.op("pool", lambda e: e.memset(ones_bf, 1.0), w=["ones_bf"])
        S.op("pool", lambda e: e.memset(Lstrict, 1.0), w=["Lstrict"])
        S.op("pool", lambda e: e.affine_select(out=Lstrict, in_=Lstrict, pattern=[[1, 128]], compare_op=ALU.is_gt,
                                              fill=0.0, base=0, channel_multiplier=-1),
             r=["Lstrict"], w=["Lstrict"])
        S.op("pool", lambda e: e.iota(out=ebase_i, pattern=[[NOWN, 32]], base=0, channel_multiplier=0),
             w=["ebase_i"])
        S.op("pool", lambda e: e.tensor_copy(out=ebase, in_=ebase_i), r=["ebase_i"], w=["ebase"])
        S.op("pool", lambda e: e.iota(out=tokid_i, pattern=[[128, 16]], base=0, channel_multiplier=1),
             w=["tokid_i"])
        S.op("pool", lambda e: e.memset(Crun, 0.0), w=["Crun"])
        for kc in range(8):
            S.op("dve", lambda e, kc=kc: e.tensor_tensor(out=wout_b[:, kc, :], in0=wo32[:, kc, :], in1=bc_g1,
                                                        op=ALU.mult), r=["wo32", "bc_g1"], w=["wout_b"])
        S.op("pool", lambda e: e.tensor_scalar(out=bc_onep, in0=bc_onep, scalar1=1.0, scalar2=None, op0=ALU.add),
             r=["bc_onep"], w=["bc_onep"])
        S.op("pool", lambda e: e.tensor_tensor(out=bc_A2, in0=bc_ln1g, in1=bc_onep, op=ALU.mult),
             r=["bc_ln1g", "bc_onep"], w=["bc_A2"])
        S.op("pool", lambda e: e.tensor_tensor(out=bc_onep, in0=bc_ln1b, in1=bc_onep, op=ALU.mult),
             r=["bc_ln1b", "bc_onep", "bc_A2"], w=["bc_onep"])
        S.op("pool", lambda e: e.tensor_tensor(out=bc_B2, in0=bc_B2, in1=bc_onep, op=ALU.add),
             r=["bc_B2", "bc_onep"], w=["bc_B2"])

        def routing(t, pr, prk):
            sl = t % 2
            R = rt[:, sl, :]
            k = ("rt", sl)
            lg = R[:, 0:36]
            S.op("dve", lambda e: e.tensor_tensor(out=lg, in0=pr[:, 0:36], in1=br_t[:], op=ALU.add),
                 r=[prk, "br"], w=[k])
            gmax, ngmax, gsum, ptg = R[:, 36:37], R[:, 37:38], R[:, 38:39], R[:, 39:40]
            eg, oh, pen = R[:, 40:44], R[:, 44:48], R[:, 48:52]
            ml = R[:, 52:84]
            top8 = R[:, 84:92]
            e2, den, c1 = R[:, 92:93], R[:, 93:94], R[:, 94:95]
            S.op("dve", lambda e: e.tensor_reduce(out=gmax, in_=lg[:, 0:4], axis=AX.X, op=ALU.max), r=[k], w=[k])
            S.op("dve", lambda e: e.tensor_scalar(out=ngmax, in0=gmax, scalar1=-1.0, scalar2=None, op0=ALU.mult),
                 r=[k], w=[k])
            S.op("act", lambda e: e.activation(out=eg, in_=lg[:, 0:4], func=AF.Exp, bias=ngmax, scale=1.0),
                 r=[k], w=[k])
            S.op("dve", lambda e: e.tensor_reduce(out=gsum, in_=eg, axis=AX.X, op=ALU.add), r=[k], w=[k])
            S.op("dve", lambda e: e.reciprocal(out=ptg, in_=gsum), r=[k], w=[k])
            S.op("dve", lambda e: e.tensor_scalar(out=oh, in0=lg[:, 0:4], scalar1=gmax, scalar2=None,
                                                 op0=ALU.is_equal), r=[k], w=[k])
            S.op("dve", lambda e: e.tensor_scalar(out=pen, in0=oh, scalar1=1e30, scalar2=-1e30, op0=ALU.mult,
                                                 op1=ALU.add), r=[k], w=[k])
            S.op("dve", lambda e: e.tensor_tensor(
                out=ml.rearrange("p (g x) -> p g x", x=8), in0=lg[:, 4:36].rearrange("p (g x) -> p g x", x=8),
                in1=pen.unsqueeze(2).broadcast_to([128, 4, 8]), op=ALU.add), r=[k], w=[k])
            S.op("dve", lambda e: e.max(out=top8, in_=ml), r=[k], w=[k])
            S.op("dve", lambda e: e.tensor_scalar(out=den, in0=top8[:, 0:1], scalar1=-1.0, scalar2=None,
                                                 op0=ALU.mult), r=[k], w=[k])
            S.op("act", lambda e: e.activation(out=e2, in_=top8[:, 1:2], func=AF.Exp, bias=den, scale=1.0),
                 r=[k], w=[k])
            S.op("dve", lambda e: e.tensor_scalar(out=den, in0=e2, scalar1=1.0, scalar2=None, op0=ALU.add),
                 r=[k], w=[k])
            S.op("dve", lambda e: e.reciprocal(out=c1, in_=den), r=[k], w=[k])
            rk_ = ("route", t)
            S.op("dve", lambda e: e.tensor_tensor(out=c12[:, 0, t:t + 1], in0=c1, in1=ptg, op=ALU.mult),
                 r=[k], w=[rk_])
            S.op("dve", lambda e: e.tensor_tensor(out=c12[:, 1, t:t + 1], in0=c12[:, 0, t:t + 1], in1=e2,
                                                 op=ALU.mult), r=[k, rk_], w=[rk_])
            S.op("dve", lambda e: e.tensor_scalar(out=oh1_all[:, t, :], in0=ml, scalar1=top8[:, 0:1], scalar2=None,
                                                 op0=ALU.is_equal), r=[k], w=[rk_])
            S.op("dve", lambda e: e.tensor_scalar(out=oh2_all[:, t, :], in0=ml, scalar1=top8[:, 1:2], scalar2=None,
                                                 op0=ALU.is_equal), r=[k], w=[rk_])
            S.op("dve", lambda e: e.tensor_tensor(out=Mbf[:, t, :], in0=oh1_all[:, t, :], in1=oh2_all[:, t, :],
                                                 op=ALU.add), r=[rk_], w=[rk_])
            pq = ps[6 + t % 2]
            S.op("pe", lambda e: e.matmul(pq[:, 64:96], lhsT=Lstrict, rhs=Mbf[:, t, :], start=True, stop=True),
                 r=[rk_, "Lstrict"], w=[prk], inc=False)
            S.op("pe", lambda e: e.matmul(pq[:, 96:128], lhsT=ones_bf, rhs=Mbf[:, t, :], start=True, stop=True),
                 r=[rk_, "ones_bf"], w=[prk])
            S.op("dve", lambda e: e.tensor_copy(out=Pall[:, t, :], in_=pq[:, 64:96]), r=[prk], w=[rk_])
            S.op("dve", lambda e: e.tensor_copy(out=Ball[:, t, :], in_=pq[:, 96:128]), r=[prk], w=[rk_])

        def d2_tile(t):
            par = t % 2
            hm2tile, hm2k = hm2tiles[par], ("hm2tile", par)
            ht, hk = htile[par], ("htile", par)
            S.dma("sp", ht, h_scr[t * 128:(t + 1) * 128, :], r=[("h_scr", t)], w=[hk])
            r_t, rk = rtile[par], ("rtile", par)
            for half in range(2):
                pb = ps[par * 2 + half]
                pk = ("ps", par * 2 + half)
                for kc in range(8):
                    S.op("pe", lambda e, pb=pb, kc=kc, half=half: e.matmul(
                        pb[:, :], lhsT=mergedT[:, kc, t * 128:(t + 1) * 128],
                        rhs=wout_b[:, kc, half * 512:(half + 1) * 512], start=(kc == 0), stop=(kc == 7)),
                        r=["mergedT", "wout_b"], w=[pk], inc=(kc == 7))
                S.op("dve", lambda e, pb=pb, half=half: e.scalar_tensor_tensor(
                    out=r_t[:, half * 512:(half + 1) * 512], in0=ht[:, half * 512:(half + 1) * 512], scalar=ALPHA,
                    in1=pb[:, :], op0=ALU.mult, op1=ALU.add), r=[hk, pk], w=[rk])
            k, rstd, nbias = layer_norm_stats(r_t, rk)
            S.op("act", lambda e, rstd=rstd, nbias=nbias: e.activation(
                out=r_t, in_=r_t, func=AF.Identity, scale=rstd, bias=nbias), r=[rk, k], w=[rk])
            hn, hnk = hntile[par], ("hntile", par)
            S.op("pool", lambda e: e.tensor_tensor(out=hn, in0=r_t, in1=bc_ln1g, op=ALU.mult),
                 r=[rk, "bc_ln1g"], w=[hnk])
            S.op("pool", lambda e: e.tensor_tensor(out=hn, in0=hn, in1=bc_ln1b, op=ALU.add),
                 r=[hnk, "bc_ln1b"], w=[hnk])
            S.dma("sp", hn_scr[t * 128:(t + 1) * 128, :], hn, r=[hnk], w=[("hn_scr", t)])
            S.op("dve", lambda e: e.tensor_tensor(out=r_t, in0=r_t, in1=bc_A2, op=ALU.mult),
                 r=[rk, "bc_A2"], w=[rk])
            S.op("dve", lambda e, hm2tile=hm2tile, r_t=r_t: e.tensor_tensor(out=hm2tile, in0=r_t, in1=bc_B2, op=ALU.add),
                 r=[rk, "bc_B2"], w=[hm2k])
            S.dma("sp", hm2_scr[t * 128:(t + 1) * 128, :], hm2tile, r=[hm2k], w=[("hm2_scr", t)])
            pstb = ps[4 + par]
            pstk = ("ps", 4 + par)
            pst = pstb[:, :].bitcast(BF)
            for kc in range(8):
                S.op("pe", lambda e, kc=kc, pst=pst, hm2tile=hm2tile: e.transpose(
                    out=pst[:, kc * 128:(kc + 1) * 128], in_=hm2tile[:, kc * 128:(kc + 1) * 128],
                    identity=ident_b[:]), r=[hm2k, "ident_b"], w=[pstk], inc=(kc == 7))
            h2t, h2k = hm2Tt[par], ("hm2Tt", par)
            evac_copy(h2t, pst.rearrange("p (a b) -> p a b", b=128), [pstk], [h2k])
            pr = ps[6 + par]
            prk = ("ps", 6 + par)
            for kc in range(8):
                S.op("pe", lambda e, kc=kc, pr=pr, h2t=h2t: e.matmul(
                    pr[:, 0:36], lhsT=h2t[:, kc, :], rhs=wr_t[:, kc, :],
                    start=(kc == 0), stop=(kc == 7)), r=[h2k, "wr"], w=[prk], inc=(kc == 7))
            routing(t, pr, prk)

        seqs = []
        for t in range(16):
            S.record()
            d2_tile(t)
            seqs.append(S.stop())
        S.play(seqs, 2)
        allr = [("route", t) for t in range(16)]
        for t in range(16):
            S.op("dve", lambda e, t=t: e.tensor_tensor(out=Pall[:, t, :], in0=Pall[:, t, :], in1=Crun, op=ALU.add),
                 r=[("route", t), "Crun"], w=[("route", t)])
            S.op("dve", lambda e, t=t: e.tensor_tensor(out=Crun, in0=Ball[:, t, :], in1=Crun, op=ALU.add),
                 r=[("route", t), "Crun"], w=["Crun"])
        S.op("dve", lambda e: e.tensor_tensor(out=Pall, in0=Pall, in1=ebase.unsqueeze(1).broadcast_to([128, 16, 32]),
                                             op=ALU.add), r=allr + ["ebase"], w=["Pall"])
        for kk, oh_all in enumerate((oh1_all, oh2_all)):
            S.op("dve", lambda e, oh_all=oh_all: e.tensor_tensor(out=tmpP, in0=Pall, in1=oh_all, op=ALU.mult),
                 r=["Pall"] + allr, w=["tmpP"])
            S.op("dve", lambda e, kk=kk: e.tensor_reduce(out=destf[:, kk * 16:(kk + 1) * 16], in_=tmpP, axis=AX.X,
                                                        op=ALU.add), r=["tmpP"], w=["destf"])
        S.op("dve", lambda e: e.tensor_copy(out=dest_i, in_=destf), r=["destf"], w=["dest_i"])
        S.op("dve", lambda e: e.tensor_copy(out=counts_i, in_=Crun), r=["Crun"], w=["counts_i"])
        lsrc4 = lsrc.rearrange("p (k t c) -> p k t c", k=2, t=16)
        lsrc4_i = lsrc_i.rearrange("p (k t c) -> p k t c", k=2, t=16)
        for kk in range(2):
            S.op("dve", lambda e, kk=kk: e.tensor_copy(out=lsrc4_i[:, kk, :, 0], in_=tokid_i), r=["tokid_i"],
                 w=["lsrc"])
            S.op("dve", lambda e, kk=kk: e.tensor_copy(out=lsrc4[:, kk, :, 1], in_=c12[:, kk, :]), r=allr,
                 w=["lsrc"])
        for kk in range(2):
            for t in range(16):
                S.dma("pool", list_scr, lsrc4_i[:, kk, t, :], r=["lsrc", "dest_i", "list_scr"], w=[("lscat", kk, t)],
                      indirect=("scatter", dest_i[:, kk * 16 + t:kk * 16 + t + 1]))
        if debug:
            S.barrier()
            S.dma("sp", dbg["dbg_gates"][:, 0, :], destf)
            S.dma("sp", dbg["dbg_gates"][:, 1, :], Crun)
            S.dma("sp", dbg["dbg_gates"][:, 2, :], c12.rearrange("p k t -> p (k t)"))
        S.barrier()
        if stop_after == "D2":
            return nc

        wexp = []
        for i in range(2):
            o = i * 24576
            wexp.append((carve(A3, o, [128, 8, 512], BF), carve(A3, o + 8192, [128, 8, 512], BF),
                         carve(A3, o + 16384, [128, 4, 1024], BF)))
        lst = [carve(A1, 23552 + i * 8, [128, 2], F32) for i in range(5)]
        RB = 24576
        xg = [carve(A4, 64 + i * 2048, [128, 1024], BF) for i in range(2)] + [carve(A1, RB, [128, 1024], BF)]
        xgT = [carve(A4, 4160 + i * 2048, [128, 8, 128], BF) for i in range(2)] + [carve(A1, RB + 2048, [128, 8, 128], BF)]
        sa_t = [carve(A4, 8256 + i * 2048, [128, 512], F32) for i in range(2)] + [carve(A1, RB + 4096, [128, 512], F32)]
        act_t = [carve(A4, 12352 + i * 1024, [128, 512], BF) for i in range(2)] + [carve(A1, RB + 6144, [128, 512], BF)]
        actT = [carve(A4, 14400 + i * 1024, [128, 4, 128], BF) for i in range(2)] + [carve(A1, RB + 7168, [128, 4, 128], BF)]
        ysb = [carve(A4, 16448 + i * 4096, [128, 1024], F32) for i in range(2)] + [carve(A1, RB + 8192, [128, 1024], F32)]
        cond_engs = ("pe", "act", "dve", "pool", "sp")
        warm_rhs = carve(A1, RB + 12288, [128, 512], BF)
        S.op("pool", lambda e: e.memset(warm_rhs, 0.0), w=["warm_rhs"])

        def keep_warm(bank, bkey, n):
            for i in range(n):
                S.op("pe", lambda e: e.matmul(bank[:, :], lhsT=ones_bf, rhs=warm_rhs, start=True, stop=True),
                     r=["warm_rhs", "ones_bf"], w=[bkey], inc=False)

        stage = [carve(A2, i * 16384, [128, 4096], F32) for i in range(4)]
        stg_ctr = [0]

        def chunks(ex):
            wg_t, wu_t, wd_t = wexp[ex % 2]
            return (("g", wg_t, weg_d[ex], 8), ("u", wu_t, weu_d[ex], 8), ("d", wd_t, wed_d[ex], 4))

        def issue_chunk(ex, c):
            if ex >= NE:
                return
            nm, dst, src, kdim = chunks(ex)[c]
            si = (3 * ex + c) % 4
            st3 = stage[si].rearrange("p (k n) -> p k n", k=kdim)
            S.dma("sp", st3, src.rearrange("(p k) n -> p k n", p=128), w=[("stage", si)])

        def cast_chunk(ex, c):
            if ex >= NE:
                return
            nm, dst, src, kdim = chunks(ex)[c]
            si = (3 * ex + c) % 4
            st3 = stage[si].rearrange("p (k n) -> p k n", k=kdim)
            sk = ("stage", si)
            h = kdim // 2
            S.op("act", lambda e: e.activation(out=dst[:, 0:h, :], in_=st3[:, 0:h, :], func=AF.Copy),
                 r=[sk], w=[("wexp", ex % 2, nm, 0)])
            S.op("dve", lambda e: e.tensor_copy(out=dst[:, h:, :], in_=st3[:, h:, :]),
                 r=[sk], w=[("wexp", ex % 2, nm, 1)])

        def prefetch_lists(ex_):
            for par_ in range(2):
                li_ = (2 * ex_ + par_) % 4
                r0_ = ex_ * NOWN + par_ * 128
                S.dma("pool", lst[li_].bitcast(I32), list_scr[r0_:r0_ + 128, :], w=[("lst", li_)],
                      semkey=("lst", li_))

        n_exp = NE
        prefetch_lists(0)
        for c in range(3):
            issue_chunk(0, c)
        for c in range(3):
            cast_chunk(0, c)
        for c in range(3):
            issue_chunk(1, c)
        issue_chunk(2, 0)
        sctr = 0
        for ex in range(n_exp):
            if ex + 1 < n_exp:
                prefetch_lists(ex + 1)
            wg_t, wu_t, wd_t = wexp[ex % 2]
            wk = ("wexp", ex % 2)
            cvals = {}
            cregs = {}
            for en in cond_engs:
                cregs[en] = S.eng[en].alloc_register(f"ncnt_{en}_{ex}")
                S.raw(en, lambda e, en=en, ex=ex: e.reg_load(cregs[en], counts_i[0:1, ex:ex + 1]), r=["counts_i"])
                cvals[en] = S.eng[en].snap(cregs[en], donate=True)
            def slot_r1(j, part):
                par = j % 2
                bi = par if j < 2 else 2
                row0 = ex * NOWN + j * 128
                li = (2 * ex + par) % 4 if j < 2 else 4
                lt, lk = lst[li], ("lst", li)
                if j >= 2 and part == 0:
                    S.dma("pool", lt.bitcast(I32), list_scr[row0:row0 + 128, :], w=[lk], semkey=("lst", li))
                xgt, xgk = xg[bi], ("xg", bi)
                xT, xTk = xgT[bi], ("xgT", bi)
                if part == 0:
                    S.dma("pool", xgt, hm2_scr, r=[lk], w=[xgk], indirect=("gather", lt.bitcast(I32)[:, 0:1]),
                          semkey=("xg", bi))
                    pxk = ("ps", 0)
                    pxt = ps[0][:, :].bitcast(BF)
                    for kc in range(8):
                        S.op("pe", lambda e, kc=kc: e.transpose(
                            out=pxt[:, kc * 128:(kc + 1) * 128],
                            in_=xgt.rearrange("t (p k) -> t k p", k=8)[:, kc, :],
                            identity=ident_b[:]), r=[xgk, "ident_b"], w=[pxk], inc=(kc == 7))
                    evac_copy(xT, pxt.rearrange("p (a b) -> p a b", b=128), [pxk], [xTk], eng="dve")
                    return
                pa_, pu_ = ps[1 + 2 * par], ps[2 + 2 * par]
                ka, ku = ("ps", 1 + 2 * par), ("ps", 2 + 2 * par)
                for kc in range(8):
                    S.op("pe", lambda e, kc=kc: e.matmul(
                        pa_[:, :], lhsT=xT[:, kc, :], rhs=wg_t[:, kc, :], start=(kc == 0), stop=(kc == 7)),
                        r=[xTk, wk + ("g", 0), wk + ("g", 1)], w=[ka], inc=(kc == 7))
                for kc in range(8):
                    S.op("pe", lambda e, kc=kc: e.matmul(
                        pu_[:, :], lhsT=xT[:, kc, :], rhs=wu_t[:, kc, :], start=(kc == 0), stop=(kc == 7)),
                        r=[xTk, wk + ("u", 0), wk + ("u", 1)], w=[ku], inc=(kc == 7))
                st, stk = sa_t[bi], ("sa", bi)
                S.op("act", lambda e: e.activation(out=st, in_=pa_[:, :], func=AF.Silu), r=[ka], w=[stk])
                at_, atk = act_t[bi], ("act", bi)
                S.op("dve", lambda e: e.scalar_tensor_tensor(
                    out=at_, in0=st, scalar=lt[:, 1:2], in1=pu_[:, :], op0=ALU.mult, op1=ALU.mult),
                    r=[stk, ku, lk], w=[atk])

            def slot_r2(j, part):
                par = j % 2
                bi = par if j < 2 else 2
                row0 = ex * NOWN + j * 128
                at_, atk = act_t[bi], ("act", bi)
                pak = ("ps5", par)
                pat = ps[5][:, par * 256:(par + 1) * 256].bitcast(BF)
                aT, aTk = actT[bi], ("actT", bi)
                if part == 0:
                    for fc in range(4):
                        S.op("pe", lambda e, fc=fc: e.transpose(
                            out=pat[:, fc * 128:(fc + 1) * 128],
                            in_=at_.rearrange("t (p k) -> t k p", k=4)[:, fc, :],
                            identity=ident_b[:]), r=[atk, "ident_b"], w=[pak], inc=(fc == 3))
                    evac_copy(aT, pat.rearrange("p (a b) -> p a b", b=128), [pak], [aTk], eng="dve")
                    return
                yb, ybk = ysb[bi], ("ysb", bi)
                for half in range(2):
                    pd = ps[6 + half]
                    pdk = ("ps", 6 + half)
                    for fc in range(4):
                        S.op("pe", lambda e, fc=fc, pd=pd, half=half: e.matmul(
                            pd[:, :], lhsT=aT[:, fc, :], rhs=wd_t[:, fc, half * 512:(half + 1) * 512],
                            start=(fc == 0), stop=(fc == 3)), r=[aTk, wk + ("d", 0), wk + ("d", 1)], w=[pdk], inc=(fc == 3))
                    evac_copy(yb[:, half * 512:(half + 1) * 512], pd[:, :], [pdk], [ybk],
                              eng=("dve" if half == 0 else "act"))
                S.dma("act", ybuf[row0:row0 + 128, :], yb, r=[ybk], w=[("ybuf", ex, j)], semkey=("yst", bi))

            def cond(thr):
                return lambda en: cvals[en] > thr

            def emit_pair(j0, hooks=None):
                steps = [(slot_r1, j0, 0), (slot_r1, j0 + 1, 0), (slot_r1, j0, 1), (slot_r1, j0 + 1, 1),
                         (slot_r2, j0, 0), (slot_r2, j0 + 1, 0), (slot_r2, j0, 1), (slot_r2, j0 + 1, 1)]
                for i_, (fn_, jj, part_) in enumerate(steps):
                    S.begin_region()
                    fn_(jj, part_)
                    S.end_region(cond(jj * 128))
                    if hooks and i_ in hooks:
                        hooks[i_]()

            cast_chunk(ex + 1, 0)
            issue_chunk(ex + 2, 1)

            def hook_mid():
                cast_chunk(ex + 1, 1)
                issue_chunk(ex + 2, 2)

            def hook_end():
                cast_chunk(ex + 1, 2)
                issue_chunk(ex + 3, 0)

            emit_pair(0, hooks={3: hook_mid, 7: hook_end})
            S.begin_region()
            emit_pair(2)
            S.begin_region()
            emit_pair(4)
            emit_pair(6)
            S.begin_region()
            for j0 in (8, 10, 12, 14):
                emit_pair(j0)
            S.end_region(cond(1024))
            S.end_region(cond(512))
            S.end_region(cond(256))
            for en in cond_engs:
                S.eng[en].free_register(cregs[en])
        S.barrier()
        if stop_after == "E":
            return nc

        bc_g2 = carve(A4, 0, [128, 1024], F32)
        bc_ln2g = carve(A4, 4096, [128, 1024], F32)
        bc_ln2b = carve(A4, 8192, [128, 1024], F32)
        FW = 4
        hnt = [carve(A3, i * 4096, [128, 1024], F32) for i in range(FW)]
        ot = [carve(A3, 16384 + i * 4096, [128, 1024], F32) for i in range(FW)]
        y1t = [carve(A2, i * 4096, [128, 1024], F32) for i in range(FW)]
        y2t = [carve(A2, 16384 + i * 4096, [128, 1024], F32) for i in range(FW)]
        S.dma("sp", bc_g2, mod_bc(5), w=["bc_g2"])
        S.dma("sp", bc_ln2g, ln2g_d.partition_broadcast(128), w=["bc_ln2g"])
        S.dma("sp", bc_ln2b, ln2b_d.partition_broadcast(128), w=["bc_ln2b"])
        def f_tile(t):
            par = t % FW
            hn, hnk = hnt[par], ("hnt", par)
            S.dma("sp", hn, hn_scr[t * 128:(t + 1) * 128, :], w=[hnk])
            y1, y1k = y1t[par], ("y1t", par)
            y2, y2k = y2t[par], ("y2t", par)
            S.dma("pool", y1, ybuf, w=[y1k], indirect=("gather", dest_i[:, t:t + 1]))
            S.dma("pool", y2, ybuf, w=[y2k], indirect=("gather", dest_i[:, 16 + t:17 + t]))
            o_t, ok = ot[par], ("ot", par)
            S.op("dve", lambda e: e.tensor_tensor(out=y1, in0=y1, in1=y2, op=ALU.add), r=[y1k, y2k], w=[y1k])
            S.op("dve", lambda e: e.tensor_tensor(out=o_t, in0=y1, in1=bc_g2, op=ALU.mult),
                 r=["bc_g2", y1k], w=[ok])
            S.op("dve", lambda e: e.scalar_tensor_tensor(
                out=o_t, in0=hn, scalar=ALPHA, in1=o_t, op0=ALU.mult, op1=ALU.add), r=[hnk, ok], w=[ok])
            k, rstd, nbias = layer_norm_stats(o_t, ok)
            S.op("act", lambda e: e.activation(
                out=o_t, in_=o_t, func=AF.Identity, scale=rstd, bias=nbias), r=[ok, k], w=[ok])
            S.op("dve", lambda e: e.tensor_tensor(out=o_t, in0=o_t, in1=bc_ln2g, op=ALU.mult),
                 r=[ok, "bc_ln2g"], w=[ok])
            S.op("pool", lambda e: e.tensor_tensor(out=o_t, in0=o_t, in1=bc_ln2b, op=ALU.add),
                 r=[ok, "bc_ln2b"], w=[ok])
            S.dma("sp", out_d[t * 128:(t + 1) * 128, :], o_t, r=[ok], w=[("out", t)])

        seqs = []
        for t in range(16):
            S.record()
            f_tile(t)
            seqs.append(S.stop())
        S.play(seqs, FW)
        S.barrier()
    return nc


def _tables(rpb, core):
    col = np.arange(GW)
    col_start = np.clip(col - 8, 0, GW - 16)
    cmask = (col[None, :] >= col_start[:, None]) & (col[None, :] < col_start[:, None] + 16)
    col_off = np.clip(col[None, :] - col[:, None], -15, 15) + 15

    def table(c, rl, first, n):
        r = RPC * c + rl
        rs_ = min(max(r - 4, 0), ROWS - 8)
        out = np.full((128, NH, n, GW), NEG, np.float32)
        for j in range(n):
            for il in range(2):
                er = 2 * (first + j) + il
                gr = RPC * c - HALO + er
                if not (rs_ <= gr < rs_ + 8):
                    continue
                ro = gr - r + 7
                vals = rpb[:, ro, :][:, col_off]
                vals = np.where(cmask[None], vals, np.float32(NEG))
                out[il * 64:(il + 1) * 64, :, j, :] = np.transpose(vals, (2, 0, 1))
        return out

    tabE = table(1, 4, 2, 4)
    tabO = table(1, 5, 2, 5)
    tabS = np.full((7, 128, NH, 6, GW), NEG, np.float32)
    for rl, slot in SPECIAL_SLOT.items():
        f, n = SPECIAL[rl]
        tabS[slot, :, :, :n, :] = table(core, rl, f, n)
    return tabE, tabO, tabS


def _pool_fix(core):
    L = ROWS * GW
    fix = np.ones((128, 80), np.float32)
    if core == 0:
        fix[:, 0:8] = 0.0
    if core == NCORES - 1:
        fix[:, 8:16] = 0.0
    base = core * NOWN
    for g, w in enumerate((2, 4, 8, 16)):
        for side in range(2):
            for j in range(8):
                t = base + (j if side == 0 else NOWN - 8 + j)
                lo = min(max(t - w // 2, 0), L)
                hi = min(max(t + w - w // 2, 0), L)
                fix[:, 16 + g * 16 + side * 8 + j] = np.float32(w) / np.float32(hi - lo)
    return fix


def make_in_maps(inputs):
    f = lambda a: np.ascontiguousarray(np.asarray(a, dtype=np.float32))
    x = f(inputs["x"])[0]
    ctx = f(inputs["ctx"])[0]
    c = f(inputs["c"])[0]
    c_ctx = f(inputs["c_ctx"])
    cc = np.stack([c, c_ctx], axis=-1).reshape(8, 128, 2).transpose(1, 0, 2)
    b_modc = f(inputs["b_mod"])[0].reshape(48, 128).T
    rpb = f(inputs["rpb"])[0]
    shared = {
        "ctx": ctx, "cc": f(cc), "w_mod": f(inputs["w_mod"])[0], "b_modc": f(b_modc),
        "ln_in_g": f(inputs["ln_in_g"]), "ln_in_b": f(inputs["ln_in_b"]), "w_in": f(inputs["w_in"])[0],
        "w_pool_grp": f(inputs["w_pool_grp"])[0],
        "pool_scale_c": f(f(inputs["pool_scale"])[0].reshape(4, 128).T),
        "w_attn_proj": f(inputs["w_attn_proj"])[0], "w_pool_proj": f(inputs["w_pool_proj"])[0],
        "w_out": f(inputs["w_out"])[0],
        "ln1_g": f(inputs["ln1_g"])[0], "ln1_b": f(inputs["ln1_b"])[0],
        "ln2_g": f(inputs["ln2_g"])[0], "ln2_b": f(inputs["ln2_b"])[0],
        "w_r": f(np.concatenate([f(inputs["w_router_group"])[0], f(inputs["w_router_expert"])[0]], axis=1)),
        "b_r": f(np.concatenate([f(inputs["b_router_group"])[0], f(inputs["b_router_expert"])[0]], axis=0)),
        "w_expert_gate": f(inputs["w_expert_gate"])[0], "w_expert_up": f(inputs["w_expert_up"])[0],
        "w_expert_down": f(inputs["w_expert_down"])[0],
    }
    in_maps = []
    for core in range(NCORES):
        xe = np.zeros((NEXT, D), np.float32)
        g0 = (RPC * core - HALO) * GW
        lo, hi = max(g0, 0), min(g0 + NEXT, ROWS * GW)
        xe[lo - g0:hi - g0] = x[lo:hi]
        tabE, tabO, tabS = _tables(rpb, core)
        m = dict(shared)
        m.update({"x_ext": xe, "tab_even": tabE, "tab_odd": tabO, "tab_sp": tabS, "pool_fix": _pool_fix(core)})
        in_maps.append(m)
    return in_maps


def kernel(**inputs):
    in_maps = make_in_maps(inputs)
    nc = build_nc()
    res = run_bass_kernel_spmd(nc, in_maps, core_ids=list(range(NCORES)))
    out = np.concatenate([np.asarray(r["out"]) for r in res.results], axis=0)
    return out.reshape(1, ROWS * GW, D).astype(np.float32)
```

```python
import numpy as np
import concourse.bass as bass
import concourse.mybir as mybir
from concourse.bass_utils import run_bass_kernel_spmd
from contextlib import ExitStack

F32 = mybir.dt.float32
BF = mybir.dt.bfloat16
AF = mybir.ActivationFunctionType
ALU = mybir.AluOpType
AX = mybir.AxisListType

NCORES = 8
D = 1024
GW = 64
ROWS = 256
RPC = 32
HALO = 4
EXT_ROWS = 40
NEXT = EXT_ROWS * GW
NOWN = RPC * GW
NCTX = 256
NALL = NEXT + NCTX
OWN0 = HALO * GW
NH = 8
HD = 64
NE = 32
DE = 512
ALPHA = 2.0 ** 0.25
EPS = 1e-5
NEG = -30000.0
SCALE = HD ** -0.5

SPECIAL = {0: (0, 6), 1: (0, 6), 2: (1, 5), 3: (1, 5), 29: (14, 5), 30: (14, 5), 31: (14, 6)}
SPECIAL_SLOT = {0: 0, 1: 1, 2: 2, 3: 3, 29: 4, 30: 5, 31: 6}


def row_window(rl):
    if rl in SPECIAL:
        f, n = SPECIAL[rl]
        return f, n, "S"
    if rl % 2 == 0:
        return rl // 2, 4, "E"
    return (rl - 1) // 2, 5, "O"


class Sched:
    def __init__(self, nc, sems, dma_sems):
        self.nc = nc
        self.eng = {"pe": nc.tensor, "act": nc.scalar, "dve": nc.vector, "pool": nc.gpsimd, "sp": nc.sync}
        self.sem = dict(sems)
        self.cnt = {k: 0 for k in sems}
        self.ndma = len(dma_sems)
        for i, s in enumerate(dma_sems):
            self.sem[("dma", i)] = s
            self.cnt[("dma", i)] = 0
        self.next_dma = 0
        self.named = {}
        self.n_named = 16
        self.seen = {e: {} for e in self.eng}
        self.last_w = {}
        self.readers = {}
        self.ninst = 0
        self.region = None
        self.rstack = []
        self.rec = None

    def _emit(self, e, fn):
        if self.region is None:
            fn()
        else:
            self.region["items"][e].append(fn)
        self.ninst += 1

    def _wait(self, e, tok):
        k, v = tok
        if e == "pe" and k == "pe":
            return
        if self.seen[e].get(k, 0) >= v:
            return
        self._emit(e, lambda: self.eng[e].wait_ge(self.sem[k], v))
        self.seen[e][k] = v

    def begin_region(self):
        self.rstack.append({"items": {e: [] for e in self.eng}, "seen0": {e: dict(self.seen[e]) for e in self.eng},
                            "cnt0": dict(self.cnt), "dmas": {e: [] for e in self.eng}})
        self.region = self.rstack[-1]

    def end_region(self, cond_fn):
        reg = self.rstack.pop()
        parent = self.rstack[-1] if self.rstack else None
        self.region = parent
        for e, items in reg["items"].items():
            if not items:
                continue
            n = self.cnt[e] - reg["cnt0"][e]
            cnt0 = reg["cnt0"][e]
            dmas = list(reg["dmas"][e])

            def emit_e(e=e, items=items, n=n, cnt0=cnt0, dmas=dmas):
                eng = self.eng[e]
                with eng.If(cond_fn(e)):
                    for it in items:
                        it()
                with eng.Else():
                    if n > 0:
                        if cnt0 > 0:
                            eng.wait_ge(self.sem[e], cnt0)
                        eng.sem_inc(self.sem[e], n)
                    for (k, prev) in dmas:
                        if prev > 0:
                            eng.wait_ge(self.sem[k], prev)
                        eng.sem_inc(self.sem[k], 16)

            if parent is None:
                emit_e()
            else:
                parent["items"][e].append(emit_e)
                parent["dmas"][e].extend(dmas)
            self.seen[e] = reg["seen0"][e]

    def raw(self, e, fn, r=()):
        self._deps(e, r, ())
        self._emit(e, lambda: fn(self.eng[e]))

    def _deps(self, e, r, w):
        for key in r:
            t = self.last_w.get(key)
            if t is not None:
                self._wait(e, t)
        for key in w:
            t = self.last_w.get(key)
            if t is not None:
                self._wait(e, t)
            for t in self.readers.get(key, ()):
                self._wait(e, t)

    def _commit(self, tok, r, w):
        for key in r:
            self.readers.setdefault(key, []).append(tok)
            if len(self.readers[key]) > 12:
                best = {}
                for k, v in self.readers[key]:
                    if best.get(k, 0) < v:
                        best[k] = v
                self.readers[key] = list(best.items())
        for key in w:
            self.last_w[key] = tok
            self.readers[key] = []

    def record(self):
        self.rec = []

    def stop(self):
        r_, self.rec = self.rec, None
        return r_

    def play(self, seqs, width):
        seqs = [q for q in seqs if q]
        active, nxt = [], 0
        while nxt < len(seqs) or active:
            while len(active) < width and nxt < len(seqs):
                active.append([seqs[nxt], 0])
                nxt += 1
            for a in list(active):
                seq = a[0]
                while True:
                    kind, args, kw = seq[a[1]]
                    a[1] += 1
                    getattr(self, kind)(*args, **kw)
                    glue = kind == "op" and not kw.get("inc", True)
                    if a[1] >= len(seq) or not glue:
                        break
                if a[1] >= len(seq):
                    active.remove(a)

    def op(self, e, fn, r=(), w=(), inc=True):
        if self.rec is not None:
            self.rec.append(("op", (e, fn), dict(r=list(r), w=list(w), inc=inc)))
            return
        self._deps(e, r, w)
        if inc:
            self.cnt[e] += 1
            self._emit(e, lambda: fn(self.eng[e]).then_inc(self.sem[e], 1))
            tok = (e, self.cnt[e])
        else:
            self._emit(e, lambda: fn(self.eng[e]))
            tok = (e, self.cnt[e] + 1)
        self._commit(tok, r, w)

    def dma(self, q, out, in_, r=(), w=(), indirect=None, semkey=None):
        if self.rec is not None:
            self.rec.append(("dma", (q, out, in_), dict(r=list(r), w=list(w), indirect=indirect, semkey=semkey)))
            return
        self._deps(q, r, w)
        if semkey is not None:
            if semkey not in self.named:
                assert len(self.named) < self.n_named
                self.named[semkey] = len(self.named)
            i = self.named[semkey]
        else:
            i = self.n_named + self.next_dma
            self.next_dma = (self.next_dma + 1) % (self.ndma - self.n_named)
        k = ("dma", i)
        prev = self.cnt[k]
        if prev > 0:
            self._wait(q, (k, prev))
        self.cnt[k] += 16
        if self.region is not None:
            self.region["dmas"][q].append((k, prev))
        if indirect is None:
            self._emit(q, lambda: self.eng[q].dma_start(out=out, in_=in_).then_inc(self.sem[k], 16))
        elif indirect[0] == "gather":
            self._emit(q, lambda: self.eng[q].indirect_dma_start(
                out=out, out_offset=None, in_=in_,
                in_offset=bass.IndirectOffsetOnAxis(ap=indirect[1], axis=0)).then_inc(self.sem[k], 16))
        else:
            self._emit(q, lambda: self.eng[q].indirect_dma_start(
                out=out, out_offset=bass.IndirectOffsetOnAxis(ap=indirect[1], axis=0), in_=in_,
                in_offset=None).then_inc(self.sem[k], 16))
        self._commit((k, self.cnt[k]), r, w)

    def barrier(self):
        for e in self.eng:
            for k, v in self.cnt.items():
                if v > 0 and k != e:
                    self._wait(e, (k, v))
        self.last_w = {}
        self.readers = {}


def carve(arena, off, shape, dt):
    esz = 4 if dt == F32 else 2
    n = int(np.prod(shape[1:]))
    nbytes = n * esz
    assert off % 4 == 0 and nbytes % 4 == 0
    assert off + nbytes <= arena.shape[1] * 4, (off, nbytes, arena.shape)
    ap = arena[:, off // 4:(off + nbytes) // 4]
    if dt != F32:
        ap = ap.bitcast(dt)
    if len(shape) == 3:
        ap = ap.rearrange("p (a b) -> p a b", b=shape[2])
    elif len(shape) == 4:
        ap = ap.rearrange("p (a b c) -> p a b c", b=shape[2], c=shape[3])
    return ap


A1_BYTES = 8 * NALL * 2
A2_BYTES = 65536
A3_BYTES = 49152
A4_BYTES = 43008


def build_nc(debug=False, stop_after=None):
    nc = bass.Bass("TRN2", target_bir_lowering=False)

    def din(name, shape, dt=F32):
        return nc.dram_tensor(name, list(shape), dt, kind="ExternalInput").ap()

    x_ext = din("x_ext", [NEXT, D])
    ctx_d = din("ctx", [NCTX, D])
    cc_d = din("cc", [128, 8, 2])
    w_mod = din("w_mod", [D, 6 * D])
    bmod_d = din("b_modc", [128, 48])
    bmodr_d = din("b_modr", [6144])
    lning_d = din("ln_in_g", [D])
    lninb_d = din("ln_in_b", [D])
    w_in = din("w_in", [D, 4096])
    tabE_d = din("tab_even", [128, 8, 4, 64])
    tabO_d = din("tab_odd", [128, 8, 5, 64])
    tabS_d = din("tab_sp", [7, 128, 8, 6, 64])
    wgrp_d = din("w_pool_grp", [4, 128, 128])
    pscale_d = din("pool_scale_c", [128, 4])
    pfix_d = din("pool_fix", [128, 16 + 64])
    wa_d = din("w_attn_proj", [512, D])
    wp_d = din("w_pool_proj", [512, D])
    wout_d = din("w_out", [D, D])
    ln1g_d = din("ln1_g", [D])
    ln1b_d = din("ln1_b", [D])
    ln2g_d = din("ln2_g", [D])
    ln2b_d = din("ln2_b", [D])
    wr_d = din("w_r", [D, 36])
    br_d = din("b_r", [36])
    n_exp_in = NE if stop_after in (None, "E") else 1
    weg_d = din("w_expert_gate", [n_exp_in, D, DE])
    weu_d = din("w_expert_up", [n_exp_in, D, DE])
    wed_d = din("w_expert_down", [n_exp_in, DE, D])
    out_d = nc.dram_tensor("out", [NOWN, D], F32, kind="ExternalOutput").ap()
    h_scr = nc.dram_tensor("h_scr", [NOWN, D], F32, kind="Internal").ap()
    hn_scr = nc.dram_tensor("hn_scr", [NOWN, D], F32, kind="Internal").ap()
    mod_scr = nc.dram_tensor("mod_scr", [48 * 128], F32, kind="Internal").ap()
    dbg = {}
    if debug:
        for nm, shp, dt in [("dbg_hmT", [128, 8, NALL], BF), ("dbg_qT", [128, 4, NOWN], BF),
                            ("dbg_kT", [128, 4, NALL], BF), ("dbg_V", [128, 22, 8, 65], BF),
                            ("dbg_ypoolT", [128, 4, NOWN], BF), ("dbg_yattnT", [128, 4, NOWN], BF),
                            ("dbg_mod", [128, 48, 2], F32), ("dbg_mergedT", [128, 8, NOWN], BF),
                            ("dbg_hm2T", [128, 8, NOWN], BF), ("dbg_gates", [128, 16, 32], F32),
                            ("dbg_yacc", [128, 16, D], F32)]:
            dbg[nm] = nc.dram_tensor(nm, shp, dt, kind="ExternalOutput").ap()

    es = ExitStack()
    with es:
        def sb(name, shape, dt=F32):
            return es.enter_context(nc.sbuf_tensor(name, list(shape), dt))

        A1 = sb("A1", [128, A1_BYTES // 4])
        A2 = sb("A2", [128, A2_BYTES // 4])
        A3 = sb("A3", [128, A3_BYTES // 4])
        A4 = sb("A4", [128, A4_BYTES // 4])
        ident_b = sb("ident_b", [128, 128], BF)
        ident_f = sb("ident_f", [128, 128], F32)
        mhalf = sb("mhalf", [128, 1])
        cc_t = sb("cc_t", [128, 8, 2])
        sc_t = sb("sc_t", [128, 8, 2])
        bmod_t = sb("bmod_t", [128, 48])
        modT = sb("modT", [128, 48, 2])
        modL = sb("modL", [128, 48])
        modrow = sb("modrow", [48, 128])
        A1c = sb("A1c", [128, 8, 2])
        stats = sb("stats", [128, 8, 2, 6])
        mv = sb("mv", [128, 8, 2])
        rs = sb("rs", [128, 8, 4])
        pscale_t = sb("pscale_t", [128, 4])
        pfix_t = sb("pfix_t", [128, 80])
        wgrp_t = sb("wgrp_t", [128, 4, 128], BF)
        wr_t = sb("wr_t", [128, 8, 36], BF)
        br_t = sb("br_t", [128, 36])
        gates_all = sb("gates_all", [128, 16, 32])
        rt = sb("rt", [128, 2, 96])
        rden = sb("rden", [64, 2, 8])
        ps = [es.enter_context(nc.psum_tensor(f"ps{i}", [128, 512], F32)) for i in range(8)]
        sem_names = ["pe", "act", "dve", "pool", "sp"]
        sems = {k: es.enter_context(nc.semaphore("s_" + k)) for k in sem_names}
        dma_sems = [es.enter_context(nc.semaphore(f"dq{i}")) for i in range(40)]
        S = Sched(nc, sems, dma_sems)

        def dbg_dump(name, ap, key):
            if debug:
                S.dma("sp", dbg[name], ap, r=[key])

        S.op("pool", lambda e: e.memset(ident_b[:], 0.0), w=["ident_b"])
        S.op("pool", lambda e: e.affine_select(out=ident_b[:], in_=ident_b[:], pattern=[[-1, 128]],
                                              compare_op=ALU.not_equal, fill=1.0, base=0, channel_multiplier=1),
             r=["ident_b"], w=["ident_b"])
        S.op("pool", lambda e: e.memset(ident_f[:], 0.0), w=["ident_f"])
        S.op("pool", lambda e: e.affine_select(out=ident_f[:], in_=ident_f[:], pattern=[[-1, 128]],
                                              compare_op=ALU.not_equal, fill=1.0, base=0, channel_multiplier=1),
             r=["ident_f"], w=["ident_f"])
        S.op("pool", lambda e: e.memset(mhalf[:], -0.5), w=["mhalf"])

        S.dma("sp", cc_t[:], cc_d, w=["cc"])
        S.dma("sp", bmod_t[:], bmod_d, w=["bmod"])
        S.dma("sp", pscale_t[:], pscale_d, w=["pscale"])
        S.dma("sp", pfix_t[:], pfix_d, w=["pfix"])
        S.dma("sp", br_t[:], br_d.partition_broadcast(128), w=["br"])
        S.dma("pool", wgrp_t[:], wgrp_d.rearrange("g i o -> i g o"), w=["wgrp"])
        S.dma("pool", wr_t[:], wr_d.rearrange("(k p) n -> p k n", p=128), w=["wr"])

        S.op("act", lambda e: e.activation(out=sc_t[:], in_=cc_t[:], func=AF.Silu), r=["cc"], w=["sc"])
        wm = [carve(A2, 0, [128, 8, 1024], F32), carve(A2, 32768, [128, 8, 1024], F32)]
        modrow2 = carve(A3, 0, [128, 6144], F32)
        bmodrow = carve(A3, 24576, [128, 6144], F32)
        S.dma("sp", bmodrow[0:2, :], bmodr_d.partition_broadcast(2), w=["bmodrow"])
        for j in range(6):
            S.dma("sp", wm[j % 2], w_mod[:, j * 1024:(j + 1) * 1024].rearrange("(k p) n -> p k n", p=128),
                  w=[("wm", j % 2)])
            for half in range(2):
                blk = j * 2 + half
                bank = blk % 6
                for kc in range(8):
                    S.op("pe", lambda e, kc=kc, bank=bank, j=j, half=half: e.matmul(
                        ps[bank][0:2, :], lhsT=sc_t[:, kc, :], rhs=wm[j % 2][:, kc, half * 512:(half + 1) * 512],
                        start=(kc == 0), stop=(kc == 7)),
                        r=[("wm", j % 2), "sc"], w=[("ps", bank)], inc=(kc == 7))
                S.op("dve", lambda e, bank=bank, blk=blk: e.tensor_tensor(
                    out=modrow2[0:2, blk * 512:(blk + 1) * 512], in0=ps[bank][0:2, :],
                    in1=bmodrow[0:2, blk * 512:(blk + 1) * 512], op=ALU.add),
                    r=[("ps", bank), "bmodrow"], w=["modrow2"])
        for mc in range(48):
            S.op("pe", lambda e, mc=mc: e.transpose(out=ps[6][:, mc * 2:(mc + 1) * 2],
                                                   in_=modrow2[0:2, mc * 128:(mc + 1) * 128],
                                                   identity=ident_f[0:2, 0:2]),
                 r=["modrow2", "ident_f"], w=[("ps", 6)], inc=(mc == 47))
        S.op("dve", lambda e: e.tensor_copy(out=modT[:], in_=ps[6][:, 0:96].rearrange("p (a b) -> p a b", b=2)),
             r=[("ps", 6)], w=["modT"])
        S.op("dve", lambda e: e.tensor_scalar(out=A1c[:], in0=modT[:, 8:16, :], scalar1=1.0, scalar2=None,
                                             op0=ALU.add), r=["modT"], w=["A1c"])
        S.dma("sp", mod_scr.rearrange("(o n) -> o n", o=1), modrow2[0:1, :], r=["modrow2"], w=["mod_scr"])
        dbg_dump("dbg_mod", modT[:], "modT")

        def mod_bc(idx):
            return mod_scr[idx * 1024:(idx + 1) * 1024].partition_broadcast(128)

        S.barrier()
        if stop_after == "mod":
            return nc

        hmT = carve(A1, 0, [128, 8, NALL], BF)
        xt = carve(A4, 0, [128, 8, 1024], F32)
        ginb = carve(A4, 32768, [128, 1024], F32)
        binb = carve(A4, 36864, [128, 1024], F32)
        S.dma("sp", ginb, lning_d.partition_broadcast(128), w=["ginb"])
        S.dma("sp", binb, lninb_d.partition_broadcast(128), w=["binb"])

        slot_ctr = [0]

        def layer_norm_stats(src, srckey):
            sl = slot_ctr[0] % 8
            slot_ctr[0] += 1
            k = ("ln", sl)
            S.op("dve", lambda e: e.bn_stats(out=stats[:, sl, 0, :], in_=src[:, 0:512]), r=[srckey], w=[k])
            S.op("dve", lambda e: e.bn_stats(out=stats[:, sl, 1, :], in_=src[:, 512:1024]), r=[srckey], w=[k])
            S.op("dve", lambda e: e.bn_aggr(out=mv[:, sl, :], in_=stats[:, sl, :, :]), r=[k], w=[k])
            S.op("pool", lambda e: e.tensor_scalar(out=rs[:, sl, 0:1], in0=mv[:, sl, 1:2], scalar1=EPS,
                                                  scalar2=None, op0=ALU.add), r=[k], w=[k])
            S.op("pool", lambda e: e.tensor_tensor(out=rs[:, sl, 1:2], in0=rs[:, sl, 0:1], in1=mhalf[:],
                                                  op=ALU.pow), r=[k, "mhalf"], w=[k])
            S.op("dve", lambda e: e.tensor_scalar(out=rs[:, sl, 2:3], in0=mv[:, sl, 0:1], scalar1=rs[:, sl, 1:2],
                                                 scalar2=-1.0, op0=ALU.mult, op1=ALU.mult), r=[k], w=[k])
            return k, rs[:, sl, 1:2], rs[:, sl, 2:3]

        groups = [list(range(g * 4, g * 4 + 4)) for g in range(5)] + [[20, 21]]
        def a_group(gi, tiles):
            which = 1 if gi == 5 else 0
            bufs = []
            for i, t in enumerate(tiles):
                bi = (gi % 2) * 4 + i
                buf = xt[:, bi, :]
                bk = ("xt", bi)
                bufs.append((buf, bk))
                src = x_ext[t * 128:(t + 1) * 128, :] if t < 20 else ctx_d[(t - 20) * 128:(t - 19) * 128, :]
                S.dma("sp", buf, src, w=[bk])
                k, rstd, nbias = layer_norm_stats(buf, bk)
                S.op("act", lambda e, buf=buf, rstd=rstd, nbias=nbias: e.activation(
                    out=buf, in_=buf, func=AF.Identity, scale=rstd, bias=nbias), r=[bk, k], w=[bk])
                S.op("dve", lambda e, buf=buf: e.tensor_tensor(out=buf, in0=buf, in1=ginb, op=ALU.mult),
                     r=[bk, "ginb"], w=[bk])
                S.op("pool", lambda e, buf=buf: e.tensor_tensor(out=buf, in0=buf, in1=binb, op=ALU.add),
                     r=[bk, "binb"], w=[bk])
                if 2 <= t < 18:
                    S.dma("sp", h_scr[(t - 2) * 128:(t - 1) * 128, :], buf, r=[bk], w=[("h_scr", t - 2)])
            ntok = 128 * len(tiles)
            tok0 = tiles[0] * 128
            for kc in range(8):
                pb = ps[kc % 4]
                pk = ("ps", kc % 4)
                for i, (buf, bk) in enumerate(bufs):
                    S.op("pe", lambda e, pb=pb, i=i, buf=buf, kc=kc: e.transpose(
                        out=pb[:, i * 128:(i + 1) * 128], in_=buf[:, kc * 128:(kc + 1) * 128], identity=ident_f[:]),
                        r=[bk, "ident_f"], w=[pk], inc=(i == len(bufs) - 1))
                S.op("act", lambda e, pb=pb, kc=kc, ntok=ntok, tok0=tok0, which=which: e.activation(
                    out=hmT[:, kc, tok0:tok0 + ntok], in_=pb[:, 0:ntok], func=AF.Identity,
                    scale=A1c[:, kc, which:which + 1], bias=modT[:, kc, which:which + 1]),
                    r=[pk, "A1c", "modT"], w=[("hmT", gi)])

        seqs = []
        for gi, tiles in enumerate(groups):
            S.record()
            a_group(gi, tiles)
            seqs.append(S.stop())
        S.play(seqs, 2)
        dbg_dump("dbg_hmT", hmT, ("hmT", 0))
        S.barrier()
        if stop_after == "A":
            return nc

        qT = carve(A2, 0, [128, 4, NOWN], BF)
        kT = carve(A2, 16384, [128, 4, NALL], BF)
        V = carve(A2, 38912, [128, 176, 65], BF)
        ypoolT = carve(A3, 0, [128, 4, NOWN], BF)
        yattnT = carve(A3, 16384, [128, 4, NOWN], BF)
        pu = carve(A3, 16384, [128, NEXT], F32)
        pa = carve(A3, 16384 + 10240, [128, NEXT], F32)
        pbuf = carve(A3, 16384 + 20480, [128, NEXT], F32)
        wblk = [carve(A4, 0, [128, 8, 512], BF), carve(A4, 8192, [128, 8, 512], BF)]
        pooledT = carve(A4, 16384, [128, NOWN], BF)
        tmp8 = carve(A4, 20480, [128, 16], F32)

        def load_wblk(j):
            S.dma("pool", wblk[j % 2], w_in[:, j * 512:(j + 1) * 512].rearrange("(k p) n -> p k n", p=128),
                  w=[("wblk", j % 2)])

        evac_ctr = [0]

        def evac_copy(out, in_, r, w, eng=None):
            evac_ctr[0] += 1
            if (eng == "act") or (eng is None and evac_ctr[0] % 2 == 0):
                S.op("act", lambda e: e.activation(out=out, in_=in_, func=AF.Copy), r=r, w=w)
            else:
                S.op("dve", lambda e: e.tensor_copy(out=out, in_=in_), r=r, w=w)

        pctr = [0]

        def next_ps(nbanks=8):
            i = pctr[0] % nbanks
            pctr[0] += 1
            return ps[i], ("ps", i)

        load_wblk(0)
        load_wblk(1)
        S.op("pool", lambda e: e.memset(V[:, :, 64:65], 1.0), w=["Vones"])
        for mc in range(4):
            for tb in range(4):
                pb, pk = next_ps()
                for kc in range(8):
                    S.op("pe", lambda e, pb=pb, kc=kc, mc=mc, tb=tb: e.matmul(
                        pb[:, :], lhsT=wblk[0][:, kc, mc * 128:(mc + 1) * 128],
                        rhs=hmT[:, kc, OWN0 + tb * 512:OWN0 + (tb + 1) * 512], start=(kc == 0), stop=(kc == 7)),
                        r=[("wblk", 0), "hmT"], w=[pk], inc=(kc == 7))
                evac_copy(qT[:, mc, tb * 512:(tb + 1) * 512], pb[:, :], [pk], ["qT"])
        load_wblk(2)
        for mc in range(4):
            for blk in range(6):
                n = 512 if blk < 5 else 256
                t0 = blk * 512
                pb, pk = next_ps()
                for kc in range(8):
                    S.op("pe", lambda e, pb=pb, kc=kc, mc=mc, t0=t0, n=n: e.matmul(
                        pb[:, 0:n], lhsT=wblk[1][:, kc, mc * 128:(mc + 1) * 128],
                        rhs=hmT[:, kc, t0:t0 + n], start=(kc == 0), stop=(kc == 7)),
                        r=[("wblk", 1), "hmT"], w=[pk], inc=(kc == 7))
                evac_copy(kT[:, mc, t0:t0 + n], pb[:, 0:n], [pk], ["kT"])
        load_wblk(3)
        for t in range(22):
            pb, pk = next_ps()
            for kc in range(8):
                S.op("pe", lambda e, pb=pb, kc=kc, t=t: e.matmul(
                    pb[:, :], lhsT=hmT[:, kc, t * 128:(t + 1) * 128], rhs=wblk[0][:, kc, :],
                    start=(kc == 0), stop=(kc == 7)),
                    r=[("wblk", 0), "hmT"], w=[pk], inc=(kc == 7))
            evac_copy(V[:, t * 8:(t + 1) * 8, 0:64], pb[:, :].rearrange("p (h d) -> p h d", d=64), [pk], [("V", t)])
        for g in range(4):
            wdw = 2 ** (g + 1)
            hw = wdw // 2
            for blk in range(5):
                pb, pk = next_ps()
                for kc in range(8):
                    S.op("pe", lambda e, pb=pb, kc=kc, g=g, blk=blk: e.matmul(
                        pb[:, :], lhsT=wblk[1][:, kc, g * 128:(g + 1) * 128],
                        rhs=hmT[:, kc, blk * 512:(blk + 1) * 512], start=(kc == 0), stop=(kc == 7)),
                        r=[("wblk", 1), "hmT"], w=[pk], inc=(kc == 7))
                evac_copy(pu[:, blk * 512:(blk + 1) * 512], pb[:, :], [pk], ["pu"])
            S.op("dve", lambda e: e.tensor_tensor(out=pu[:, OWN0 - 8:OWN0], in0=pu[:, OWN0 - 8:OWN0],
                                                 in1=pfix_t[:, 0:8], op=ALU.mult), r=["pu", "pfix"], w=["pu"])
            S.op("dve", lambda e: e.tensor_tensor(out=pu[:, OWN0 + NOWN:OWN0 + NOWN + 8],
                                                 in0=pu[:, OWN0 + NOWN:OWN0 + NOWN + 8],
                                                 in1=pfix_t[:, 8:16], op=ALU.mult), r=["pu", "pfix"], w=["pu"])
            lo = OWN0 - 16
            src, srck = pu, "pu"
            dsts = [(pa, "pa"), (pbuf, "pb")]
            for k in range(g + 1):
                sh = 2 ** k
                hi = OWN0 + NOWN + 32 - 2 * sh * 2
                dst, dstk = dsts[k % 2]
                S.op("dve", lambda e, src=src, dst=dst, sh=sh, hi=hi: e.tensor_tensor(
                    out=dst[:, lo:hi], in0=src[:, lo:hi], in1=src[:, lo + sh:hi + sh], op=ALU.add),
                    r=[srck], w=[dstk])
                src, srck = dst, dstk
            S.op("dve", lambda e, src=src: e.scalar_tensor_tensor(
                out=pooledT[:, :], in0=src[:, OWN0 - hw:OWN0 + NOWN - hw], scalar=1.0 / wdw,
                in1=pu[:, OWN0:OWN0 + NOWN], op0=ALU.mult, op1=ALU.subtract), r=[srck, "pu"], w=["pooledT"])
            for side in range(2):
                o0 = OWN0 if side == 0 else OWN0 + NOWN - 8
                S.op("dve", lambda e, src=src, o0=o0, side=side: e.tensor_tensor(
                    out=tmp8[:, side * 8:(side + 1) * 8], in0=src[:, o0 - hw:o0 - hw + 8],
                    in1=pfix_t[:, 16 + g * 16 + side * 8:16 + g * 16 + side * 8 + 8], op=ALU.mult),
                    r=[srck, "pfix"], w=["tmp8"])
                S.op("dve", lambda e, o0=o0, side=side: e.scalar_tensor_tensor(
                    out=pooledT[:, o0 - OWN0:o0 - OWN0 + 8], in0=tmp8[:, side * 8:(side + 1) * 8],
                    scalar=1.0 / wdw, in1=pu[:, o0:o0 + 8], op0=ALU.mult, op1=ALU.subtract),
                    r=["tmp8", "pu"], w=["pooledT"])
            for tb in range(4):
                pb, pk = next_ps()
                S.op("pe", lambda e, pb=pb, tb=tb: e.matmul(
                    pb[:, :], lhsT=wgrp_t[:, g, :], rhs=pooledT[:, tb * 512:(tb + 1) * 512], start=True, stop=True),
                    r=["wgrp", "pooledT"], w=[pk])
                S.op("act", lambda e, pb=pb, tb=tb: e.activation(
                    out=ypoolT[:, g, tb * 512:(tb + 1) * 512], in_=pb[:, :], func=AF.Identity,
                    scale=pscale_t[:, g:g + 1]), r=[pk, "pscale"], w=["ypoolT"])
        dbg_dump("dbg_qT", qT, "qT")
        dbg_dump("dbg_kT", kT, "kT")
        if debug:
            S.barrier()
            S.dma("sp", dbg["dbg_V"].rearrange("p t h d -> p (t h) d"), V)
        dbg_dump("dbg_ypoolT", ypoolT, "ypoolT")
        S.barrier()
        if stop_after == "B":
            return nc

        tabE = carve(A4, 0, [128, 8, 256], F32)
        tabO = carve(A4, 8192, [128, 8, 320], F32)
        tabS2 = [carve(A4, 18432, [128, 8, 384], F32), carve(A3, 32768, [128, 8, 384], F32)]
        sbuf_s = [carve(A4, 30720 + i * 1536, [128, 384], F32) for i in range(2)]
        pT = [carve(A4, 33792 + i * 1024, [128, 512], BF) for i in range(3)]
        Obf = [carve(A4, 36864 + i * 1024, [128, 512], BF) for i in range(2)]
        S.dma("sp", tabE, tabE_d.rearrange("p h j q -> p h (j q)"), w=["tabE"])
        S.dma("sp", tabO, tabO_d.rearrange("p h j q -> p h (j q)"), w=["tabO"])
        units = [(rl, h) for rl in range(RPC) for h in range(NH)]

        def emit_qk(u):
            rl, h = units[u]
            f, nl, kind = row_window(rl)
            hp, half = h // 2, h % 2
            pb = ps[u % 3]
            pk = ("ps", u % 3)
            if kind == "S" and h == 0:
                sp_ = SPECIAL_SLOT[rl] % 2
                S.dma("sp", tabS2[sp_], tabS_d[SPECIAL_SLOT[rl]].rearrange("p h j q -> p h (j q)"), w=[("tabS", sp_)])
            q_ap = qT[half * 64:(half + 1) * 64, hp, rl * 64:(rl + 1) * 64]
            for j in range(nl + 2):
                t0 = (f + j) * 128 if j < nl else NEXT + (j - nl) * 128
                S.op("pe", lambda e, pb=pb, j=j, t0=t0: e.matmul(
                    pb[:, j * 64:(j + 1) * 64], lhsT=kT[half * 64:(half + 1) * 64, hp, t0:t0 + 128], rhs=q_ap,
                    start=True, stop=True), r=["qT", "kT"], w=[pk], inc=(j == nl + 1))

        def emit_softmax(u):
            rl, h = units[u]
            f, nl, kind = row_window(rl)
            pb = ps[u % 3]
            pk = ("ps", u % 3)
            if kind == "S":
                sp_ = SPECIAL_SLOT[rl] % 2
                tab, tk = tabS2[sp_], ("tabS", sp_)
            else:
                tab, tk = {"E": (tabE, "tabE"), "O": (tabO, "tabO")}[kind]
            sbt = sbuf_s[u % 2]
            sk = ("sbs", u % 2)
            pt = pT[u % 3]
            ptk = ("pT", u % 3)
            S.op("dve", lambda e: e.scalar_tensor_tensor(
                out=sbt[:, 0:nl * 64], in0=pb[:, 0:nl * 64], scalar=SCALE,
                in1=tab[:, h, 0:nl * 64], op0=ALU.mult, op1=ALU.add), r=[pk, tk], w=[sk])
            S.op("act", lambda e: e.activation(out=pt[:, 0:nl * 64], in_=sbt[:, 0:nl * 64], func=AF.Exp),
                 r=[sk], w=[ptk])
            S.op("act", lambda e: e.activation(out=pt[:, nl * 64:(nl + 2) * 64], in_=pb[:, nl * 64:(nl + 2) * 64],
                                              func=AF.Exp, scale=SCALE), r=[pk], w=[ptk])

        def emit_pv(u):
            rl, h = units[u]
            f, nl, kind = row_window(rl)
            pt = pT[u % 3]
            ptk = ("pT", u % 3)
            bank = 3 + (rl % 2) * 2 + h // 4
            ob = ps[bank]
            hh = h % 4
            for j in range(nl + 2):
                vt = f + j if j < nl else 20 + (j - nl)
                S.op("pe", lambda e, j=j, vt=vt: e.matmul(
                    ob[0:64, hh * 65:(hh + 1) * 65], lhsT=pt[:, j * 64:(j + 1) * 64], rhs=V[:, vt * 8 + h, :],
                    start=(j == 0), stop=(j == nl + 1)),
                    r=[ptk, ("V", vt), "Vones"], w=[("ps", bank)], inc=(j == nl + 1))

        def emit_norm(rl):
            par = rl % 2
            for b in range(2):
                bank = 3 + par * 2 + b
                ob = ps[bank][0:64, 0:260].rearrange("p (h d) -> p h d", d=65)
                S.op("dve", lambda e, ob=ob, b=b: e.reciprocal(out=rden[:, par, b * 4:(b + 1) * 4], in_=ob[:, :, 64]),
                     r=[("ps", bank)], w=[("rden", par, b)])
                S.op("dve", lambda e, ob=ob, b=b: e.tensor_tensor(
                    out=Obf[par][0:64, b * 256:(b + 1) * 256].rearrange("p (h d) -> p h d", d=64),
                    in0=ob[:, :, 0:64],
                    in1=rden[:, par, b * 4:(b + 1) * 4].unsqueeze(2).broadcast_to([64, 4, 64]), op=ALU.mult),
                    r=[("ps", bank), ("rden", par, b)], w=[("Obf", par)])

        def emit_finish(rl):
            par = rl % 2
            pst = ps[7][:, 0:128].bitcast(BF)
            for mc in range(4):
                S.op("pe", lambda e, mc=mc: e.transpose(
                    out=pst[:, mc * 64:(mc + 1) * 64], in_=Obf[par][0:64, mc * 128:(mc + 1) * 128],
                    identity=ident_b[0:64, 0:64]), r=[("Obf", par), "ident_b"], w=[("ps", 7)], inc=(mc == 3))
            evac_copy(yattnT[:, :, rl * 64:(rl + 1) * 64], pst.rearrange("p (a b) -> p a b", b=64),
                      [("ps", 7)], ["yattnT"])

        nu = len(units)
        emit_qk(0)
        emit_qk(1)
        for u in range(nu):
            rl, h = units[u]
            if u + 2 < nu:
                emit_qk(u + 2)
            emit_softmax(u)
            emit_pv(u)
            if h == 7:
                emit_norm(rl)
            if h == 1 and rl > 0:
                emit_finish(rl - 1)
        emit_finish(RPC - 1)
        dbg_dump("dbg_yattnT", yattnT, "yattnT")
        S.barrier()
        if stop_after == "C":
            return nc

        wa_t = carve(A2, 0, [128, 4, 1024], BF)
        wp_t = carve(A2, 8192, [128, 4, 1024], BF)
        mergedT = carve(A2, 16384, [128, 8, NOWN], BF)
        wout_b = carve(A2, 49152, [128, 8, 1024], BF)
        wga = carve(A4, 0, [128, 8, 1024], BF)
        wgb = carve(A4, 16384, [128, 8, 1024], BF)
        sg = [carve(A4, 32768 + i * 1024, [128, 512], BF) for i in range(4)]
        t12 = [carve(A4, 36864 + i * 2048, [128, 512], F32) for i in range(3)]
        S.dma("pool", wga, w_in[:, 2048:3072].rearrange("(k p) n -> p k n", p=128), w=["wga"])
        S.dma("pool", wgb, w_in[:, 3072:4096].rearrange("(k p) n -> p k n", p=128), w=["wgb"])
        S.dma("pool", wa_t, wa_d.rearrange("(k p) n -> p k n", p=128), w=["wa"])
        S.dma("pool", wp_t, wp_d.rearrange("(k p) n -> p k n", p=128), w=["wp"])
        un = 0
        for tb in range(4):
            tsl = slice(tb * 512, (tb + 1) * 512)
            hsl = slice(OWN0 + tb * 512, OWN0 + (tb + 1) * 512)
            for mc in range(8):
                base = (un % 2) * 4
                un += 1
                pg, pgb, pya, pyp = ps[base], ps[base + 1], ps[base + 2], ps[base + 3]
                kg, kgb, kya, kyp = [("ps", base + i) for i in range(4)]
                msl = slice(mc * 128, (mc + 1) * 128)
                for kc in range(8):
                    S.op("pe", lambda e, kc=kc, pg=pg, msl=msl, hsl=hsl: e.matmul(
                        pg[:, :], lhsT=wga[:, kc, msl], rhs=hmT[:, kc, hsl], start=(kc == 0), stop=(kc == 7)),
                        r=["wga", "hmT"], w=[kg], inc=(kc == 7))
                for kc in range(8):
                    S.op("pe", lambda e, kc=kc, pgb=pgb, msl=msl, hsl=hsl: e.matmul(
                        pgb[:, :], lhsT=wgb[:, kc, msl], rhs=hmT[:, kc, hsl], start=(kc == 0), stop=(kc == 7)),
                        r=["wgb", "hmT"], w=[kgb], inc=(kc == 7))
                for kc in range(4):
                    S.op("pe", lambda e, kc=kc, pya=pya, msl=msl, tsl=tsl: e.matmul(
                        pya[:, :], lhsT=wa_t[:, kc, msl], rhs=yattnT[:, kc, tsl], start=(kc == 0), stop=(kc == 3)),
                        r=["wa", "yattnT"], w=[kya], inc=(kc == 3))
                for kc in range(4):
                    S.op("pe", lambda e, kc=kc, pyp=pyp, msl=msl, tsl=tsl: e.matmul(
                        pyp[:, :], lhsT=wp_t[:, kc, msl], rhs=ypoolT[:, kc, tsl], start=(kc == 0), stop=(kc == 3)),
                        r=["wp", "ypoolT"], w=[kyp], inc=(kc == 3))
                s0, s1 = sg[(un % 2) * 2], sg[(un % 2) * 2 + 1]
                ks0, ks1 = ("sg", (un % 2) * 2), ("sg", (un % 2) * 2 + 1)
                S.op("act", lambda e, s0=s0, pg=pg: e.activation(out=s0, in_=pg[:, :], func=AF.Sigmoid),
                     r=[kg], w=[ks0])
                S.op("act", lambda e, s1=s1, pgb=pgb: e.activation(out=s1, in_=pgb[:, :], func=AF.Sigmoid),
                     r=[kgb], w=[ks1])
                ta, tbb = t12[0], t12[1]
                S.op("dve", lambda e, s0=s0, pya=pya: e.tensor_tensor(out=ta, in0=s0, in1=pya[:, :], op=ALU.mult),
                     r=[ks0, kya], w=["t12a"])
                S.op("dve", lambda e, s1=s1, pyp=pyp: e.tensor_tensor(out=tbb, in0=s1, in1=pyp[:, :], op=ALU.mult),
                     r=[ks1, kyp], w=["t12b"])
                S.op("dve", lambda e, mc=mc, tsl=tsl: e.tensor_tensor(out=mergedT[:, mc, tsl], in0=ta, in1=tbb,
                                                                    op=ALU.add),
                     r=["t12a", "t12b"], w=["mergedT"])
        dbg_dump("dbg_mergedT", mergedT, "mergedT")
        S.barrier()
        if stop_after == "D1":
            return nc

        I32 = mybir.dt.int32
        hm2Tt = [carve(A1, i * 2048, [128, 8, 128], BF) for i in range(2)]
        oh1_all = carve(A1, 4096, [128, 16, 32], F32)
        oh2_all = carve(A1, 6144, [128, 16, 32], F32)
        Pall = carve(A1, 8192, [128, 16, 32], F32)
        tmpP = carve(A1, 10240, [128, 16, 32], F32)
        Mbf = carve(A1, 12288, [128, 16, 32], BF)
        c12 = carve(A1, 13312, [128, 2, 16], F32)
        Crun = carve(A1, 13440, [128, 32], F32)
        ebase = carve(A1, 13568, [128, 32], F32)
        ebase_i = carve(A1, 13696, [128, 32], F32).bitcast(I32)
        destf = carve(A1, 13824, [128, 32], F32)
        dest_i = carve(A1, 13952, [128, 32], F32).bitcast(I32)
        tokid_i = carve(A1, 14080, [128, 16], F32).bitcast(I32)
        lsrc = carve(A1, 14144, [128, 64], F32)
        lsrc_i = lsrc.bitcast(I32)
        counts_i = carve(A1, 14400, [128, 32], F32).bitcast(I32)
        Lstrict = carve(A1, 14528, [128, 128], BF)
        ones_bf = carve(A1, 14784, [128, 128], BF)
        zeros_i = carve(A1, 15040, [128, 1024], F32)
        Ball = carve(A1, 21504, [128, 16, 32], F32)
        hm2_scr = nc.dram_tensor("hm2_scr", [NOWN, D], BF, kind="Internal").ap()
        list_scr = nc.dram_tensor("list_scr", [NE * NOWN, 2], I32, kind="Internal").ap()
        ybuf = nc.dram_tensor("ybuf", [NE * NOWN, D], F32, kind="Internal").ap()

        wo32 = carve(A3, 0, [128, 8, 1024], F32)
        bc_g1 = carve(A3, 32768, [128, 1024], F32)
        bc_onep = carve(A3, 36864, [128, 1024], F32)
        bc_ln1g = carve(A4, 0, [128, 1024], F32)
        bc_ln1b = carve(A4, 4096, [128, 1024], F32)
        bc_A2 = carve(A4, 8192, [128, 1024], F32)
        bc_B2 = carve(A4, 12288, [128, 1024], F32)
        htile = [carve(A4, 16384 + i * 4096, [128, 1024], F32) for i in range(2)]
        rtile = [carve(A4, 24576 + i * 4096, [128, 1024], F32) for i in range(2)]
        hntile = [carve(A4, 32768 + i * 4096, [128, 1024], F32) for i in range(2)]
        hm2tiles = [carve(A4, 40960, [128, 1024], BF), carve(A1, 19456, [128, 1024], BF)]
        S.dma("sp", wo32, wout_d.rearrange("(k p) n -> p k n", p=128), w=["wo32"])
        S.dma("sp", bc_g1, mod_bc(2), w=["bc_g1"])
        S.dma("sp", bc_onep, mod_bc(4), w=["bc_onep"])
        S.dma("sp", bc_B2, mod_bc(3), w=["bc_B2"])
        S.dma("sp", bc_ln1g, ln1g_d.partition_broadcast(128), w=["bc_ln1g"])
        S.dma("sp", bc_ln1b, ln1b_d.partition_broadcast(128), w=["bc_ln1b"])
        S.op("pool", lambda e: e.memset(zeros_i, 0.0), w=["zeros_i"])
        S.dma("sp", list_scr.rearrange("(p r) c -> p (r c)", p=128), zeros_i.bitcast(I32), r=["zeros_i"],
              w=["list_scr"])
        S.op("pool", lambda e: e.memset(ones_bf, 1.0), w=["ones_bf"])
        S.op("pool", lambda e: e.memset(Lstrict, 1.0), w=["Lstrict"])
        S.op("pool", lambda e: e.affine_select(out=Lstrict, in_=Lstrict, pattern=[[1, 128]], compare_op=ALU.is_gt,
                                              fill=0.0, base=0, channel_multiplier=-1),
             r=["Lstrict"], w=["Lstrict"])
        S.op("pool", lambda e: e.iota(out=ebase_i, pattern=[[NOWN, 32]], base=0, channel_multiplier=0),
             w=["ebase_i"])
        S.op("pool", lambda e: e.tensor_copy(out=ebase, in_=ebase_i), r=["ebase_i"], w=["ebase"])
        S.op("pool", lambda e: e.iota(out=tokid_i, pattern=[[128, 16]], base=0, channel_multiplier=1),
             w=["tokid_i"])
        S.op("pool", lambda e: e.memset(Crun, 0.0), w=["Crun"])
        for kc in range(8):
            S.op("dve", lambda e, kc=kc: e.tensor_tensor(out=wout_b[:, kc, :], in0=wo32[:, kc, :], in1=bc_g1,
                                                        op=ALU.mult), r=["wo32", "bc_g1"], w=["wout_b"])
        S.op("pool", lambda e: e.tensor_scalar(out=bc_onep, in0=bc_onep, scalar1=1.0, scalar2=None, op0=ALU.add),
             r=["bc_onep"], w=["bc_onep"])
        S.op("pool", lambda e: e.tensor_tensor(out=bc_A2, in0=bc_ln1g, in1=bc_onep, op=ALU.mult),
             r=["bc_ln1g", "bc_onep"], w=["bc_A2"])
        S.op("pool", lambda e: e.tensor_tensor(out=bc_onep, in0=bc_ln1b, in1=bc_onep, op=ALU.mult),
             r=["bc_ln1b", "bc_onep", "bc_A2"], w=["bc_onep"])
        S.op("pool", lambda e: e.tensor_tensor(out=bc_B2, in0=bc_B2, in1=bc_onep, op=ALU.add),
             r=["bc_B2", "bc_onep"], w=["bc_B2"])

        def routing(t, pr, prk):
            sl = t % 2
            R = rt[:, sl, :]
            k = ("rt", sl)
            lg = R[:, 0:36]
            S.op("dve", lambda e: e.tensor_tensor(out=lg, in0=pr[:, 0:36], in1=br_t[:], op=ALU.add),
                 r=[prk, "br"], w=[k])
            gmax, ngmax, gsum, ptg = R[:, 36:37], R[:, 37:38], R[:, 38:39], R[:, 39:40]
            eg, oh, pen = R[:, 40:44], R[:, 44:48], R[:, 48:52]
            ml = R[:, 52:84]
            top8 = R[:, 84:92]
            e2, den, c1 = R[:, 92:93], R[:, 93:94], R[:, 94:95]
            S.op("dve", lambda e: e.tensor_reduce(out=gmax, in_=lg[:, 0:4], axis=AX.X, op=ALU.max), r=[k], w=[k])
            S.op("dve", lambda e: e.tensor_scalar(out=ngmax, in0=gmax, scalar1=-1.0, scalar2=None, op0=ALU.mult),
                 r=[k], w=[k])
            S.op("act", lambda e: e.activation(out=eg, in_=lg[:, 0:4], func=AF.Exp, bias=ngmax, scale=1.0),
                 r=[k], w=[k])
            S.op("dve", lambda e: e.tensor_reduce(out=gsum, in_=eg, axis=AX.X, op=ALU.add), r=[k], w=[k])
            S.op("dve", lambda e: e.reciprocal(out=ptg, in_=gsum), r=[k], w=[k])
            S.op("dve", lambda e: e.tensor_scalar(out=oh, in0=lg[:, 0:4], scalar1=gmax, scalar2=None,
                                                 op0=ALU.is_equal), r=[k], w=[k])
            S.op("dve", lambda e: e.tensor_scalar(out=pen, in0=oh, scalar1=1e30, scalar2=-1e30, op0=ALU.mult,
                                                 op1=ALU.add), r=[k], w=[k])
            S.op("dve", lambda e: e.tensor_tensor(
                out=ml.rearrange("p (g x) -> p g x", x=8), in0=lg[:, 4:36].rearrange("p (g x) -> p g x", x=8),
                in1=pen.unsqueeze(2).broadcast_to([128, 4, 8]), op=ALU.add), r=[k], w=[k])
            S.op("dve", lambda e: e.max(out=top8, in_=ml), r=[k], w=[k])
            S.op("dve", lambda e: e.tensor_scalar(out=den, in0=top8[:, 0:1], scalar1=-1.0, scalar2=None,
                                                 op0=ALU.mult), r=[k], w=[k])
            S.op("act", lambda e: e.activation(out=e2, in_=top8[:, 1:2], func=AF.Exp, bias=den, scale=1.0),
                 r=[k], w=[k])
            S.op("dve", lambda e: e.tensor_scalar(out=den, in0=e2, scalar1=1.0, scalar2=None, op0=ALU.add),
                 r=[k], w=[k])
            S.op("dve", lambda e: e.reciprocal(out=c1, in_=den), r=[k], w=[k])
            rk_ = ("route", t)
            S.op("dve", lambda e: e.tensor_tensor(out=c12[:, 0, t:t + 1], in0=c1, in1=ptg, op=ALU.mult),
                 r=[k], w=[rk_])
            S.op("dve", lambda e: e.tensor_tensor(out=c12[:, 1, t:t + 1], in0=c12[:, 0, t:t + 1], in1=e2,
                                                 op=ALU.mult), r=[k, rk_], w=[rk_])
            S.op("dve", lambda e: e.tensor_scalar(out=oh1_all[:, t, :], in0=ml, scalar1=top8[:, 0:1], scalar2=None,
                                                 op0=ALU.is_equal), r=[k], w=[rk_])
            S.op("dve", lambda e: e.tensor_scalar(out=oh2_all[:, t, :], in0=ml, scalar1=top8[:, 1:2], scalar2=None,
                                                 op0=ALU.is_equal), r=[k], w=[rk_])
            S.op("dve", lambda e: e.tensor_tensor(out=Mbf[:, t, :], in0=oh1_all[:, t, :], in1=oh2_all[:, t, :],
                                                 op=ALU.add), r=[rk_], w=[rk_])
            pq = ps[6 + t % 2]
            S.op("pe", lambda e: e.matmul(pq[:, 64:96], lhsT=Lstrict, rhs=Mbf[:, t, :], start=True, stop=True),
                 r=[rk_, "Lstrict"], w=[prk], inc=False)
            S.op("pe", lambda e: e.matmul(pq[:, 96:128], lhsT=ones_bf, rhs=Mbf[:, t, :], start=True, stop=True),
                 r=[rk_, "ones_bf"], w=[prk])
            S.op("dve", lambda e: e.tensor_copy(out=Pall[:, t, :], in_=pq[:, 64:96]), r=[prk], w=[rk_])
            S.op("dve", lambda e: e.tensor_copy(out=Ball[:, t, :], in_=pq[:, 96:128]), r=[prk], w=[rk_])

        def d2_tile(t):
            par = t % 2
            hm2tile, hm2k = hm2tiles[par], ("hm2tile", par)
            ht, hk = htile[par], ("htile", par)
            S.dma("sp", ht, h_scr[t * 128:(t + 1) * 128, :], r=[("h_scr", t)], w=[hk])
            r_t, rk = rtile[par], ("rtile", par)
            for half in range(2):
                pb = ps[par * 2 + half]
                pk = ("ps", par * 2 + half)
                for kc in range(8):
                    S.op("pe", lambda e, pb=pb, kc=kc, half=half: e.matmul(
                        pb[:, :], lhsT=mergedT[:, kc, t * 128:(t + 1) * 128],
                        rhs=wout_b[:, kc, half * 512:(half + 1) * 512], start=(kc == 0), stop=(kc == 7)),
                        r=["mergedT", "wout_b"], w=[pk], inc=(kc == 7))
                S.op("dve", lambda e, pb=pb, half=half: e.scalar_tensor_tensor(
                    out=r_t[:, half * 512:(half + 1) * 512], in0=ht[:, half * 512:(half + 1) * 512], scalar=ALPHA,
                    in1=pb[:, :], op0=ALU.mult, op1=ALU.add), r=[hk, pk], w=[rk])
            k, rstd, nbias = layer_norm_stats(r_t, rk)
            S.op("act", lambda e, rstd=rstd, nbias=nbias: e.activation(
                out=r_t, in_=r_t, func=AF.Identity, scale=rstd, bias=nbias), r=[rk, k], w=[rk])
            hn, hnk = hntile[par], ("hntile", par)
            S.op("pool", lambda e: e.tensor_tensor(out=hn, in0=r_t, in1=bc_ln1g, op=ALU.mult),
                 r=[rk, "bc_ln1g"], w=[hnk])
            S.op("pool", lambda e: e.tensor_tensor(out=hn, in0=hn, in1=bc_ln1b, op=ALU.add),
                 r=[hnk, "bc_ln1b"], w=[hnk])
            S.dma("sp", hn_scr[t * 128:(t + 1) * 128, :], hn, r=[hnk], w=[("hn_scr", t)])
            S.op("dve", lambda e: e.tensor_tensor(out=r_t, in0=r_t, in1=bc_A2, op=ALU.mult),
                 r=[rk, "bc_A2"], w=[rk])
            S.op("dve", lambda e, hm2tile=hm2tile, r_t=r_t: e.tensor_tensor(out=hm2tile, in0=r_t, in1=bc_B2, op=ALU.add),
                 r=[rk, "bc_B2"], w=[hm2k])
            S.dma("sp", hm2_scr[t * 128:(t + 1) * 128, :], hm2tile, r=[hm2k], w=[("hm2_scr", t)])
            pstb = ps[4 + par]
            pstk = ("ps", 4 + par)
            pst = pstb[:, :].bitcast(BF)
            for kc in range(8):
                S.op("pe", lambda e, kc=kc, pst=pst, hm2tile=hm2tile: e.transpose(
                    out=pst[:, kc * 128:(kc + 1) * 128], in_=hm2tile[:, kc * 128:(kc + 1) * 128],
                    identity=ident_b[:]), r=[hm2k, "ident_b"], w=[pstk], inc=(kc == 7))
            h2t, h2k = hm2Tt[par], ("hm2Tt", par)
            evac_copy(h2t, pst.rearrange("p (a b) -> p a b", b=128), [pstk], [h2k])
            pr = ps[6 + par]
            prk = ("ps", 6 + par)
            for kc in range(8):
                S.op("pe", lambda e, kc=kc, pr=pr, h2t=h2t: e.matmul(
                    pr[:, 0:36], lhsT=h2t[:, kc, :], rhs=wr_t[:, kc, :],
                    start=(kc == 0), stop=(kc == 7)), r=[h2k, "wr"], w=[prk], inc=(kc == 7))
            routing(t, pr, prk)

        seqs = []
        for t in range(16):
            S.record()
            d2_tile(t)
            seqs.append(S.stop())
        S.play(seqs, 2)
        allr = [("route", t) for t in range(16)]
        for t in range(16):
            S.op("dve", lambda e, t=t: e.tensor_tensor(out=Pall[:, t, :], in0=Pall[:, t, :], in1=Crun, op=ALU.add),
                 r=[("route", t), "Crun"], w=[("route", t)])
            S.op("dve", lambda e, t=t: e.tensor_tensor(out=Crun, in0=Ball[:, t, :], in1=Crun, op=ALU.add),
                 r=[("route", t), "Crun"], w=["Crun"])
        S.op("dve", lambda e: e.tensor_tensor(out=Pall, in0=Pall, in1=ebase.unsqueeze(1).broadcast_to([128, 16, 32]),
                                             op=ALU.add), r=allr + ["ebase"], w=["Pall"])
        for kk, oh_all in enumerate((oh1_all, oh2_all)):
            S.op("dve", lambda e, oh_all=oh_all: e.tensor_tensor(out=tmpP, in0=Pall, in1=oh_all, op=ALU.mult),
                 r=["Pall"] + allr, w=["tmpP"])
            S.op("dve", lambda e, kk=kk: e.tensor_reduce(out=destf[:, kk * 16:(kk + 1) * 16], in_=tmpP, axis=AX.X,
                                                        op=ALU.add), r=["tmpP"], w=["destf"])
        S.op("dve", lambda e: e.tensor_copy(out=dest_i, in_=destf), r=["destf"], w=["dest_i"])
        S.op("dve", lambda e: e.tensor_copy(out=counts_i, in_=Crun), r=["Crun"], w=["counts_i"])
        lsrc4 = lsrc.rearrange("p (k t c) -> p k t c", k=2, t=16)
        lsrc4_i = lsrc_i.rearrange("p (k t c) -> p k t c", k=2, t=16)
        for kk in range(2):
            S.op("dve", lambda e, kk=kk: e.tensor_copy(out=lsrc4_i[:, kk, :, 0], in_=tokid_i), r=["tokid_i"],
                 w=["lsrc"])
            S.op("dve", lambda e, kk=kk: e.tensor_copy(out=lsrc4[:, kk, :, 1], in_=c12[:, kk, :]), r=allr,
                 w=["lsrc"])
        for kk in range(2):
            for t in range(16):
                S.dma("pool", list_scr, lsrc4_i[:, kk, t, :], r=["lsrc", "dest_i", "list_scr"], w=[("lscat", kk, t)],
                      indirect=("scatter", dest_i[:, kk * 16 + t:kk * 16 + t + 1]))
        if debug:
            S.barrier()
            S.dma("sp", dbg["dbg_gates"][:, 0, :], destf)
            S.dma("sp", dbg["dbg_gates"][:, 1, :], Crun)
            S.dma("sp", dbg["dbg_gates"][:, 2, :], c12.rearrange("p k t -> p (k t)"))
        S.barrier()
        if stop_after == "D2":
            return nc

        wexp = []
        for i in range(2):
            o = i * 24576
            wexp.append((carve(A3, o, [128, 8, 512], BF), carve(A3, o + 8192, [128, 8, 512], BF),
                         carve(A3, o + 16384, [128, 4, 1024], BF)))
        lst = [carve(A1, 23552 + i * 8, [128, 2], F32) for i in range(5)]
        RB = 24576
        xg = [carve(A4, 64 + i * 2048, [128, 1024], BF) for i in range(2)] + [carve(A1, RB, [128, 1024], BF)]
        xgT = [carve(A4, 4160 + i * 2048, [128, 8, 128], BF) for i in range(2)] + [carve(A1, RB + 2048, [128, 8, 128], BF)]
        sa_t = [carve(A4, 8256 + i * 2048, [128, 512], F32) for i in range(2)] + [carve(A1, RB + 4096, [128, 512], F32)]
        act_t = [carve(A4, 12352 + i * 1024, [128, 512], BF) for i in range(2)] + [carve(A1, RB + 6144, [128, 512], BF)]
        actT = [carve(A4, 14400 + i * 1024, [128, 4, 128], BF) for i in range(2)] + [carve(A1, RB + 7168, [128, 4, 128], BF)]
        ysb = [carve(A4, 16448 + i * 4096, [128, 1024], F32) for i in range(2)] + [carve(A1, RB + 8192, [128, 1024], F32)]
        cond_engs = ("pe", "act", "dve", "pool", "sp")
        warm_rhs = carve(A1, RB + 12288, [128, 512], BF)
        S.op("pool", lambda e: e.memset(warm_rhs, 0.0), w=["warm_rhs"])

        def keep_warm(bank, bkey, n):
            for i in range(n):
                S.op("pe", lambda e: e.matmul(bank[:, :], lhsT=ones_bf, rhs=warm_rhs, start=True, stop=True),
                     r=["warm_rhs", "ones_bf"], w=[bkey], inc=False)

        stage = [carve(A2, i * 16384, [128, 4096], F32) for i in range(4)]
        stg_ctr = [0]

        def chunks(ex):
            wg_t, wu_t, wd_t = wexp[ex % 2]
            return (("g", wg_t, weg_d[ex], 8), ("u", wu_t, weu_d[ex], 8), ("d", wd_t, wed_d[ex], 4))

        def issue_chunk(ex, c):
            if ex >= NE:
                return
            nm, dst, src, kdim = chunks(ex)[c]
            si = (3 * ex + c) % 4
            st3 = stage[si].rearrange("p (k n) -> p k n", k=kdim)
            S.dma("sp", st3, src.rearrange("(p k) n -> p k n", p=128), w=[("stage", si)])

        def cast_chunk(ex, c):
            if ex >= NE:
                return
            nm, dst, src, kdim = chunks(ex)[c]
            si = (3 * ex + c) % 4
            st3 = stage[si].rearrange("p (k n) -> p k n", k=kdim)
            sk = ("stage", si)
            h = kdim // 2
            S.op("act", lambda e: e.activation(out=dst[:, 0:h, :], in_=st3[:, 0:h, :], func=AF.Copy),
                 r=[sk], w=[("wexp", ex % 2, nm, 0)])
            S.op("dve", lambda e: e.tensor_copy(out=dst[:, h:, :], in_=st3[:, h:, :]),
                 r=[sk], w=[("wexp", ex % 2, nm, 1)])

        def prefetch_lists(ex_):
            for par_ in range(2):
                li_ = (2 * ex_ + par_) % 4
                r0_ = ex_ * NOWN + par_ * 128
                S.dma("pool", lst[li_].bitcast(I32), list_scr[r0_:r0_ + 128, :], w=[("lst", li_)],
                      semkey=("lst", li_))

        n_exp = NE
        prefetch_lists(0)
        for c in range(3):
            issue_chunk(0, c)
        for c in range(3):
            cast_chunk(0, c)
        for c in range(3):
            issue_chunk(1, c)
        issue_chunk(2, 0)
        sctr = 0
        for ex in range(n_exp):
            if ex + 1 < n_exp:
                prefetch_lists(ex + 1)
            wg_t, wu_t, wd_t = wexp[ex % 2]
            wk = ("wexp", ex % 2)
            cvals = {}
            cregs = {}
            for en in cond_engs:
                cregs[en] = S.eng[en].alloc_register(f"ncnt_{en}_{ex}")
                S.raw(en, lambda e, en=en, ex=ex: e.reg_load(cregs[en], counts_i[0:1, ex:ex + 1]), r=["counts_i"])
                cvals[en] = S.eng[en].snap(cregs[en], donate=True)
            def slot_r1(j, part):
                par = j % 2
                bi = par if j < 2 else 2
                row0 = ex * NOWN + j * 128
                li = (2 * ex + par) % 4 if j < 2 else 4
                lt, lk = lst[li], ("lst", li)
                if j >= 2 and part == 0:
                    S.dma("pool", lt.bitcast(I32), list_scr[row0:row0 + 128, :], w=[lk], semkey=("lst", li))
                xgt, xgk = xg[bi], ("xg", bi)
                xT, xTk = xgT[bi], ("xgT", bi)
                if part == 0:
                    S.dma("pool", xgt, hm2_scr, r=[lk], w=[xgk], indirect=("gather", lt.bitcast(I32)[:, 0:1]),
                          semkey=("xg", bi))
                    pxk = ("ps", 0)
                    pxt = ps[0][:, :].bitcast(BF)
                    for kc in range(8):
                        S.op("pe", lambda e, kc=kc: e.transpose(
                            out=pxt[:, kc * 128:(kc + 1) * 128],
                            in_=xgt.rearrange("t (p k) -> t k p", k=8)[:, kc, :],
                            identity=ident_b[:]), r=[xgk, "ident_b"], w=[pxk], inc=(kc == 7))
                    evac_copy(xT, pxt.rearrange("p (a b) -> p a b", b=128), [pxk], [xTk], eng="dve")
                    return
                pa_, pu_ = ps[1 + 2 * par], ps[2 + 2 * par]
                ka, ku = ("ps", 1 + 2 * par), ("ps", 2 + 2 * par)
                for kc in range(8):
                    S.op("pe", lambda e, kc=kc: e.matmul(
                        pa_[:, :], lhsT=xT[:, kc, :], rhs=wg_t[:, kc, :], start=(kc == 0), stop=(kc == 7)),
                        r=[xTk, wk + ("g", 0), wk + ("g", 1)], w=[ka], inc=(kc == 7))
                for kc in range(8):
                    S.op("pe", lambda e, kc=kc: e.matmul(
                        pu_[:, :], lhsT=xT[:, kc, :], rhs=wu_t[:, kc, :], start=(kc == 0), stop=(kc == 7)),
                        r=[xTk, wk + ("u", 0), wk + ("u", 1)], w=[ku], inc=(kc == 7))
                st, stk = sa_t[bi], ("sa", bi)
                S.op("act", lambda e: e.activation(out=st, in_=pa_[:, :], func=AF.Silu), r=[ka], w=[stk])
                at_, atk = act_t[bi], ("act", bi)
                S.op("dve", lambda e: e.scalar_tensor_tensor(
                    out=at_, in0=st, scalar=lt[:, 1:2], in1=pu_[:, :], op0=ALU.mult, op1=ALU.mult),
                    r=[stk, ku, lk], w=[atk])

            def slot_r2(j, part):
                par = j % 2
                bi = par if j < 2 else 2
                row0 = ex * NOWN + j * 128
                at_, atk = act_t[bi], ("act", bi)
                pak = ("ps5", par)
                pat = ps[5][:, par * 256:(par + 1) * 256].bitcast(BF)
                aT, aTk = actT[bi], ("actT", bi)
                if part == 0:
                    for fc in range(4):
                        S.op("pe", lambda e, fc=fc: e.transpose(
                            out=pat[:, fc * 128:(fc + 1) * 128],
                            in_=at_.rearrange("t (p k) -> t k p", k=4)[:, fc, :],
                            identity=ident_b[:]), r=[atk, "ident_b"], w=[pak], inc=(fc == 3))
                    evac_copy(aT, pat.rearrange("p (a b) -> p a b", b=128), [pak], [aTk], eng="dve")
                    return
                yb, ybk = ysb[bi], ("ysb", bi)
                for half in range(2):
                    pd = ps[6 + half]
                    pdk = ("ps", 6 + half)
                    for fc in range(4):
                        S.op("pe", lambda e, fc=fc, pd=pd, half=half: e.matmul(
                            pd[:, :], lhsT=aT[:, fc, :], rhs=wd_t[:, fc, half * 512:(half + 1) * 512],
                            start=(fc == 0), stop=(fc == 3)), r=[aTk, wk + ("d", 0), wk + ("d", 1)], w=[pdk], inc=(fc == 3))
                    evac_copy(yb[:, half * 512:(half + 1) * 512], pd[:, :], [pdk], [ybk],
                              eng=("dve" if half == 0 else "act"))
                S.dma("act", ybuf[row0:row0 + 128, :], yb, r=[ybk], w=[("ybuf", ex, j)], semkey=("yst", bi))

            def cond(thr):
                return lambda en: cvals[en] > thr

            def emit_pair(j0, hooks=None):
                steps = [(slot_r1, j0, 0), (slot_r1, j0 + 1, 0), (slot_r1, j0, 1), (slot_r1, j0 + 1, 1),
                         (slot_r2, j0, 0), (slot_r2, j0 + 1, 0), (slot_r2, j0, 1), (slot_r2, j0 + 1, 1)]
                for i_, (fn_, jj, part_) in enumerate(steps):
                    S.begin_region()
                    fn_(jj, part_)
                    S.end_region(cond(jj * 128))
                    if hooks and i_ in hooks:
                        hooks[i_]()

            cast_chunk(ex + 1, 0)
            issue_chunk(ex + 2, 1)

            def hook_mid():
                cast_chunk(ex + 1, 1)
                issue_chunk(ex + 2, 2)

            def hook_end():
                cast_chunk(ex + 1, 2)
                issue_chunk(ex + 3, 0)

            emit_pair(0, hooks={3: hook_mid, 7: hook_end})
            S.begin_region()
            emit_pair(2)
            S.begin_region()
            emit_pair(4)
            emit_pair(6)
            S.begin_region()
            for j0 in (8, 10, 12, 14):
                emit_pair(j0)
            S.end_region(cond(1024))
            S.end_region(cond(512))
            S.end_region(cond(256))
            for en in cond_engs:
                S.eng[en].free_register(cregs[en])
        S.barrier()
        if stop_after == "E":
            return nc

        bc_g2 = carve(A4, 0, [128, 1024], F32)
        bc_ln2g = carve(A4, 4096, [128, 1024], F32)
        bc_ln2b = carve(A4, 8192, [128, 1024], F32)
        FW = 4
        hnt = [carve(A3, i * 4096, [128, 1024], F32) for i in range(FW)]
        ot = [carve(A3, 16384 + i * 4096, [128, 1024], F32) for i in range(FW)]
        y1t = [carve(A2, i * 4096, [128, 1024], F32) for i in range(FW)]
        y2t = [carve(A2, 16384 + i * 4096, [128, 1024], F32) for i in range(FW)]
        S.dma("sp", bc_g2, mod_bc(5), w=["bc_g2"])
        S.dma("sp", bc_ln2g, ln2g_d.partition_broadcast(128), w=["bc_ln2g"])
        S.dma("sp", bc_ln2b, ln2b_d.partition_broadcast(128), w=["bc_ln2b"])
        def f_tile(t):
            par = t % FW
            hn, hnk = hnt[par], ("hnt", par)
            S.dma("sp", hn, hn_scr[t * 128:(t + 1) * 128, :], w=[hnk])
            y1, y1k = y1t[par], ("y1t", par)
            y2, y2k = y2t[par], ("y2t", par)
            S.dma("pool", y1, ybuf, w=[y1k], indirect=("gather", dest_i[:, t:t + 1]))
            S.dma("pool", y2, ybuf, w=[y2k], indirect=("gather", dest_i[:, 16 + t:17 + t]))
            o_t, ok = ot[par], ("ot", par)
            S.op("dve", lambda e: e.tensor_tensor(out=y1, in0=y1, in1=y2, op=ALU.add), r=[y1k, y2k], w=[y1k])
            S.op("dve", lambda e: e.tensor_tensor(out=o_t, in0=y1, in1=bc_g2, op=ALU.mult),
                 r=["bc_g2", y1k], w=[ok])
            S.op("dve", lambda e: e.scalar_tensor_tensor(
                out=o_t, in0=hn, scalar=ALPHA, in1=o_t, op0=ALU.mult, op1=ALU.add), r=[hnk, ok], w=[ok])
            k, rstd, nbias = layer_norm_stats(o_t, ok)
            S.op("act", lambda e: e.activation(
                out=o_t, in_=o_t, func=AF.Identity, scale=rstd, bias=nbias), r=[ok, k], w=[ok])
            S.op("dve", lambda e: e.tensor_tensor(out=o_t, in0=o_t, in1=bc_ln2g, op=ALU.mult),
                 r=[ok, "bc_ln2g"], w=[ok])
            S.op("pool", lambda e: e.tensor_tensor(out=o_t, in0=o_t, in1=bc_ln2b, op=ALU.add),
                 r=[ok, "bc_ln2b"], w=[ok])
            S.dma("sp", out_d[t * 128:(t + 1) * 128, :], o_t, r=[ok], w=[("out", t)])

        seqs = []
        for t in range(16):
            S.record()
            f_tile(t)
            seqs.append(S.stop())
        S.play(seqs, FW)
        S.barrier()
    return nc


def _tables(rpb, core):
    col = np.arange(GW)
    col_start = np.clip(col - 8, 0, GW - 16)
    cmask = (col[None, :] >= col_start[:, None]) & (col[None, :] < col_start[:, None] + 16)
    col_off = np.clip(col[None, :] - col[:, None], -15, 15) + 15

    def table(c, rl, first, n):
        r = RPC * c + rl
        rs_ = min(max(r - 4, 0), ROWS - 8)
        out = np.full((128, NH, n, GW), NEG, np.float32)
        for j in range(n):
            for il in range(2):
                er = 2 * (first + j) + il
                gr = RPC * c - HALO + er
                if not (rs_ <= gr < rs_ + 8):
                    continue
                ro = gr - r + 7
                vals = rpb[:, ro, :][:, col_off]
                vals = np.where(cmask[None], vals, np.float32(NEG))
                out[il * 64:(il + 1) * 64, :, j, :] = np.transpose(vals, (2, 0, 1))
        return out

    tabE = table(1, 4, 2, 4)
    tabO = table(1, 5, 2, 5)
    tabS = np.full((7, 128, NH, 6, GW), NEG, np.float32)
    for rl, slot in SPECIAL_SLOT.items():
        f, n = SPECIAL[rl]
        tabS[slot, :, :, :n, :] = table(core, rl, f, n)
    return tabE, tabO, tabS


def _pool_fix(core):
    L = ROWS * GW
    fix = np.ones((128, 80), np.float32)
    if core == 0:
        fix[:, 0:8] = 0.0
    if core == NCORES - 1:
        fix[:, 8:16] = 0.0
    base = core * NOWN
    for g, w in enumerate((2, 4, 8, 16)):
        for side in range(2):
            for j in range(8):
                t = base + (j if side == 0 else NOWN - 8 + j)
                lo = min(max(t - w // 2, 0), L)
                hi = min(max(t + w - w // 2, 0), L)
                fix[:, 16 + g * 16 + side * 8 + j] = np.float32(w) / np.float32(hi - lo)
    return fix


def make_in_maps(inputs):
    f = lambda a: np.ascontiguousarray(np.asarray(a, dtype=np.float32))
    x = f(inputs["x"])[0]
    ctx = f(inputs["ctx"])[0]
    c = f(inputs["c"])[0]
    c_ctx = f(inputs["c_ctx"])
    cc = np.stack([c, c_ctx], axis=-1).reshape(8, 128, 2).transpose(1, 0, 2)
    b_modc = f(inputs["b_mod"])[0].reshape(48, 128).T
    rpb = f(inputs["rpb"])[0]
    shared = {
        "ctx": ctx, "cc": f(cc), "w_mod": f(inputs["w_mod"])[0], "b_modc": f(b_modc), "b_modr": f(inputs["b_mod"])[0],
        "ln_in_g": f(inputs["ln_in_g"]), "ln_in_b": f(inputs["ln_in_b"]), "w_in": f(inputs["w_in"])[0],
        "w_pool_grp": f(inputs["w_pool_grp"])[0],
        "pool_scale_c": f(f(inputs["pool_scale"])[0].reshape(4, 128).T),
        "w_attn_proj": f(inputs["w_attn_proj"])[0], "w_pool_proj": f(inputs["w_pool_proj"])[0],
        "w_out": f(inputs["w_out"])[0],
        "ln1_g": f(inputs["ln1_g"])[0], "ln1_b": f(inputs["ln1_b"])[0],
        "ln2_g": f(inputs["ln2_g"])[0], "ln2_b": f(inputs["ln2_b"])[0],
        "w_r": f(np.concatenate([f(inputs["w_router_group"])[0], f(inputs["w_router_expert"])[0]], axis=1)),
        "b_r": f(np.concatenate([f(inputs["b_router_group"])[0], f(inputs["b_router_expert"])[0]], axis=0)),
        "w_expert_gate": f(inputs["w_expert_gate"])[0], "w_expert_up": f(inputs["w_expert_up"])[0],
        "w_expert_down": f(inputs["w_expert_down"])[0],
    }
    in_maps = []
    for core in range(NCORES):
        xe = np.zeros((NEXT, D), np.float32)
        g0 = (RPC * core - HALO) * GW
        lo, hi = max(g0, 0), min(g0 + NEXT, ROWS * GW)
        xe[lo - g0:hi - g0] = x[lo:hi]
        tabE, tabO, tabS = _tables(rpb, core)
        m = dict(shared)
        m.update({"x_ext": xe, "tab_even": tabE, "tab_odd": tabO, "tab_sp": tabS, "pool_fix": _pool_fix(core)})
        in_maps.append(m)
    return in_maps


def kernel(**inputs):
    in_maps = make_in_maps(inputs)
    nc = build_nc()
    res = run_bass_kernel_spmd(nc, in_maps, core_ids=list(range(NCORES)))
    out = np.concatenate([np.asarray(r["out"]) for r in res.results], axis=0)
    return out.reshape(1, ROWS * GW, D).astype(np.float32)
```

```python
import numpy as np
import concourse.bass as bass
import concourse.mybir as mybir
from concourse.bass_utils import run_bass_kernel_spmd
from contextlib import ExitStack

F32 = mybir.dt.float32
BF = mybir.dt.bfloat16
AF = mybir.ActivationFunctionType
ALU = mybir.AluOpType
AX = mybir.AxisListType

NCORES = 8
D = 1024
GW = 64
ROWS = 256
RPC = 32
HALO = 4
EXT_ROWS = 40
NEXT = EXT_ROWS * GW
NOWN = RPC * GW
NCTX = 256
NALL = NEXT + NCTX
OWN0 = HALO * GW
NH = 8
HD = 64
NE = 32
DE = 512
ALPHA = 2.0 ** 0.25
EPS = 1e-5
NEG = -30000.0
SCALE = HD ** -0.5

SPECIAL = {0: (0, 6), 1: (0, 6), 2: (1, 5), 3: (1, 5), 29: (14, 5), 30: (14, 5), 31: (14, 6)}
SPECIAL_SLOT = {0: 0, 1: 1, 2: 2, 3: 3, 29: 4, 30: 5, 31: 6}


def row_window(rl):
    if rl in SPECIAL:
        f, n = SPECIAL[rl]
        return f, n, "S"
    if rl % 2 == 0:
        return rl // 2, 4, "E"
    return (rl - 1) // 2, 5, "O"


class Sched:
    def __init__(self, nc, sems, dma_sems):
        self.nc = nc
        self.eng = {"pe": nc.tensor, "act": nc.scalar, "dve": nc.vector, "pool": nc.gpsimd, "sp": nc.sync}
        self.sem = dict(sems)
        self.cnt = {k: 0 for k in sems}
        self.ndma = len(dma_sems)
        for i, s in enumerate(dma_sems):
            self.sem[("dma", i)] = s
            self.cnt[("dma", i)] = 0
        self.next_dma = 0
        self.named = {}
        self.n_named = 16
        self.seen = {e: {} for e in self.eng}
        self.last_w = {}
        self.readers = {}
        self.ninst = 0
        self.region = None
        self.rstack = []
        self.rec = None

    def _emit(self, e, fn):
        if self.region is None:
            fn()
        else:
            self.region["items"][e].append(fn)
        self.ninst += 1

    def _wait(self, e, tok):
        k, v = tok
        if e == "pe" and k == "pe":
            return
        if self.seen[e].get(k, 0) >= v:
            return
        self._emit(e, lambda: self.eng[e].wait_ge(self.sem[k], v))
        self.seen[e][k] = v

    def begin_region(self):
        self.rstack.append({"items": {e: [] for e in self.eng}, "seen0": {e: dict(self.seen[e]) for e in self.eng},
                            "cnt0": dict(self.cnt), "dmas": {e: [] for e in self.eng}})
        self.region = self.rstack[-1]

    def end_region(self, cond_fn):
        reg = self.rstack.pop()
        parent = self.rstack[-1] if self.rstack else None
        self.region = parent
        for e, items in reg["items"].items():
            if not items:
                continue
            n = self.cnt[e] - reg["cnt0"][e]
            cnt0 = reg["cnt0"][e]
            dmas = list(reg["dmas"][e])

            def emit_e(e=e, items=items, n=n, cnt0=cnt0, dmas=dmas):
                eng = self.eng[e]
                with eng.If(cond_fn(e)):
                    for it in items:
                        it()
                with eng.Else():
                    if n > 0:
                        if cnt0 > 0:
                            eng.wait_ge(self.sem[e], cnt0)
                        eng.sem_inc(self.sem[e], n)
                    for (k, prev) in dmas:
                        if prev > 0:
                            eng.wait_ge(self.sem[k], prev)
                        eng.sem_inc(self.sem[k], 16)

            if parent is None:
                emit_e()
            else:
                parent["items"][e].append(emit_e)
                parent["dmas"][e].extend(dmas)
            self.seen[e] = reg["seen0"][e]

    def raw(self, e, fn, r=()):
        self._deps(e, r, ())
        self._emit(e, lambda: fn(self.eng[e]))

    def _deps(self, e, r, w):
        for key in r:
            t = self.last_w.get(key)
            if t is not None:
                self._wait(e, t)
        for key in w:
            t = self.last_w.get(key)
            if t is not None:
                self._wait(e, t)
            for t in self.readers.get(key, ()):
                self._wait(e, t)

    def _commit(self, tok, r, w):
        for key in r:
            self.readers.setdefault(key, []).append(tok)
            if len(self.readers[key]) > 12:
                best = {}
                for k, v in self.readers[key]:
                    if best.get(k, 0) < v:
                        best[k] = v
                self.readers[key] = list(best.items())
        for key in w:
            self.last_w[key] = tok
            self.readers[key] = []

    def record(self):
        self.rec = []

    def stop(self):
        r_, self.rec = self.rec, None
        return r_

    def play(self, seqs, width):
        seqs = [q for q in seqs if q]
        active, nxt = [], 0
        while nxt < len(seqs) or active:
            while len(active) < width and nxt < len(seqs):
                active.append([seqs[nxt], 0])
                nxt += 1
            for a in list(active):
                seq = a[0]
                while True:
                    kind, args, kw = seq[a[1]]
                    a[1] += 1
                    getattr(self, kind)(*args, **kw)
                    glue = kind == "op" and not kw.get("inc", True)
                    if a[1] >= len(seq) or not glue:
                        break
                if a[1] >= len(seq):
                    active.remove(a)

    def op(self, e, fn, r=(), w=(), inc=True):
        if self.rec is not None:
            self.rec.append(("op", (e, fn), dict(r=list(r), w=list(w), inc=inc)))
            return
        self._deps(e, r, w)
        if inc:
            self.cnt[e] += 1
            self._emit(e, lambda: fn(self.eng[e]).then_inc(self.sem[e], 1))
            tok = (e, self.cnt[e])
        else:
            self._emit(e, lambda: fn(self.eng[e]))
            tok = (e, self.cnt[e] + 1)
        self._commit(tok, r, w)

    def dma(self, q, out, in_, r=(), w=(), indirect=None, semkey=None):
        if self.rec is not None:
            self.rec.append(("dma", (q, out, in_), dict(r=list(r), w=list(w), indirect=indirect, semkey=semkey)))
            return
        self._deps(q, r, w)
        if semkey is not None:
            if semkey not in self.named:
                assert len(self.named) < self.n_named
                self.named[semkey] = len(self.named)
            i = self.named[semkey]
        else:
            i = self.n_named + self.next_dma
            self.next_dma = (self.next_dma + 1) % (self.ndma - self.n_named)
        k = ("dma", i)
        prev = self.cnt[k]
        if prev > 0:
            self._wait(q, (k, prev))
        self.cnt[k] += 16
        if self.region is not None:
            self.region["dmas"][q].append((k, prev))
        if indirect is None:
            self._emit(q, lambda: self.eng[q].dma_start(out=out, in_=in_).then_inc(self.sem[k], 16))
        elif indirect[0] == "gather":
            self._emit(q, lambda: self.eng[q].indirect_dma_start(
                out=out, out_offset=None, in_=in_,
                in_offset=bass.IndirectOffsetOnAxis(ap=indirect[1], axis=0)).then_inc(self.sem[k], 16))
        else:
            self._emit(q, lambda: self.eng[q].indirect_dma_start(
                out=out, out_offset=bass.IndirectOffsetOnAxis(ap=indirect[1], axis=0), in_=in_,
                in_offset=None).then_inc(self.sem[k], 16))
        self._commit((k, self.cnt[k]), r, w)

    def barrier(self):
        for e in self.eng:
            for k, v in self.cnt.items():
                if v > 0 and k != e:
                    self._wait(e, (k, v))
        self.last_w = {}
        self.readers = {}


def carve(arena, off, shape, dt):
    esz = 4 if dt == F32 else 2
    n = int(np.prod(shape[1:]))
    nbytes = n * esz
    assert off % 4 == 0 and nbytes % 4 == 0
    assert off + nbytes <= arena.shape[1] * 4, (off, nbytes, arena.shape)
    ap = arena[:, off // 4:(off + nbytes) // 4]
    if dt != F32:
        ap = ap.bitcast(dt)
    if len(shape) == 3:
        ap = ap.rearrange("p (a b) -> p a b", b=shape[2])
    elif len(shape) == 4:
        ap = ap.rearrange("p (a b c) -> p a b c", b=shape[2], c=shape[3])
    return ap


A1_BYTES = 8 * NALL * 2
A2_BYTES = 65536
A3_BYTES = 49152
A4_BYTES = 43008


def build_nc(debug=False, stop_after=None):
    nc = bass.Bass("TRN2", target_bir_lowering=False)

    def din(name, shape, dt=F32):
        return nc.dram_tensor(name, list(shape), dt, kind="ExternalInput").ap()

    x_ext = din("x_ext", [NEXT, D])
    ctx_d = din("ctx", [NCTX, D])
    cc_d = din("cc", [128, 8, 2])
    w_mod = din("w_mod", [D, 6 * D])
    bmod_d = din("b_modc", [128, 48])
    bmodr_d = din("b_modr", [6144])
    lning_d = din("ln_in_g", [D])
    lninb_d = din("ln_in_b", [D])
    w_in = din("w_in", [D, 4096])
    tabE_d = din("tab_even", [128, 8, 4, 64])
    tabO_d = din("tab_odd", [128, 8, 5, 64])
    tabS_d = din("tab_sp", [7, 128, 8, 6, 64])
    wgrp_d = din("w_pool_grp", [4, 128, 128])
    pscale_d = din("pool_scale_c", [128, 4])
    pfix_d = din("pool_fix", [128, 16 + 64])
    wa_d = din("w_attn_proj", [512, D])
    wp_d = din("w_pool_proj", [512, D])
    wout_d = din("w_out", [D, D])
    ln1g_d = din("ln1_g", [D])
    ln1b_d = din("ln1_b", [D])
    ln2g_d = din("ln2_g", [D])
    ln2b_d = din("ln2_b", [D])
    wr_d = din("w_r", [D, 36])
    br_d = din("b_r", [36])
    n_exp_in = NE if stop_after in (None, "E") else 1
    weg_d = din("w_expert_gate", [n_exp_in, D, DE])
    weu_d = din("w_expert_up", [n_exp_in, D, DE])
    wed_d = din("w_expert_down", [n_exp_in, DE, D])
    out_d = nc.dram_tensor("out", [NOWN, D], F32, kind="ExternalOutput").ap()
    h_scr = nc.dram_tensor("h_scr", [NOWN, D], F32, kind="Internal").ap()
    hn_scr = nc.dram_tensor("hn_scr", [NOWN, D], F32, kind="Internal").ap()
    mod_scr = nc.dram_tensor("mod_scr", [48 * 128], F32, kind="Internal").ap()
    dbg = {}
    if debug:
        for nm, shp, dt in [("dbg_hmT", [128, 8, NALL], BF), ("dbg_qT", [128, 4, NOWN], BF),
                            ("dbg_kT", [128, 4, NALL], BF), ("dbg_V", [128, 22, 8, 65], BF),
                            ("dbg_ypoolT", [128, 4, NOWN], BF), ("dbg_yattnT", [128, 4, NOWN], BF),
                            ("dbg_mod", [128, 48, 2], F32), ("dbg_mergedT", [128, 8, NOWN], BF),
                            ("dbg_hm2T", [128, 8, NOWN], BF), ("dbg_gates", [128, 16, 32], F32),
                            ("dbg_yacc", [128, 16, D], F32)]:
            dbg[nm] = nc.dram_tensor(nm, shp, dt, kind="ExternalOutput").ap()

    es = ExitStack()
    with es:
        def sb(name, shape, dt=F32):
            return es.enter_context(nc.sbuf_tensor(name, list(shape), dt))

        A1 = sb("A1", [128, A1_BYTES // 4])
        A2 = sb("A2", [128, A2_BYTES // 4])
        A3 = sb("A3", [128, A3_BYTES // 4])
        A4 = sb("A4", [128, A4_BYTES // 4])
        ident_b = sb("ident_b", [128, 128], BF)
        ident_f = sb("ident_f", [128, 128], F32)
        mhalf = sb("mhalf", [128, 1])
        cc_t = sb("cc_t", [128, 8, 2])
        sc_t = sb("sc_t", [128, 8, 2])
        bmod_t = sb("bmod_t", [128, 48])
        modT = sb("modT", [128, 48, 2])
        modL = sb("modL", [128, 48])
        modrow = sb("modrow", [48, 128])
        A1c = sb("A1c", [128, 8, 2])
        stats = sb("stats", [128, 8, 2, 6])
        mv = sb("mv", [128, 8, 2])
        rs = sb("rs", [128, 8, 4])
        pscale_t = sb("pscale_t", [128, 4])
        pfix_t = sb("pfix_t", [128, 80])
        wgrp_t = sb("wgrp_t", [128, 4, 128], BF)
        wr_t = sb("wr_t", [128, 8, 36], BF)
        br_t = sb("br_t", [128, 36])
        gates_all = sb("gates_all", [128, 16, 32])
        rt = sb("rt", [128, 2, 96])
        rden = sb("rden", [64, 2, 8])
        ps = [es.enter_context(nc.psum_tensor(f"ps{i}", [128, 512], F32)) for i in range(8)]
        sem_names = ["pe", "act", "dve", "pool", "sp"]
        sems = {k: es.enter_context(nc.semaphore("s_" + k)) for k in sem_names}
        dma_sems = [es.enter_context(nc.semaphore(f"dq{i}")) for i in range(40)]
        S = Sched(nc, sems, dma_sems)

        def dbg_dump(name, ap, key):
            if debug:
                S.dma("sp", dbg[name], ap, r=[key])

        S.op("pool", lambda e: e.memset(ident_b[:], 0.0), w=["ident_b"])
        S.op("pool", lambda e: e.affine_select(out=ident_b[:], in_=ident_b[:], pattern=[[-1, 128]],
                                              compare_op=ALU.not_equal, fill=1.0, base=0, channel_multiplier=1),
             r=["ident_b"], w=["ident_b"])
        S.op("pool", lambda e: e.memset(ident_f[:], 0.0), w=["ident_f"])
        S.op("pool", lambda e: e.affine_select(out=ident_f[:], in_=ident_f[:], pattern=[[-1, 128]],
                                              compare_op=ALU.not_equal, fill=1.0, base=0, channel_multiplier=1),
             r=["ident_f"], w=["ident_f"])
        S.op("pool", lambda e: e.memset(mhalf[:], -0.5), w=["mhalf"])

        S.dma("sp", cc_t[:], cc_d, w=["cc"])
        S.dma("sp", bmod_t[:], bmod_d, w=["bmod"])
        S.dma("sp", pscale_t[:], pscale_d, w=["pscale"])
        S.dma("sp", pfix_t[:], pfix_d, w=["pfix"])
        S.dma("sp", br_t[:], br_d.partition_broadcast(128), w=["br"])
        S.dma("pool", wgrp_t[:], wgrp_d.rearrange("g i o -> i g o"), w=["wgrp"])
        S.dma("pool", wr_t[:], wr_d.rearrange("(k p) n -> p k n", p=128), w=["wr"])

        S.record()
        S.op("act", lambda e: e.activation(out=sc_t[:], in_=cc_t[:], func=AF.Silu), r=["cc"], w=["sc"])
        wm = [carve(A2, 0, [128, 8, 1024], F32), carve(A2, 32768, [128, 8, 1024], F32)]
        modrow2 = carve(A3, 0, [128, 6144], F32)
        bmodrow = carve(A3, 24576, [128, 6144], F32)
        S.dma("sp", bmodrow[0:2, :], bmodr_d.partition_broadcast(2), w=["bmodrow"])
        for j in range(6):
            S.dma("sp", wm[j % 2], w_mod[:, j * 1024:(j + 1) * 1024].rearrange("(k p) n -> p k n", p=128),
                  w=[("wm", j % 2)])
            for half in range(2):
                blk = j * 2 + half
                bank = blk % 6
                for kc in range(8):
                    S.op("pe", lambda e, kc=kc, bank=bank, j=j, half=half: e.matmul(
                        ps[bank][0:2, :], lhsT=sc_t[:, kc, :], rhs=wm[j % 2][:, kc, half * 512:(half + 1) * 512],
                        start=(kc == 0), stop=(kc == 7)),
                        r=[("wm", j % 2), "sc"], w=[("ps", bank)], inc=(kc == 7))
                S.op("dve", lambda e, bank=bank, blk=blk: e.tensor_tensor(
                    out=modrow2[0:2, blk * 512:(blk + 1) * 512], in0=ps[bank][0:2, :],
                    in1=bmodrow[0:2, blk * 512:(blk + 1) * 512], op=ALU.add),
                    r=[("ps", bank), "bmodrow"], w=["modrow2"])
        for mc in range(48):
            S.op("pe", lambda e, mc=mc: e.transpose(out=ps[6][:, mc * 2:(mc + 1) * 2],
                                                   in_=modrow2[0:2, mc * 128:(mc + 1) * 128],
                                                   identity=ident_f[0:2, 0:2]),
                 r=["modrow2", "ident_f"], w=[("ps", 6)], inc=(mc == 47))
        S.op("dve", lambda e: e.tensor_copy(out=modT[:], in_=ps[6][:, 0:96].rearrange("p (a b) -> p a b", b=2)),
             r=[("ps", 6)], w=["modT"])
        S.op("dve", lambda e: e.tensor_scalar(out=A1c[:], in0=modT[:, 8:16, :], scalar1=1.0, scalar2=None,
                                             op0=ALU.add), r=["modT"], w=["A1c"])
        S.dma("sp", mod_scr.rearrange("(o n) -> o n", o=1), modrow2[0:1, :], r=["modrow2"], w=["mod_scr"])
        phase0_seq = S.stop()

        def mod_bc(idx):
            return mod_scr[idx * 1024:(idx + 1) * 1024].partition_broadcast(128)

        hmT = carve(A1, 0, [128, 8, NALL], BF)
        xt = carve(A4, 0, [128, 8, 1024], F32)
        ginb = carve(A4, 32768, [128, 1024], F32)
        binb = carve(A4, 36864, [128, 1024], F32)
        S.dma("sp", ginb, lning_d.partition_broadcast(128), w=["ginb"])
        S.dma("sp", binb, lninb_d.partition_broadcast(128), w=["binb"])

        slot_ctr = [0]

        def layer_norm_stats(src, srckey):
            sl = slot_ctr[0] % 8
            slot_ctr[0] += 1
            k = ("ln", sl)
            S.op("dve", lambda e: e.bn_stats(out=stats[:, sl, 0, :], in_=src[:, 0:512]), r=[srckey], w=[k])
            S.op("dve", lambda e: e.bn_stats(out=stats[:, sl, 1, :], in_=src[:, 512:1024]), r=[srckey], w=[k])
            S.op("dve", lambda e: e.bn_aggr(out=mv[:, sl, :], in_=stats[:, sl, :, :]), r=[k], w=[k])
            S.op("pool", lambda e: e.tensor_scalar(out=rs[:, sl, 0:1], in0=mv[:, sl, 1:2], scalar1=EPS,
                                                  scalar2=None, op0=ALU.add), r=[k], w=[k])
            S.op("pool", lambda e: e.tensor_tensor(out=rs[:, sl, 1:2], in0=rs[:, sl, 0:1], in1=mhalf[:],
                                                  op=ALU.pow), r=[k, "mhalf"], w=[k])
            S.op("dve", lambda e: e.tensor_scalar(out=rs[:, sl, 2:3], in0=mv[:, sl, 0:1], scalar1=rs[:, sl, 1:2],
                                                 scalar2=-1.0, op0=ALU.mult, op1=ALU.mult), r=[k], w=[k])
            return k, rs[:, sl, 1:2], rs[:, sl, 2:3]

        groups = [list(range(g * 4, g * 4 + 4)) for g in range(5)] + [[20, 21]]
        def a_group(gi, tiles, part):
            which = 1 if gi == 5 else 0
            bufs = []
            for i, t in enumerate(tiles):
                bi = (gi % 2) * 4 + i
                buf = xt[:, bi, :]
                bk = ("xt", bi)
                bufs.append((buf, bk))
                if part == 2:
                    continue
                src = x_ext[t * 128:(t + 1) * 128, :] if t < 20 else ctx_d[(t - 20) * 128:(t - 19) * 128, :]
                S.dma("sp", buf, src, w=[bk])
                k, rstd, nbias = layer_norm_stats(buf, bk)
                S.op("act", lambda e, buf=buf, rstd=rstd, nbias=nbias: e.activation(
                    out=buf, in_=buf, func=AF.Identity, scale=rstd, bias=nbias), r=[bk, k], w=[bk])
                S.op("dve", lambda e, buf=buf: e.tensor_tensor(out=buf, in0=buf, in1=ginb, op=ALU.mult),
                     r=[bk, "ginb"], w=[bk])
                S.op("pool", lambda e, buf=buf: e.tensor_tensor(out=buf, in0=buf, in1=binb, op=ALU.add),
                     r=[bk, "binb"], w=[bk])
                if 2 <= t < 18:
                    S.dma("sp", h_scr[(t - 2) * 128:(t - 1) * 128, :], buf, r=[bk], w=[("h_scr", t - 2)])
            if part == 1:
                return
            ntok = 128 * len(tiles)
            tok0 = tiles[0] * 128
            for kc in range(8):
                bnk = (gi % 2) * 4 + kc % 4
                pb = ps[bnk]
                pk = ("ps", bnk)
                for i, (buf, bk) in enumerate(bufs):
                    S.op("pe", lambda e, pb=pb, i=i, buf=buf, kc=kc: e.transpose(
                        out=pb[:, i * 128:(i + 1) * 128], in_=buf[:, kc * 128:(kc + 1) * 128], identity=ident_f[:]),
                        r=[bk, "ident_f"], w=[pk], inc=(i == len(bufs) - 1))
                S.op("act", lambda e, pb=pb, kc=kc, ntok=ntok, tok0=tok0, which=which: e.activation(
                    out=hmT[:, kc, tok0:tok0 + ntok], in_=pb[:, 0:ntok], func=AF.Identity,
                    scale=A1c[:, kc, which:which + 1], bias=modT[:, kc, which:which + 1]),
                    r=[pk, "A1c", "modT"], w=[("hmT", gi)])

        seqs = [phase0_seq]
        for gi in (0, 1):
            S.record()
            a_group(gi, groups[gi], 1)
            seqs.append(S.stop())
        S.play(seqs, 3)
        seqs = []
        for gi, tiles in enumerate(groups):
            S.record()
            if gi >= 2:
                a_group(gi, tiles, 1)
            a_group(gi, tiles, 2)
            seqs.append(S.stop())
        S.play(seqs, 2)
        dbg_dump("dbg_hmT", hmT, ("hmT", 0))
        S.barrier()
        if stop_after == "A":
            return nc

        qT = carve(A2, 0, [128, 4, NOWN], BF)
        kT = carve(A2, 16384, [128, 4, NALL], BF)
        V = carve(A2, 38912, [128, 176, 65], BF)
        ypoolT = carve(A3, 0, [128, 4, NOWN], BF)
        yattnT = carve(A3, 16384, [128, 4, NOWN], BF)
        pu = carve(A3, 16384, [128, NEXT], F32)
        pa = carve(A3, 16384 + 10240, [128, NEXT], F32)
        pbuf = carve(A3, 16384 + 20480, [128, NEXT], F32)
        wblk = [carve(A4, 0, [128, 8, 512], BF), carve(A4, 8192, [128, 8, 512], BF)]
        pooledT = carve(A4, 16384, [128, NOWN], BF)
        tmp8 = carve(A4, 20480, [128, 16], F32)

        def load_wblk(j):
            S.dma("pool", wblk[j % 2], w_in[:, j * 512:(j + 1) * 512].rearrange("(k p) n -> p k n", p=128),
                  w=[("wblk", j % 2)])

        evac_ctr = [0]

        def evac_copy(out, in_, r, w, eng=None):
            evac_ctr[0] += 1
            if (eng == "act") or (eng is None and evac_ctr[0] % 2 == 0):
                S.op("act", lambda e: e.activation(out=out, in_=in_, func=AF.Copy), r=r, w=w)
            else:
                S.op("dve", lambda e: e.tensor_copy(out=out, in_=in_), r=r, w=w)

        pctr = [0]

        def next_ps(nbanks=8):
            i = pctr[0] % nbanks
            pctr[0] += 1
            return ps[i], ("ps", i)

        load_wblk(0)
        load_wblk(1)
        S.op("pool", lambda e: e.memset(V[:, :, 64:65], 1.0), w=["Vones"])
        for mc in range(4):
            for tb in range(4):
                pb, pk = next_ps()
                for kc in range(8):
                    S.op("pe", lambda e, pb=pb, kc=kc, mc=mc, tb=tb: e.matmul(
                        pb[:, :], lhsT=wblk[0][:, kc, mc * 128:(mc + 1) * 128],
                        rhs=hmT[:, kc, OWN0 + tb * 512:OWN0 + (tb + 1) * 512], start=(kc == 0), stop=(kc == 7)),
                        r=[("wblk", 0), "hmT"], w=[pk], inc=(kc == 7))
                evac_copy(qT[:, mc, tb * 512:(tb + 1) * 512], pb[:, :], [pk], ["qT"])
        load_wblk(2)
        for mc in range(4):
            for blk in range(6):
                n = 512 if blk < 5 else 256
                t0 = blk * 512
                pb, pk = next_ps()
                for kc in range(8):
                    S.op("pe", lambda e, pb=pb, kc=kc, mc=mc, t0=t0, n=n: e.matmul(
                        pb[:, 0:n], lhsT=wblk[1][:, kc, mc * 128:(mc + 1) * 128],
                        rhs=hmT[:, kc, t0:t0 + n], start=(kc == 0), stop=(kc == 7)),
                        r=[("wblk", 1), "hmT"], w=[pk], inc=(kc == 7))
                evac_copy(kT[:, mc, t0:t0 + n], pb[:, 0:n], [pk], ["kT"])
        load_wblk(3)
        for t in range(22):
            pb, pk = next_ps()
            for kc in range(8):
                S.op("pe", lambda e, pb=pb, kc=kc, t=t: e.matmul(
                    pb[:, :], lhsT=hmT[:, kc, t * 128:(t + 1) * 128], rhs=wblk[0][:, kc, :],
                    start=(kc == 0), stop=(kc == 7)),
                    r=[("wblk", 0), "hmT"], w=[pk], inc=(kc == 7))
            evac_copy(V[:, t * 8:(t + 1) * 8, 0:64], pb[:, :].rearrange("p (h d) -> p h d", d=64), [pk], [("V", t)])
        for g in range(4):
            wdw = 2 ** (g + 1)
            hw = wdw // 2
            for blk in range(5):
                pb, pk = next_ps()
                for kc in range(8):
                    S.op("pe", lambda e, pb=pb, kc=kc, g=g, blk=blk: e.matmul(
                        pb[:, :], lhsT=wblk[1][:, kc, g * 128:(g + 1) * 128],
                        rhs=hmT[:, kc, blk * 512:(blk + 1) * 512], start=(kc == 0), stop=(kc == 7)),
                        r=[("wblk", 1), "hmT"], w=[pk], inc=(kc == 7))
                evac_copy(pu[:, blk * 512:(blk + 1) * 512], pb[:, :], [pk], ["pu"])
            S.op("dve", lambda e: e.tensor_tensor(out=pu[:, OWN0 - 8:OWN0], in0=pu[:, OWN0 - 8:OWN0],
                                                 in1=pfix_t[:, 0:8], op=ALU.mult), r=["pu", "pfix"], w=["pu"])
            S.op("dve", lambda e: e.tensor_tensor(out=pu[:, OWN0 + NOWN:OWN0 + NOWN + 8],
                                                 in0=pu[:, OWN0 + NOWN:OWN0 + NOWN + 8],
                                                 in1=pfix_t[:, 8:16], op=ALU.mult), r=["pu", "pfix"], w=["pu"])
            lo = OWN0 - 16
            src, srck = pu, "pu"
            dsts = [(pa, "pa"), (pbuf, "pb")]
            for k in range(g + 1):
                sh = 2 ** k
                hi = OWN0 + NOWN + 32 - 2 * sh * 2
                dst, dstk = dsts[k % 2]
                S.op("dve", lambda e, src=src, dst=dst, sh=sh, hi=hi: e.tensor_tensor(
                    out=dst[:, lo:hi], in0=src[:, lo:hi], in1=src[:, lo + sh:hi + sh], op=ALU.add),
                    r=[srck], w=[dstk])
                src, srck = dst, dstk
            S.op("dve", lambda e, src=src: e.scalar_tensor_tensor(
                out=pooledT[:, :], in0=src[:, OWN0 - hw:OWN0 + NOWN - hw], scalar=1.0 / wdw,
                in1=pu[:, OWN0:OWN0 + NOWN], op0=ALU.mult, op1=ALU.subtract), r=[srck, "pu"], w=["pooledT"])
            for side in range(2):
                o0 = OWN0 if side == 0 else OWN0 + NOWN - 8
                S.op("dve", lambda e, src=src, o0=o0, side=side: e.tensor_tensor(
                    out=tmp8[:, side * 8:(side + 1) * 8], in0=src[:, o0 - hw:o0 - hw + 8],
                    in1=pfix_t[:, 16 + g * 16 + side * 8:16 + g * 16 + side * 8 + 8], op=ALU.mult),
                    r=[srck, "pfix"], w=["tmp8"])
                S.op("dve", lambda e, o0=o0, side=side: e.scalar_tensor_tensor(
                    out=pooledT[:, o0 - OWN0:o0 - OWN0 + 8], in0=tmp8[:, side * 8:(side + 1) * 8],
                    scalar=1.0 / wdw, in1=pu[:, o0:o0 + 8], op0=ALU.mult, op1=ALU.subtract),
                    r=["tmp8", "pu"], w=["pooledT"])
            for tb in range(4):
                pb, pk = next_ps()
                S.op("pe", lambda e, pb=pb, tb=tb: e.matmul(
                    pb[:, :], lhsT=wgrp_t[:, g, :], rhs=pooledT[:, tb * 512:(tb + 1) * 512], start=True, stop=True),
                    r=["wgrp", "pooledT"], w=[pk])
                S.op("act", lambda e, pb=pb, tb=tb: e.activation(
                    out=ypoolT[:, g, tb * 512:(tb + 1) * 512], in_=pb[:, :], func=AF.Identity,
                    scale=pscale_t[:, g:g + 1]), r=[pk, "pscale"], w=["ypoolT"])
        dbg_dump("dbg_qT", qT, "qT")
        dbg_dump("dbg_kT", kT, "kT")
        if debug:
            S.barrier()
            S.dma("sp", dbg["dbg_V"].rearrange("p t h d -> p (t h) d"), V)
        dbg_dump("dbg_ypoolT", ypoolT, "ypoolT")
        S.barrier()
        if stop_after == "B":
            return nc

        tabE = carve(A4, 0, [128, 8, 256], F32)
        tabO = carve(A4, 8192, [128, 8, 320], F32)
        tabS2 = [carve(A4, 18432, [128, 8, 384], F32), carve(A3, 32768, [128, 8, 384], F32)]
        sbuf_s = [carve(A4, 30720 + i * 1536, [128, 384], F32) for i in range(2)]
        pT = [carve(A4, 33792 + i * 1024, [128, 512], BF) for i in range(3)]
        Obf = [carve(A4, 36864 + i * 1024, [128, 512], BF) for i in range(2)]
        S.dma("sp", tabE, tabE_d.rearrange("p h j q -> p h (j q)"), w=["tabE"])
        S.dma("sp", tabO, tabO_d.rearrange("p h j q -> p h (j q)"), w=["tabO"])
        units = [(rl, h) for rl in range(RPC) for h in range(NH)]

        def emit_qk(u):
            rl, h = units[u]
            f, nl, kind = row_window(rl)
            hp, half = h // 2, h % 2
            pb = ps[u % 3]
            pk = ("ps", u % 3)
            if kind == "S" and h == 0:
                sp_ = SPECIAL_SLOT[rl] % 2
                S.dma("sp", tabS2[sp_], tabS_d[SPECIAL_SLOT[rl]].rearrange("p h j q -> p h (j q)"), w=[("tabS", sp_)])
            q_ap = qT[half * 64:(half + 1) * 64, hp, rl * 64:(rl + 1) * 64]
            for j in range(nl + 2):
                t0 = (f + j) * 128 if j < nl else NEXT + (j - nl) * 128
                S.op("pe", lambda e, pb=pb, j=j, t0=t0: e.matmul(
                    pb[:, j * 64:(j + 1) * 64], lhsT=kT[half * 64:(half + 1) * 64, hp, t0:t0 + 128], rhs=q_ap,
                    start=True, stop=True), r=["qT", "kT"], w=[pk], inc=(j == nl + 1))

        def emit_softmax(u):
            rl, h = units[u]
            f, nl, kind = row_window(rl)
            pb = ps[u % 3]
            pk = ("ps", u % 3)
            if kind == "S":
                sp_ = SPECIAL_SLOT[rl] % 2
                tab, tk = tabS2[sp_], ("tabS", sp_)
            else:
                tab, tk = {"E": (tabE, "tabE"), "O": (tabO, "tabO")}[kind]
            sbt = sbuf_s[u % 2]
            sk = ("sbs", u % 2)
            pt = pT[u % 3]
            ptk = ("pT", u % 3)
            S.op("dve", lambda e: e.scalar_tensor_tensor(
                out=sbt[:, 0:nl * 64], in0=pb[:, 0:nl * 64], scalar=SCALE,
                in1=tab[:, h, 0:nl * 64], op0=ALU.mult, op1=ALU.add), r=[pk, tk], w=[sk])
            S.op("act", lambda e: e.activation(out=pt[:, 0:nl * 64], in_=sbt[:, 0:nl * 64], func=AF.Exp),
                 r=[sk], w=[ptk])
            S.op("act", lambda e: e.activation(out=pt[:, nl * 64:(nl + 2) * 64], in_=pb[:, nl * 64:(nl + 2) * 64],
                                              func=AF.Exp, scale=SCALE), r=[pk], w=[ptk])

        def emit_pv(u):
            rl, h = units[u]
            f, nl, kind = row_window(rl)
            pt = pT[u % 3]
            ptk = ("pT", u % 3)
            bank = 3 + (rl % 2) * 2 + h // 4
            ob = ps[bank]
            hh = h % 4
            for j in range(nl + 2):
                vt = f + j if j < nl else 20 + (j - nl)
                S.op("pe", lambda e, j=j, vt=vt: e.matmul(
                    ob[0:64, hh * 65:(hh + 1) * 65], lhsT=pt[:, j * 64:(j + 1) * 64], rhs=V[:, vt * 8 + h, :],
                    start=(j == 0), stop=(j == nl + 1)),
                    r=[ptk, ("V", vt), "Vones"], w=[("ps", bank)], inc=(j == nl + 1))

        def emit_norm(rl):
            par = rl % 2
            for b in range(2):
                bank = 3 + par * 2 + b
                ob = ps[bank][0:64, 0:260].rearrange("p (h d) -> p h d", d=65)
                S.op("dve", lambda e, ob=ob, b=b: e.reciprocal(out=rden[:, par, b * 4:(b + 1) * 4], in_=ob[:, :, 64]),
                     r=[("ps", bank)], w=[("rden", par, b)])
                S.op("dve", lambda e, ob=ob, b=b: e.tensor_tensor(
                    out=Obf[par][0:64, b * 256:(b + 1) * 256].rearrange("p (h d) -> p h d", d=64),
                    in0=ob[:, :, 0:64],
                    in1=rden[:, par, b * 4:(b + 1) * 4].unsqueeze(2).broadcast_to([64, 4, 64]), op=ALU.mult),
                    r=[("ps", bank), ("rden", par, b)], w=[("Obf", par)])

        def emit_finish(rl):
            par = rl % 2
            pst = ps[7][:, 0:128].bitcast(BF)
            for mc in range(4):
                S.op("pe", lambda e, mc=mc: e.transpose(
                    out=pst[:, mc * 64:(mc + 1) * 64], in_=Obf[par][0:64, mc * 128:(mc + 1) * 128],
                    identity=ident_b[0:64, 0:64]), r=[("Obf", par), "ident_b"], w=[("ps", 7)], inc=(mc == 3))
            evac_copy(yattnT[:, :, rl * 64:(rl + 1) * 64], pst.rearrange("p (a b) -> p a b", b=64),
                      [("ps", 7)], ["yattnT"])

        nu = len(units)
        emit_qk(0)
        emit_qk(1)
        for u in range(nu):
            rl, h = units[u]
            if u + 2 < nu:
                emit_qk(u + 2)
            emit_softmax(u)
            emit_pv(u)
            if h == 7:
                emit_norm(rl)
            if h == 1 and rl > 0:
                emit_finish(rl - 1)
        emit_finish(RPC - 1)
        dbg_dump("dbg_yattnT", yattnT, "yattnT")
        S.barrier()
        if stop_after == "C":
            return nc

        wa_t = carve(A2, 0, [128, 4, 1024], BF)
        wp_t = carve(A2, 8192, [128, 4, 1024], BF)
        mergedT = carve(A2, 16384, [128, 8, NOWN], BF)
        wout_b = carve(A2, 49152, [128, 8, 1024], BF)
        wga = carve(A4, 0, [128, 8, 1024], BF)
        wgb = carve(A4, 16384, [128, 8, 1024], BF)
        sg = [carve(A4, 32768 + i * 1024, [128, 512], BF) for i in range(4)]
        t12 = [carve(A4, 36864 + i * 2048, [128, 512], F32) for i in range(3)]
        S.dma("pool", wga, w_in[:, 2048:3072].rearrange("(k p) n -> p k n", p=128), w=["wga"])
        S.dma("pool", wgb, w_in[:, 3072:4096].rearrange("(k p) n -> p k n", p=128), w=["wgb"])
        S.dma("pool", wa_t, wa_d.rearrange("(k p) n -> p k n", p=128), w=["wa"])
        S.dma("pool", wp_t, wp_d.rearrange("(k p) n -> p k n", p=128), w=["wp"])
        un = 0
        for tb in range(4):
            tsl = slice(tb * 512, (tb + 1) * 512)
            hsl = slice(OWN0 + tb * 512, OWN0 + (tb + 1) * 512)
            for mc in range(8):
                base = (un % 2) * 4
                un += 1
                pg, pgb, pya, pyp = ps[base], ps[base + 1], ps[base + 2], ps[base + 3]
                kg, kgb, kya, kyp = [("ps", base + i) for i in range(4)]
                msl = slice(mc * 128, (mc + 1) * 128)
                for kc in range(8):
                    S.op("pe", lambda e, kc=kc, pg=pg, msl=msl, hsl=hsl: e.matmul(
                        pg[:, :], lhsT=wga[:, kc, msl], rhs=hmT[:, kc, hsl], start=(kc == 0), stop=(kc == 7)),
                        r=["wga", "hmT"], w=[kg], inc=(kc == 7))
                for kc in range(8):
                    S.op("pe", lambda e, kc=kc, pgb=pgb, msl=msl, hsl=hsl: e.matmul(
                        pgb[:, :], lhsT=wgb[:, kc, msl], rhs=hmT[:, kc, hsl], start=(kc == 0), stop=(kc == 7)),
                        r=["wgb", "hmT"], w=[kgb], inc=(kc == 7))
                for kc in range(4):
                    S.op("pe", lambda e, kc=kc, pya=pya, msl=msl, tsl=tsl: e.matmul(
                        pya[:, :], lhsT=wa_t[:, kc, msl], rhs=yattnT[:, kc, tsl], start=(kc == 0), stop=(kc == 3)),
                        r=["wa", "yattnT"], w=[kya], inc=(kc == 3))
                for kc in range(4):
                    S.op("pe", lambda e, kc=kc, pyp=pyp, msl=msl, tsl=tsl: e.matmul(
                        pyp[:, :], lhsT=wp_t[:, kc, msl], rhs=ypoolT[:, kc, tsl], start=(kc == 0), stop=(kc == 3)),
                        r=["wp", "ypoolT"], w=[kyp], inc=(kc == 3))
                s0, s1 = sg[(un % 2) * 2], sg[(un % 2) * 2 + 1]
                ks0, ks1 = ("sg", (un % 2) * 2), ("sg", (un % 2) * 2 + 1)
                S.op("act", lambda e, s0=s0, pg=pg: e.activation(out=s0, in_=pg[:, :], func=AF.Sigmoid),
                     r=[kg], w=[ks0])
                S.op("act", lambda e, s1=s1, pgb=pgb: e.activation(out=s1, in_=pgb[:, :], func=AF.Sigmoid),
                     r=[kgb], w=[ks1])
                ta, tbb = t12[0], t12[1]
                S.op("dve", lambda e, s0=s0, pya=pya: e.tensor_tensor(out=ta, in0=s0, in1=pya[:, :], op=ALU.mult),
                     r=[ks0, kya], w=["t12a"])
                S.op("dve", lambda e, s1=s1, pyp=pyp: e.tensor_tensor(out=tbb, in0=s1, in1=pyp[:, :], op=ALU.mult),
                     r=[ks1, kyp], w=["t12b"])
                S.op("dve", lambda e, mc=mc, tsl=tsl: e.tensor_tensor(out=mergedT[:, mc, tsl], in0=ta, in1=tbb,
                                                                    op=ALU.add),
                     r=["t12a", "t12b"], w=["mergedT"])
        dbg_dump("dbg_mergedT", mergedT, "mergedT")
        S.barrier()
        if stop_after == "D1":
            return nc

        I32 = mybir.dt.int32
        hm2Tt = [carve(A1, i * 2048, [128, 8, 128], BF) for i in range(2)]
        oh1_all = carve(A1, 4096, [128, 16, 32], F32)
        oh2_all = carve(A1, 6144, [128, 16, 32], F32)
        Pall = carve(A1, 8192, [128, 16, 32], F32)
        tmpP = carve(A1, 10240, [128, 16, 32], F32)
        Mbf = carve(A1, 12288, [128, 16, 32], BF)
        c12 = carve(A1, 13312, [128, 2, 16], F32)
        Crun = carve(A1, 13440, [128, 32], F32)
        ebase = carve(A1, 13568, [128, 32], F32)
        ebase_i = carve(A1, 13696, [128, 32], F32).bitcast(I32)
        destf = carve(A1, 13824, [128, 32], F32)
        dest_i = carve(A1, 13952, [128, 32], F32).bitcast(I32)
        tokid_i = carve(A1, 14080, [128, 16], F32).bitcast(I32)
        lsrc = carve(A1, 14144, [128, 64], F32)
        lsrc_i = lsrc.bitcast(I32)
        counts_i = carve(A1, 14400, [128, 32], F32).bitcast(I32)
        Lstrict = carve(A1, 14528, [128, 128], BF)
        ones_bf = carve(A1, 14784, [128, 128], BF)
        zeros_i = carve(A1, 15040, [128, 1024], F32)
        Ball = carve(A1, 21504, [128, 16, 32], F32)
        hm2_scr = nc.dram_tensor("hm2_scr", [NOWN, D], BF, kind="Internal").ap()
        list_scr = nc.dram_tensor("list_scr", [NE * NOWN, 2], I32, kind="Internal").ap()
        ybuf = nc.dram_tensor("ybuf", [NE * NOWN, D], F32, kind="Internal").ap()

        wo32 = carve(A3, 0, [128, 8, 1024], F32)
        bc_g1 = carve(A3, 32768, [128, 1024], F32)
        bc_onep = carve(A3, 36864, [128, 1024], F32)
        bc_ln1g = carve(A4, 0, [128, 1024], F32)
        bc_ln1b = carve(A4, 4096, [128, 1024], F32)
        bc_A2 = carve(A4, 8192, [128, 1024], F32)
        bc_B2 = carve(A4, 12288, [128, 1024], F32)
        htile = [carve(A4, 16384 + i * 4096, [128, 1024], F32) for i in range(2)]
        rtile = [carve(A4, 24576 + i * 4096, [128, 1024], F32) for i in range(2)]
        hntile = [carve(A4, 32768 + i * 4096, [128, 1024], F32) for i in range(2)]
        hm2tiles = [carve(A4, 40960, [128, 1024], BF), carve(A1, 19456, [128, 1024], BF)]
        S.dma("sp", wo32, wout_d.rearrange("(k p) n -> p k n", p=128), w=["wo32"])
        S.dma("sp", bc_g1, mod_bc(2), w=["bc_g1"])
        S.dma("sp", bc_onep, mod_bc(4), w=["bc_onep"])
        S.dma("sp", bc_B2, mod_bc(3), w=["bc_B2"])
        S.dma("sp", bc_ln1g, ln1g_d.partition_broadcast(128), w=["bc_ln1g"])
        S.dma("sp", bc_ln1b, ln1b_d.partition_broadcast(128), w=["bc_ln1b"])
        S.op("pool", lambda e: e.memset(zeros_i, 0.0), w=["zeros_i"])
        S.dma("sp", list_scr.rearrange("(p r) c -> p (r c)", p=128), zeros_i.bitcast(I32), r=["zeros_i"],
              w=["list_scr"])
        S.op("pool", lambda e: e.memset(ones_bf, 1.0), w=["ones_bf"])
        S.op("pool", lambda e: e.memset(Lstrict, 1.0), w=["Lstrict"])
        S.op("pool", lambda e: e.affine_select(out=Lstrict, in_=Lstrict, pattern=[[1, 128]], compare_op=ALU.is_gt,
                                              fill=0.0, base=0, channel_multiplier=-1),
             r=["Lstrict"], w=["Lstrict"])
        S.op("pool", lambda e: e.iota(out=ebase_i, pattern=[[NOWN, 32]], base=0, channel_multiplier=0),
             w=["ebase_i"])
        S.op("pool", lambda e: e.tensor_copy(out=ebase, in_=ebase_i), r=["ebase_i"], w=["ebase"])
        S.op("pool", lambda e: e.iota(out=tokid_i, pattern=[[128, 16]], base=0, channel_multiplier=1),
             w=["tokid_i"])
        S.op("pool", lambda e: e.memset(Crun, 0.0), w=["Crun"])
        for kc in range(8):
            S.op("dve", lambda e, kc=kc: e.tensor_tensor(out=wout_b[:, kc, :], in0=wo32[:, kc, :], in1=bc_g1,
                                                        op=ALU.mult), r=["wo32", "bc_g1"], w=["wout_b"])
        S.op("pool", lambda e: e.tensor_scalar(out=bc_onep, in0=bc_onep, scalar1=1.0, scalar2=None, op0=ALU.add),
             r=["bc_onep"], w=["bc_onep"])
        S.op("pool", lambda e: e.tensor_tensor(out=bc_A2, in0=bc_ln1g, in1=bc_onep, op=ALU.mult),
             r=["bc_ln1g", "bc_onep"], w=["bc_A2"])
        S.op("pool", lambda e: e.tensor_tensor(out=bc_onep, in0=bc_ln1b, in1=bc_onep, op=ALU.mult),
             r=["bc_ln1b", "bc_onep", "bc_A2"], w=["bc_onep"])
        S.op("pool", lambda e: e.tensor_tensor(out=bc_B2, in0=bc_B2, in1=bc_onep, op=ALU.add),
             r=["bc_B2", "bc_onep"], w=["bc_B2"])

        def routing(t, pr, prk):
            sl = t % 2
            R = rt[:, sl, :]
            k = ("rt", sl)
            lg = R[:, 0:36]
            S.op("dve", lambda e: e.tensor_tensor(out=lg, in0=pr[:, 0:36], in1=br_t[:], op=ALU.add),
                 r=[prk, "br"], w=[k])
            gmax, ngmax, gsum, ptg = R[:, 36:37], R[:, 37:38], R[:, 38:39], R[:, 39:40]
            eg, oh, pen = R[:, 40:44], R[:, 44:48], R[:, 48:52]
            ml = R[:, 52:84]
            top8 = R[:, 84:92]
            e2, den, c1 = R[:, 92:93], R[:, 93:94], R[:, 94:95]
            S.op("dve", lambda e: e.tensor_reduce(out=gmax, in_=lg[:, 0:4], axis=AX.X, op=ALU.max), r=[k], w=[k])
            S.op("dve", lambda e: e.tensor_scalar(out=ngmax, in0=gmax, scalar1=-1.0, scalar2=None, op0=ALU.mult),
                 r=[k], w=[k])
            S.op("act", lambda e: e.activation(out=eg, in_=lg[:, 0:4], func=AF.Exp, bias=ngmax, scale=1.0),
                 r=[k], w=[k])
            S.op("dve", lambda e: e.tensor_reduce(out=gsum, in_=eg, axis=AX.X, op=ALU.add), r=[k], w=[k])
            S.op("dve", lambda e: e.reciprocal(out=ptg, in_=gsum), r=[k], w=[k])
            S.op("dve", lambda e: e.tensor_scalar(out=oh, in0=lg[:, 0:4], scalar1=gmax, scalar2=None,
                                                 op0=ALU.is_equal), r=[k], w=[k])
            S.op("dve", lambda e: e.tensor_scalar(out=pen, in0=oh, scalar1=1e30, scalar2=-1e30, op0=ALU.mult,
                                                 op1=ALU.add), r=[k], w=[k])
            S.op("dve", lambda e: e.tensor_tensor(
                out=ml.rearrange("p (g x) -> p g x", x=8), in0=lg[:, 4:36].rearrange("p (g x) -> p g x", x=8),
                in1=pen.unsqueeze(2).broadcast_to([128, 4, 8]), op=ALU.add), r=[k], w=[k])
            S.op("dve", lambda e: e.max(out=top8, in_=ml), r=[k], w=[k])
            S.op("dve", lambda e: e.tensor_scalar(out=den, in0=top8[:, 0:1], scalar1=-1.0, scalar2=None,
                                                 op0=ALU.mult), r=[k], w=[k])
            S.op("act", lambda e: e.activation(out=e2, in_=top8[:, 1:2], func=AF.Exp, bias=den, scale=1.0),
                 r=[k], w=[k])
            S.op("dve", lambda e: e.tensor_scalar(out=den, in0=e2, scalar1=1.0, scalar2=None, op0=ALU.add),
                 r=[k], w=[k])
            S.op("dve", lambda e: e.reciprocal(out=c1, in_=den), r=[k], w=[k])
            rk_ = ("route", t)
            S.op("dve", lambda e: e.tensor_tensor(out=c12[:, 0, t:t + 1], in0=c1, in1=ptg, op=ALU.mult),
                 r=[k], w=[rk_])
            S.op("dve", lambda e: e.tensor_tensor(out=c12[:, 1, t:t + 1], in0=c12[:, 0, t:t + 1], in1=e2,
                                                 op=ALU.mult), r=[k, rk_], w=[rk_])
            S.op("dve", lambda e: e.tensor_scalar(out=oh1_all[:, t, :], in0=ml, scalar1=top8[:, 0:1], scalar2=None,
                                                 op0=ALU.is_equal), r=[k], w=[rk_])
            S.op("dve", lambda e: e.tensor_scalar(out=oh2_all[:, t, :], in0=ml, scalar1=top8[:, 1:2], scalar2=None,
                                                 op0=ALU.is_equal), r=[k], w=[rk_])
            S.op("dve", lambda e: e.tensor_tensor(out=Mbf[:, t, :], in0=oh1_all[:, t, :], in1=oh2_all[:, t, :],
                                                 op=ALU.add), r=[rk_], w=[rk_])
            pq = ps[6 + t % 2]
            S.op("pe", lambda e: e.matmul(pq[:, 64:96], lhsT=Lstrict, rhs=Mbf[:, t, :], start=True, stop=True),
                 r=[rk_, "Lstrict"], w=[prk], inc=False)
            S.op("pe", lambda e: e.matmul(pq[:, 96:128], lhsT=ones_bf, rhs=Mbf[:, t, :], start=True, stop=True),
                 r=[rk_, "ones_bf"], w=[prk])
            S.op("dve", lambda e: e.tensor_copy(out=Pall[:, t, :], in_=pq[:, 64:96]), r=[prk], w=[rk_])
            S.op("dve", lambda e: e.tensor_copy(out=Ball[:, t, :], in_=pq[:, 96:128]), r=[prk], w=[rk_])

        def d2_tile(t):
            par = t % 2
            hm2tile, hm2k = hm2tiles[par], ("hm2tile", par)
            ht, hk = htile[par], ("htile", par)
            S.dma("sp", ht, h_scr[t * 128:(t + 1) * 128, :], r=[("h_scr", t)], w=[hk])
            r_t, rk = rtile[par], ("rtile", par)
            for half in range(2):
                pb = ps[par * 2 + half]
                pk = ("ps", par * 2 + half)
                for kc in range(8):
                    S.op("pe", lambda e, pb=pb, kc=kc, half=half: e.matmul(
                        pb[:, :], lhsT=mergedT[:, kc, t * 128:(t + 1) * 128],
                        rhs=wout_b[:, kc, half * 512:(half + 1) * 512], start=(kc == 0), stop=(kc == 7)),
                        r=["mergedT", "wout_b"], w=[pk], inc=(kc == 7))
                S.op("dve", lambda e, pb=pb, half=half: e.scalar_tensor_tensor(
                    out=r_t[:, half * 512:(half + 1) * 512], in0=ht[:, half * 512:(half + 1) * 512], scalar=ALPHA,
                    in1=pb[:, :], op0=ALU.mult, op1=ALU.add), r=[hk, pk], w=[rk])
            k, rstd, nbias = layer_norm_stats(r_t, rk)
            S.op("act", lambda e, rstd=rstd, nbias=nbias: e.activation(
                out=r_t, in_=r_t, func=AF.Identity, scale=rstd, bias=nbias), r=[rk, k], w=[rk])
            hn, hnk = hntile[par], ("hntile", par)
            S.op("pool", lambda e: e.tensor_tensor(out=hn, in0=r_t, in1=bc_ln1g, op=ALU.mult),
                 r=[rk, "bc_ln1g"], w=[hnk])
            S.op("pool", lambda e: e.tensor_tensor(out=hn, in0=hn, in1=bc_ln1b, op=ALU.add),
                 r=[hnk, "bc_ln1b"], w=[hnk])
            S.dma("sp", hn_scr[t * 128:(t + 1) * 128, :], hn, r=[hnk], w=[("hn_scr", t)])
            S.op("dve", lambda e: e.tensor_tensor(out=r_t, in0=r_t, in1=bc_A2, op=ALU.mult),
                 r=[rk, "bc_A2"], w=[rk])
            S.op("dve", lambda e, hm2tile=hm2tile, r_t=r_t: e.tensor_tensor(out=hm2tile, in0=r_t, in1=bc_B2, op=ALU.add),
                 r=[rk, "bc_B2"], w=[hm2k])
            S.dma("sp", hm2_scr[t * 128:(t + 1) * 128, :], hm2tile, r=[hm2k], w=[("hm2_scr", t)])
            pstb = ps[4 + par]
            pstk = ("ps", 4 + par)
            pst = pstb[:, :].bitcast(BF)
            for kc in range(8):
                S.op("pe", lambda e, kc=kc, pst=pst, hm2tile=hm2tile: e.transpose(
                    out=pst[:, kc * 128:(kc + 1) * 128], in_=hm2tile[:, kc * 128:(kc + 1) * 128],
                    identity=ident_b[:]), r=[hm2k, "ident_b"], w=[pstk], inc=(kc == 7))
            h2t, h2k = hm2Tt[par], ("hm2Tt", par)
            evac_copy(h2t, pst.rearrange("p (a b) -> p a b", b=128), [pstk], [h2k])
            pr = ps[6 + par]
            prk = ("ps", 6 + par)
            for kc in range(8):
                S.op("pe", lambda e, kc=kc, pr=pr, h2t=h2t: e.matmul(
                    pr[:, 0:36], lhsT=h2t[:, kc, :], rhs=wr_t[:, kc, :],
                    start=(kc == 0), stop=(kc == 7)), r=[h2k, "wr"], w=[prk], inc=(kc == 7))
            routing(t, pr, prk)

        seqs = []
        for t in range(16):
            S.record()
            d2_tile(t)
            seqs.append(S.stop())
        S.play(seqs, 2)
        allr = [("route", t) for t in range(16)]
        for t in range(16):
            S.op("dve", lambda e, t=t: e.tensor_tensor(out=Pall[:, t, :], in0=Pall[:, t, :], in1=Crun, op=ALU.add),
                 r=[("route", t), "Crun"], w=[("route", t)])
            S.op("dve", lambda e, t=t: e.tensor_tensor(out=Crun, in0=Ball[:, t, :], in1=Crun, op=ALU.add),
                 r=[("route", t), "Crun"], w=["Crun"])
        S.op("dve", lambda e: e.tensor_tensor(out=Pall, in0=Pall, in1=ebase.unsqueeze(1).broadcast_to([128, 16, 32]),
                                             op=ALU.add), r=allr + ["ebase"], w=["Pall"])
        for kk, oh_all in enumerate((oh1_all, oh2_all)):
            S.op("dve", lambda e, oh_all=oh_all: e.tensor_tensor(out=tmpP, in0=Pall, in1=oh_all, op=ALU.mult),
                 r=["Pall"] + allr, w=["tmpP"])
            S.op("dve", lambda e, kk=kk: e.tensor_reduce(out=destf[:, kk * 16:(kk + 1) * 16], in_=tmpP, axis=AX.X,
                                                        op=ALU.add), r=["tmpP"], w=["destf"])
        S.op("dve", lambda e: e.tensor_copy(out=dest_i, in_=destf), r=["destf"], w=["dest_i"])
        S.op("dve", lambda e: e.tensor_copy(out=counts_i, in_=Crun), r=["Crun"], w=["counts_i"])
        lsrc4 = lsrc.rearrange("p (k t c) -> p k t c", k=2, t=16)
        lsrc4_i = lsrc_i.rearrange("p (k t c) -> p k t c", k=2, t=16)
        for kk in range(2):
            S.op("dve", lambda e, kk=kk: e.tensor_copy(out=lsrc4_i[:, kk, :, 0], in_=tokid_i), r=["tokid_i"],
                 w=["lsrc"])
            S.op("dve", lambda e, kk=kk: e.tensor_copy(out=lsrc4[:, kk, :, 1], in_=c12[:, kk, :]), r=allr,
                 w=["lsrc"])
        for kk in range(2):
            for t in range(16):
                S.dma("pool", list_scr, lsrc4_i[:, kk, t, :], r=["lsrc", "dest_i", "list_scr"], w=[("lscat", kk, t)],
                      indirect=("scatter", dest_i[:, kk * 16 + t:kk * 16 + t + 1]))
        if debug:
            S.barrier()
            S.dma("sp", dbg["dbg_gates"][:, 0, :], destf)
            S.dma("sp", dbg["dbg_gates"][:, 1, :], Crun)
            S.dma("sp", dbg["dbg_gates"][:, 2, :], c12.rearrange("p k t -> p (k t)"))
        S.barrier()
        if stop_after == "D2":
            return nc

        wexp = []
        for i in range(2):
            o = i * 24576
            wexp.append((carve(A3, o, [128, 8, 512], BF), carve(A3, o + 8192, [128, 8, 512], BF),
                         carve(A3, o + 16384, [128, 4, 1024], BF)))
        lst = [carve(A1, 23552 + i * 8, [128, 2], F32) for i in range(5)]
        RB = 24576
        xg = [carve(A4, 64 + i * 2048, [128, 1024], BF) for i in range(2)] + [carve(A1, RB, [128, 1024], BF)]
        xgT = [carve(A4, 4160 + i * 2048, [128, 8, 128], BF) for i in range(2)] + [carve(A1, RB + 2048, [128, 8, 128], BF)]
        sa_t = [carve(A4, 8256 + i * 2048, [128, 512], F32) for i in range(2)] + [carve(A1, RB + 4096, [128, 512], F32)]
        act_t = [carve(A4, 12352 + i * 1024, [128, 512], BF) for i in range(2)] + [carve(A1, RB + 6144, [128, 512], BF)]
        actT = [carve(A4, 14400 + i * 1024, [128, 4, 128], BF) for i in range(2)] + [carve(A1, RB + 7168, [128, 4, 128], BF)]
        ysb = [carve(A4, 16448 + i * 4096, [128, 1024], F32) for i in range(2)] + [carve(A1, RB + 8192, [128, 1024], F32)]
        cond_engs = ("pe", "act", "dve", "pool", "sp")
        warm_rhs = carve(A1, RB + 12288, [128, 512], BF)
        S.op("pool", lambda e: e.memset(warm_rhs, 0.0), w=["warm_rhs"])

        def keep_warm(bank, bkey, n):
            for i in range(n):
                S.op("pe", lambda e: e.matmul(bank[:, :], lhsT=ones_bf, rhs=warm_rhs, start=True, stop=True),
                     r=["warm_rhs", "ones_bf"], w=[bkey], inc=False)

        stage = [carve(A2, i * 16384, [128, 4096], F32) for i in range(4)]
        stg_ctr = [0]

        def chunks(ex):
            wg_t, wu_t, wd_t = wexp[ex % 2]
            return (("g", wg_t, weg_d[ex], 8), ("u", wu_t, weu_d[ex], 8), ("d", wd_t, wed_d[ex], 4))

        def issue_chunk(ex, c):
            if ex >= NE:
                return
            nm, dst, src, kdim = chunks(ex)[c]
            si = (3 * ex + c) % 4
            st3 = stage[si].rearrange("p (k n) -> p k n", k=kdim)
            S.dma("sp", st3, src.rearrange("(p k) n -> p k n", p=128), w=[("stage", si)])

        def cast_chunk(ex, c):
            if ex >= NE:
                return
            nm, dst, src, kdim = chunks(ex)[c]
            si = (3 * ex + c) % 4
            st3 = stage[si].rearrange("p (k n) -> p k n", k=kdim)
            sk = ("stage", si)
            h = kdim // 2
            S.op("act", lambda e: e.activation(out=dst[:, 0:h, :], in_=st3[:, 0:h, :], func=AF.Copy),
                 r=[sk], w=[("wexp", ex % 2, nm, 0)])
            S.op("dve", lambda e: e.tensor_copy(out=dst[:, h:, :], in_=st3[:, h:, :]),
                 r=[sk], w=[("wexp", ex % 2, nm, 1)])

        def prefetch_lists(ex_):
            for par_ in range(2):
                li_ = (2 * ex_ + par_) % 4
                r0_ = ex_ * NOWN + par_ * 128
                S.dma("pool", lst[li_].bitcast(I32), list_scr[r0_:r0_ + 128, :], w=[("lst", li_)],
                      semkey=("lst", li_))

        n_exp = NE
        prefetch_lists(0)
        for c in range(3):
            issue_chunk(0, c)
        for c in range(3):
            cast_chunk(0, c)
        for c in range(3):
            issue_chunk(1, c)
        issue_chunk(2, 0)
        sctr = 0
        for ex in range(n_exp):
            if ex + 1 < n_exp:
                prefetch_lists(ex + 1)
            wg_t, wu_t, wd_t = wexp[ex % 2]
            wk = ("wexp", ex % 2)
            cvals = {}
            cregs = {}
            for en in cond_engs:
                cregs[en] = S.eng[en].alloc_register(f"ncnt_{en}_{ex}")
                S.raw(en, lambda e, en=en, ex=ex: e.reg_load(cregs[en], counts_i[0:1, ex:ex + 1]), r=["counts_i"])
                cvals[en] = S.eng[en].snap(cregs[en], donate=True)
            def slot_r1(j, part):
                par = j % 2
                bi = par if j < 2 else 2
                row0 = ex * NOWN + j * 128
                li = (2 * ex + par) % 4 if j < 2 else 4
                lt, lk = lst[li], ("lst", li)
                if j >= 2 and part == 0:
                    S.dma("pool", lt.bitcast(I32), list_scr[row0:row0 + 128, :], w=[lk], semkey=("lst", li))
                xgt, xgk = xg[bi], ("xg", bi)
                xT, xTk = xgT[bi], ("xgT", bi)
                if part == 0:
                    S.dma("pool", xgt, hm2_scr, r=[lk], w=[xgk], indirect=("gather", lt.bitcast(I32)[:, 0:1]),
                          semkey=("xg", bi))
                    pxk = ("ps", 0)
                    pxt = ps[0][:, :].bitcast(BF)
                    for kc in range(8):
                        S.op("pe", lambda e, kc=kc: e.transpose(
                            out=pxt[:, kc * 128:(kc + 1) * 128],
                            in_=xgt.rearrange("t (p k) -> t k p", k=8)[:, kc, :],
                            identity=ident_b[:]), r=[xgk, "ident_b"], w=[pxk], inc=(kc == 7))
                    evac_copy(xT, pxt.rearrange("p (a b) -> p a b", b=128), [pxk], [xTk], eng="dve")
                    return
                pa_, pu_ = ps[1 + 2 * par], ps[2 + 2 * par]
                ka, ku = ("ps", 1 + 2 * par), ("ps", 2 + 2 * par)
                for kc in range(8):
                    S.op("pe", lambda e, kc=kc: e.matmul(
                        pa_[:, :], lhsT=xT[:, kc, :], rhs=wg_t[:, kc, :], start=(kc == 0), stop=(kc == 7)),
                        r=[xTk, wk + ("g", 0), wk + ("g", 1)], w=[ka], inc=(kc == 7))
                for kc in range(8):
                    S.op("pe", lambda e, kc=kc: e.matmul(
                        pu_[:, :], lhsT=xT[:, kc, :], rhs=wu_t[:, kc, :], start=(kc == 0), stop=(kc == 7)),
                        r=[xTk, wk + ("u", 0), wk + ("u", 1)], w=[ku], inc=(kc == 7))
                st, stk = sa_t[bi], ("sa", bi)
                S.op("act", lambda e: e.activation(out=st, in_=pa_[:, :], func=AF.Silu), r=[ka], w=[stk])
                at_, atk = act_t[bi], ("act", bi)
                S.op("dve", lambda e: e.scalar_tensor_tensor(
                    out=at_, in0=st, scalar=lt[:, 1:2], in1=pu_[:, :], op0=ALU.mult, op1=ALU.mult),
                    r=[stk, ku, lk], w=[atk])

            def slot_r2(j, part):
                par = j % 2
                bi = par if j < 2 else 2
                row0 = ex * NOWN + j * 128
                at_, atk = act_t[bi], ("act", bi)
                pak = ("ps5", par)
                pat = ps[5][:, par * 256:(par + 1) * 256].bitcast(BF)
                aT, aTk = actT[bi], ("actT", bi)
                if part == 0:
                    for fc in range(4):
                        S.op("pe", lambda e, fc=fc: e.transpose(
                            out=pat[:, fc * 128:(fc + 1) * 128],
                            in_=at_.rearrange("t (p k) -> t k p", k=4)[:, fc, :],
                            identity=ident_b[:]), r=[atk, "ident_b"], w=[pak], inc=(fc == 3))
                    evac_copy(aT, pat.rearrange("p (a b) -> p a b", b=128), [pak], [aTk], eng="dve")
                    return
                yb, ybk = ysb[bi], ("ysb", bi)
                for half in range(2):
                    pd = ps[6 + half]
                    pdk = ("ps", 6 + half)
                    for fc in range(4):
                        S.op("pe", lambda e, fc=fc, pd=pd, half=half: e.matmul(
                            pd[:, :], lhsT=aT[:, fc, :], rhs=wd_t[:, fc, half * 512:(half + 1) * 512],
                            start=(fc == 0), stop=(fc == 3)), r=[aTk, wk + ("d", 0), wk + ("d", 1)], w=[pdk], inc=(fc == 3))
                    evac_copy(yb[:, half * 512:(half + 1) * 512], pd[:, :], [pdk], [ybk],
                              eng=("dve" if half == 0 else "act"))
                S.dma("act", ybuf[row0:row0 + 128, :], yb, r=[ybk], w=[("ybuf", ex, j)], semkey=("yst", bi))

            def cond(thr):
                return lambda en: cvals[en] > thr

            def emit_pair(j0, hooks=None):
                steps = [(slot_r1, j0, 0), (slot_r1, j0 + 1, 0), (slot_r1, j0, 1), (slot_r1, j0 + 1, 1),
                         (slot_r2, j0, 0), (slot_r2, j0 + 1, 0), (slot_r2, j0, 1), (slot_r2, j0 + 1, 1)]
                for i_, (fn_, jj, part_) in enumerate(steps):
                    S.begin_region()
                    fn_(jj, part_)
                    S.end_region(cond(jj * 128))
                    if hooks and i_ in hooks:
                        hooks[i_]()

            cast_chunk(ex + 1, 0)
            issue_chunk(ex + 2, 1)

            def hook_mid():
                cast_chunk(ex + 1, 1)
                issue_chunk(ex + 2, 2)

            def hook_end():
                cast_chunk(ex + 1, 2)
                issue_chunk(ex + 3, 0)

            emit_pair(0, hooks={3: hook_mid, 7: hook_end})
            S.begin_region()
            emit_pair(2)
            S.begin_region()
            emit_pair(4)
            emit_pair(6)
            S.begin_region()
            for j0 in (8, 10, 12, 14):
                emit_pair(j0)
            S.end_region(cond(1024))
            S.end_region(cond(512))
            S.end_region(cond(256))
            for en in cond_engs:
                S.eng[en].free_register(cregs[en])
        S.barrier()
        if stop_after == "E":
            return nc

        bc_g2 = carve(A4, 0, [128, 1024], F32)
        bc_ln2g = carve(A4, 4096, [128, 1024], F32)
        bc_ln2b = carve(A4, 8192, [128, 1024], F32)
        FW = 4
        hnt = [carve(A3, i * 4096, [128, 1024], F32) for i in range(FW)]
        ot = [carve(A3, 16384 + i * 4096, [128, 1024], F32) for i in range(FW)]
        y1t = [carve(A2, i * 4096, [128, 1024], F32) for i in range(FW)]
        y2t = [carve(A2, 16384 + i * 4096, [128, 1024], F32) for i in range(FW)]
        S.dma("sp", bc_g2, mod_bc(5), w=["bc_g2"])
        S.dma("sp", bc_ln2g, ln2g_d.partition_broadcast(128), w=["bc_ln2g"])
        S.dma("sp", bc_ln2b, ln2b_d.partition_broadcast(128), w=["bc_ln2b"])
        def f_tile(t):
            par = t % FW
            hn, hnk = hnt[par], ("hnt", par)
            S.dma("sp", hn, hn_scr[t * 128:(t + 1) * 128, :], w=[hnk])
            y1, y1k = y1t[par], ("y1t", par)
            y2, y2k = y2t[par], ("y2t", par)
            S.dma("pool", y1, ybuf, w=[y1k], indirect=("gather", dest_i[:, t:t + 1]))
            S.dma("pool", y2, ybuf, w=[y2k], indirect=("gather", dest_i[:, 16 + t:17 + t]))
            o_t, ok = ot[par], ("ot", par)
            S.op("dve", lambda e: e.tensor_tensor(out=y1, in0=y1, in1=y2, op=ALU.add), r=[y1k, y2k], w=[y1k])
            S.op("dve", lambda e: e.tensor_tensor(out=o_t, in0=y1, in1=bc_g2, op=ALU.mult),
                 r=["bc_g2", y1k], w=[ok])
            S.op("dve", lambda e: e.scalar_tensor_tensor(
                out=o_t, in0=hn, scalar=ALPHA, in1=o_t, op0=ALU.mult, op1=ALU.add), r=[hnk, ok], w=[ok])
            k, rstd, nbias = layer_norm_stats(o_t, ok)
            S.op("act", lambda e: e.activation(
                out=o_t, in_=o_t, func=AF.Identity, scale=rstd, bias=nbias), r=[ok, k], w=[ok])
            S.op("dve", lambda e: e.tensor_tensor(out=o_t, in0=o_t, in1=bc_ln2g, op=ALU.mult),
                 r=[ok, "bc_ln2g"], w=[ok])
            S.op("pool", lambda e: e.tensor_tensor(out=o_t, in0=o_t, in1=bc_ln2b, op=ALU.add),
                 r=[ok, "bc_ln2b"], w=[ok])
            S.dma("sp", out_d[t * 128:(t + 1) * 128, :], o_t, r=[ok], w=[("out", t)])

        seqs = []
        for t in range(16):
            S.record()
            f_tile(t)
            seqs.append(S.stop())
        S.play(seqs, FW)
        S.barrier()
    return nc


def _tables(rpb, core):
    col = np.arange(GW)
    col_start = np.clip(col - 8, 0, GW - 16)
    cmask = (col[None, :] >= col_start[:, None]) & (col[None, :] < col_start[:, None] + 16)
    col_off = np.clip(col[None, :] - col[:, None], -15, 15) + 15

    def table(c, rl, first, n):
        r = RPC * c + rl
        rs_ = min(max(r - 4, 0), ROWS - 8)
        out = np.full((128, NH, n, GW), NEG, np.float32)
        for j in range(n):
            for il in range(2):
                er = 2 * (first + j) + il
                gr = RPC * c - HALO + er
                if not (rs_ <= gr < rs_ + 8):
                    continue
                ro = gr - r + 7
                vals = rpb[:, ro, :][:, col_off]
                vals = np.where(cmask[None], vals, np.float32(NEG))
                out[il * 64:(il + 1) * 64, :, j, :] = np.transpose(vals, (2, 0, 1))
        return out

    tabE = table(1, 4, 2, 4)
    tabO = table(1, 5, 2, 5)
    tabS = np.full((7, 128, NH, 6, GW), NEG, np.float32)
    for rl, slot in SPECIAL_SLOT.items():
        f, n = SPECIAL[rl]
        tabS[slot, :, :, :n, :] = table(core, rl, f, n)
    return tabE, tabO, tabS


def _pool_fix(core):
    L = ROWS * GW
    fix = np.ones((128, 80), np.float32)
    if core == 0:
        fix[:, 0:8] = 0.0
    if core == NCORES - 1:
        fix[:, 8:16] = 0.0
    base = core * NOWN
    for g, w in enumerate((2, 4, 8, 16)):
        for side in range(2):
            for j in range(8):
                t = base + (j if side == 0 else NOWN - 8 + j)
                lo = min(max(t - w // 2, 0), L)
                hi = min(max(t + w - w // 2, 0), L)
                fix[:, 16 + g * 16 + side * 8 + j] = np.float32(w) / np.float32(hi - lo)
    return fix


def make_in_maps(inputs):
    f = lambda a: np.ascontiguousarray(np.asarray(a, dtype=np.float32))
    x = f(inputs["x"])[0]
    ctx = f(inputs["ctx"])[0]
    c = f(inputs["c"])[0]
    c_ctx = f(inputs["c_ctx"])
    cc = np.stack([c, c_ctx], axis=-1).reshape(8, 128, 2).transpose(1, 0, 2)
    b_modc = f(inputs["b_mod"])[0].reshape(48, 128).T
    rpb = f(inputs["rpb"])[0]
    shared = {
        "ctx": ctx, "cc": f(cc), "w_mod": f(inputs["w_mod"])[0], "b_modc": f(b_modc), "b_modr": f(inputs["b_mod"])[0],
        "ln_in_g": f(inputs["ln_in_g"]), "ln_in_b": f(inputs["ln_in_b"]), "w_in": f(inputs["w_in"])[0],
        "w_pool_grp": f(inputs["w_pool_grp"])[0],
        "pool_scale_c": f(f(inputs["pool_scale"])[0].reshape(4, 128).T),
        "w_attn_proj": f(inputs["w_attn_proj"])[0], "w_pool_proj": f(inputs["w_pool_proj"])[0],
        "w_out": f(inputs["w_out"])[0],
        "ln1_g": f(inputs["ln1_g"])[0], "ln1_b": f(inputs["ln1_b"])[0],
        "ln2_g": f(inputs["ln2_g"])[0], "ln2_b": f(inputs["ln2_b"])[0],
        "w_r": f(np.concatenate([f(inputs["w_router_group"])[0], f(inputs["w_router_expert"])[0]], axis=1)),
        "b_r": f(np.concatenate([f(inputs["b_router_group"])[0], f(inputs["b_router_expert"])[0]], axis=0)),
        "w_expert_gate": f(inputs["w_expert_gate"])[0], "w_expert_up": f(inputs["w_expert_up"])[0],
        "w_expert_down": f(inputs["w_expert_down"])[0],
    }
    in_maps = []
    for core in range(NCORES):
        xe = np.zeros((NEXT, D), np.float32)
        g0 = (RPC * core - HALO) * GW
        lo, hi = max(g0, 0), min(g0 + NEXT, ROWS * GW)
        xe[lo - g0:hi - g0] = x[lo:hi]
        tabE, tabO, tabS = _tables(rpb, core)
        m = dict(shared)
        m.update({"x_ext": xe, "tab_even": tabE, "tab_odd": tabO, "tab_sp": tabS, "pool_fix": _pool_fix(core)})
        in_maps.append(m)
    return in_maps


def kernel(**inputs):
    in_maps = make_in_maps(inputs)
    nc = build_nc()
    res = run_bass_kernel_spmd(nc, in_maps, core_ids=list(range(NCORES)))
    out = np.concatenate([np.asarray(r["out"]) for r in res.results], axis=0)
    return out.reshape(1, ROWS * GW, D).astype(np.float32)
```
